# Optimizing a Trainium2 kernel written in Bass

```python
import math
import jax, jax.numpy as jnp
from jax import lax
import numpy as np

D_MODEL = 1024
BATCH = 32
SEQ = 2048
DEPTH = 4
DEC_BATCH = 2
DEC_SEQ = 8192
PAST_LEN = 128

HEAD_DIM = 64
EPS = 1e-6
NEG = -1e30
A_HEADS = 16
A_KV_HEADS = 4
A_RADIUS = 128
B_PAIRS = ((128, 1), (512, 4), (2048, 16))
B_HEADS = 8
C_HEADS = 8
C_QK_DIM = 64
C_V_DIM = 2 * C_QK_DIM
C_QBLOCK = 128
D_HEADS = 4
D_K_DIM = 128
D_V_DIM = 128
D_GATE_RANK = 16
D_GATE_TAU = 16.0
D_CHUNK = 64

N_EVEN = (DEPTH + 1) // 2
N_ODD = DEPTH // 2
A_WIDTH = A_HEADS * HEAD_DIM
A_KV_WIDTH = A_KV_HEADS * HEAD_DIM
B_WIDTH = B_HEADS * HEAD_DIM
B_QKV_WIDTH = len(B_PAIRS) * B_WIDTH
EVEN_SPLITS = (A_WIDTH, A_KV_WIDTH, A_KV_WIDTH, A_WIDTH, B_QKV_WIDTH, B_QKV_WIDTH, B_QKV_WIDTH, B_WIDTH)
EVEN_COLS = sum(EVEN_SPLITS)
EVEN_WIDTH = A_WIDTH + B_WIDTH
C_QK_WIDTH = C_HEADS * 2 * C_QK_DIM
C_WIDTH = C_HEADS * C_V_DIM
D_KEY_WIDTH = D_HEADS * D_K_DIM
D_WIDTH = D_HEADS * D_V_DIM
ODD_SPLITS = (C_QK_WIDTH, C_QK_WIDTH, C_WIDTH, C_WIDTH, D_KEY_WIDTH, D_KEY_WIDTH, D_WIDTH, D_WIDTH, 2 * D_GATE_RANK)
ODD_COLS = sum(ODD_SPLITS)
ODD_WIDTH = C_WIDTH + D_WIDTH

kernel_name = "hybrid_bidir_encoder_trunk"


def rmsnorm(x, g):
    x32 = x.astype(jnp.float32)
    y = x32 * lax.rsqrt(jnp.mean(x32 * x32, axis=-1, keepdims=True) + EPS) * g.astype(jnp.float32)
    return y.astype(x.dtype)


def split_cols(z, sizes):
    idx = np.cumsum(np.array(sizes))[:-1].tolist()
    return jnp.split(z, idx, axis=-1)


def alibi_slopes(n):
    return jnp.asarray([2.0 ** (-8.0 * (i + 1) / n) for i in range(n)], dtype=jnp.float32)


def banded_attention(q, k, v, slopes, radius, stride, sink=None):
    Bsz, Hq, T, hd = q.shape
    Hkv = k.shape[1]
    G = Hq // Hkv
    blk = radius
    n = -(-T // blk)
    Tp = n * blk
    pad = Tp - T
    qb = jnp.pad(q, ((0, 0), (0, 0), (0, pad), (0, 0))).reshape(Bsz, Hkv, G, n, blk, hd)

    def windows(a):
        ap = jnp.pad(a, ((0, 0), (0, 0), (blk, blk + pad), (0, 0))).reshape(Bsz, Hkv, n + 2, blk, hd)
        return jnp.concatenate([ap[:, :, :-2], ap[:, :, 1:-1], ap[:, :, 2:]], axis=-2)

    kw, vw = windows(k), windows(v)
    s = jnp.einsum('bkgnqd,bknsd->bkgnqs', qb, kw, preferred_element_type=jnp.float32) * (hd ** -0.5)
    qpos = jnp.arange(Tp).reshape(n, blk)[:, :, None]
    kpos = (jnp.arange(n)[:, None] * blk - blk + jnp.arange(3 * blk)[None, :])[:, None, :]
    dist = jnp.abs(qpos - kpos)
    valid = (dist <= radius) & (kpos >= 0) & (kpos < T)
    s = s - slopes.reshape(Hkv, G)[None, :, :, None, None, None] * (stride * dist).astype(jnp.float32)
    s = jnp.where(valid, s, NEG)
    m = jnp.max(s, axis=-1)
    if sink is not None:
        sk = sink.astype(jnp.float32).reshape(Hkv, G)[None, :, :, None, None]
        m = jnp.maximum(m, sk)
    p = jnp.exp(s - m[..., None])
    denom = jnp.sum(p, axis=-1)
    if sink is not None:
        denom = denom + jnp.exp(sk - m)
    o = jnp.einsum('bkgnqs,bknsd->bkgnqd', p.astype(v.dtype), vw, preferred_element_type=jnp.float32) / denom[..., None]
    lse = m + jnp.log(denom)
    o = o.reshape(Bsz, Hq, Tp, hd)[:, :, :T]
    lse = lse.reshape(Bsz, Hq, Tp)[:, :, :T]
    return o, lse


def dilated_mixture(q, k, v, slopes):
    P, Bsz, H, T, hd = q.shape
    outs, lses = [], []
    for p, (window, dil) in enumerate(B_PAIRS):
        radius = window // (2 * dil)
        L = T // dil

        def gather(a):
            return a.reshape(Bsz, H, L, dil, hd).transpose(0, 3, 1, 2, 4).reshape(Bsz * dil, H, L, hd)

        o, lse = banded_attention(gather(q[p]), gather(k[p]), gather(v[p]), slopes, radius, dil)
        outs.append(o.reshape(Bsz, dil, H, L, hd).transpose(0, 2, 3, 1, 4).reshape(Bsz, H, T, hd))
        lses.append(lse.reshape(Bsz, dil, H, L).transpose(0, 2, 3, 1).reshape(Bsz, H, T))
    wts = jax.nn.softmax(jnp.stack(lses, axis=0), axis=0)
    return jnp.einsum('pbht,pbhtd->bhtd', wts, jnp.stack(outs, axis=0))


def diff_attention(q, k, v, slopes, lam):
    Bsz, H, _, T, dk = q.shape
    nq = T // C_QBLOCK
    qb = jnp.moveaxis(q.reshape(Bsz, H, 2, nq, C_QBLOCK, dk), 3, 0)
    starts = jnp.arange(nq) * C_QBLOCK
    kpos = jnp.arange(T)

    def block(args):
        qi, start = args
        s = jnp.einsum('bhiqd,bhisd->bhiqs', qi, k, preferred_element_type=jnp.float32) * (dk ** -0.5)
        qpos = start + jnp.arange(C_QBLOCK)
        dist = jnp.abs(qpos[:, None] - kpos[None, :]).astype(jnp.float32)
        s = s - slopes[None, :, None, None, None] * dist[None, None, None]
        p = jax.nn.softmax(s, axis=-1)
        a = p[:, :, 0] - lam * p[:, :, 1]
        return jnp.einsum('bhqs,bhsd->bhqd', a.astype(v.dtype), v, preferred_element_type=jnp.float32)

    o = lax.map(block, (qb, starts))
    return jnp.moveaxis(o, 0, 2).reshape(Bsz, H, T, v.shape[-1])


def gla_direction(q, k, v, log_a, strict):
    Bsz, H, T, dk = q.shape
    dv = v.shape[-1]
    C = D_CHUNK
    n = T // C
    q = q.reshape(Bsz, H, n, C, dk)
    k = k.reshape(Bsz, H, n, C, dk)
    v = v.reshape(Bsz, H, n, C, dv)
    b = jnp.cumsum(log_a.reshape(Bsz, H, n, C, dk), axis=3)
    b_last = b[:, :, :, -1:, :]
    q_e = q * jnp.exp(b)
    k_e = k * jnp.exp(-b)
    k_s = k * jnp.exp(b_last - b)
    mask = jnp.tril(jnp.ones((C, C), jnp.float32), k=-1 if strict else 0)
    a = jnp.einsum('bhncd,bhnsd->bhncs', q_e, k_e) * mask
    intra = jnp.einsum('bhncs,bhnsv->bhncv', a, v)
    kv = jnp.einsum('bhncd,bhncv->bhndv', k_s, v)
    decay = jnp.exp(b_last[:, :, :, 0, :])

    def step(S, inp):
        dec, kv_c = inp
        return dec[..., None] * S + kv_c, S

    S0 = jnp.zeros((Bsz, H, dk, dv), jnp.float32)
    _, S_prev = lax.scan(step, S0, (jnp.moveaxis(decay, 2, 0), jnp.moveaxis(kv, 2, 0)))
    S_prev = jnp.moveaxis(S_prev, 0, 2)
    inter = jnp.einsum('bhncd,bhndv->bhncv', q_e, S_prev)
    return (intra + inter).reshape(Bsz, H, T, dv)


def even_layer(x, g, w_in, w_out, sink):
    Bsz, T, _ = x.shape
    h = rmsnorm(x, g)
    z = jnp.einsum('btd,dc->btc', h, w_in)
    qa, ka, va, ga, qb, kb, vb, gb = split_cols(z, EVEN_SPLITS)

    def heads(a, nh):
        return a.reshape(Bsz, T, nh, HEAD_DIM).transpose(0, 2, 1, 3)

    oa, _ = banded_attention(heads(qa, A_HEADS), heads(ka, A_KV_HEADS), heads(va, A_KV_HEADS),
                             alibi_slopes(A_HEADS), A_RADIUS, 1, sink)
    oa = oa.transpose(0, 2, 1, 3).reshape(Bsz, T, A_WIDTH)

    def groups(a):
        return a.reshape(Bsz, T, len(B_PAIRS), B_HEADS, HEAD_DIM).transpose(2, 0, 3, 1, 4)

    ob = dilated_mixture(groups(qb), groups(kb), groups(vb), alibi_slopes(B_HEADS))
    ob = ob.transpose(0, 2, 1, 3).reshape(Bsz, T, B_WIDTH)
    y = jnp.concatenate([oa * jax.nn.silu(ga.astype(jnp.float32)), ob * jax.nn.silu(gb.astype(jnp.float32))], axis=-1)
    return x + jnp.einsum('btc,cd->btd', y.astype(x.dtype), w_out).astype(x.dtype)


def odd_layer(x, g, w_in, w_out, lam_p, subln_g, w_gate2, b_gate, gla_g, layer_idx):
    Bsz, T, _ = x.shape
    h = rmsnorm(x, g)
    z = jnp.einsum('btd,dc->btc', h, w_in)
    qc, kc, vc, gc, qd, kd, vd, gd, ad = split_cols(z, ODD_SPLITS)

    def qk_pair(a):
        return a.reshape(Bsz, T, C_HEADS, 2, C_QK_DIM).transpose(0, 2, 3, 1, 4)

    vch = vc.reshape(Bsz, T, C_HEADS, C_V_DIM).transpose(0, 2, 1, 3)
    lam_init = 0.8 - 0.6 * math.exp(-0.3 * layer_idx)
    lp = lam_p.astype(jnp.float32)
    lam = jnp.exp(jnp.sum(lp[0] * lp[1])) - jnp.exp(jnp.sum(lp[2] * lp[3])) + lam_init
    oc = diff_attention(qk_pair(qc), qk_pair(kc), vch, alibi_slopes(C_HEADS), lam)
    oc = rmsnorm(oc, subln_g) * (1.0 - lam_init)
    oc = oc.transpose(0, 2, 1, 3).reshape(Bsz, T, C_WIDTH)

    rank = ad.astype(jnp.float32).reshape(Bsz, T, 2, D_GATE_RANK)
    log_a = jax.nn.log_sigmoid(jnp.einsum('btpr,prk->pbtk', rank, w_gate2.astype(jnp.float32))
                               + b_gate.astype(jnp.float32)[:, None, None, :]) / D_GATE_TAU

    def dheads(a, dh):
        return a.astype(jnp.float32).reshape(Bsz, T, D_HEADS, dh).transpose(0, 2, 1, 3)

    qh = dheads(qd, D_K_DIM) * (D_K_DIM ** -0.5)
    kh = dheads(kd, D_K_DIM)
    vh = dheads(vd, D_V_DIM)
    la_f = dheads(log_a[0], D_K_DIM)
    la_b = dheads(log_a[1], D_K_DIM)
    fl = lambda a: jnp.flip(a, axis=2)
    o_f = gla_direction(qh, kh, vh, la_f, strict=False)
    o_b = fl(gla_direction(fl(qh), fl(kh), fl(vh), fl(la_b), strict=True))
    od = rmsnorm(o_f + o_b, gla_g).transpose(0, 2, 1, 3).reshape(Bsz, T, D_WIDTH)

    y = jnp.concatenate([oc * jax.nn.silu(gc.astype(jnp.float32)), od * jax.nn.silu(gd.astype(jnp.float32))], axis=-1)
    return x + jnp.einsum('btc,cd->btd', y.astype(x.dtype), w_out).astype(x.dtype)


def trunk(x, norm_g, final_norm_g, even_w_in, even_w_out, sink_logit, odd_w_in, odd_w_out,
          diff_lambda, diff_subln_g, gla_w_gate2, gla_b_gate, gla_norm_g):
    for i in range(DEPTH):
        j = i // 2
        if i % 2 == 0:
            x = even_layer(x, norm_g[i], even_w_in[j], even_w_out[j], sink_logit[j])
        else:
            x = odd_layer(x, norm_g[i], odd_w_in[j], odd_w_out[j], diff_lambda[j], diff_subln_g[j],
                          gla_w_gate2[j], gla_b_gate[j], gla_norm_g[j], i)
    return rmsnorm(x, final_norm_g)


def setup_inputs(seed: int = 0) -> dict:
    key = jax.random.key(seed)
    ks = jax.random.split(key, 14)
    nrm = lambda k, s: jax.random.normal(k, s, jnp.float32)
    return {
        "x_prompt": nrm(ks[0], (BATCH, SEQ, D_MODEL)),
        "x_sample": nrm(ks[1], (DEC_BATCH, DEC_SEQ, D_MODEL)),
        "norm_g": 1.0 + 0.01 * nrm(ks[2], (DEPTH, D_MODEL)),
        "final_norm_g": 1.0 + 0.01 * nrm(ks[3], (D_MODEL,)),
        "even_w_in": nrm(ks[4], (N_EVEN, D_MODEL, EVEN_COLS)) * D_MODEL ** -0.5,
        "even_w_out": nrm(ks[5], (N_EVEN, EVEN_WIDTH, D_MODEL)) * EVEN_WIDTH ** -0.5,
        "sink_logit": nrm(ks[6], (N_EVEN, A_HEADS)),
        "odd_w_in": nrm(ks[7], (N_ODD, D_MODEL, ODD_COLS)) * D_MODEL ** -0.5,
        "odd_w_out": nrm(ks[8], (N_ODD, ODD_WIDTH, D_MODEL)) * ODD_WIDTH ** -0.5,
        "diff_lambda": 0.1 * nrm(ks[9], (N_ODD, 4, C_QK_DIM)),
        "diff_subln_g": 1.0 + 0.01 * nrm(ks[10], (N_ODD, C_V_DIM)),
        "gla_w_gate2": nrm(ks[11], (N_ODD, 2, D_GATE_RANK, D_KEY_WIDTH)) * D_GATE_RANK ** -0.5,
        "gla_b_gate": 0.1 * nrm(ks[12], (N_ODD, 2, D_KEY_WIDTH)),
        "gla_norm_g": 1.0 + 0.01 * nrm(ks[13], (N_ODD, D_V_DIM)),
    }


def reference(x_prompt, x_sample, norm_g, final_norm_g, even_w_in, even_w_out, sink_logit,
              odd_w_in, odd_w_out, diff_lambda, diff_subln_g, gla_w_gate2, gla_b_gate, gla_norm_g):
    y_prompt = trunk(x_prompt, norm_g, final_norm_g, even_w_in, even_w_out, sink_logit, odd_w_in, odd_w_out,
                     diff_lambda, diff_subln_g, gla_w_gate2, gla_b_gate, gla_norm_g)
    y_sample = trunk(x_sample, norm_g, final_norm_g, even_w_in, even_w_out, sink_logit, odd_w_in, odd_w_out,
                     diff_lambda, diff_subln_g, gla_w_gate2, gla_b_gate, gla_norm_g)
    return (y_prompt, y_sample)
```

```python
import contextlib
import math
import os
import numpy as np
import ml_dtypes
import concourse.bass as bass
import concourse.mybir as mybir
from concourse.bass_utils import run_bass_kernel_spmd

F32 = mybir.dt.float32
BF16 = mybir.dt.bfloat16
AF = mybir.ActivationFunctionType
ALU = mybir.AluOpType
AX = mybir.AxisListType

D = 1024
EPS = 1e-6
EVEN_COLS = 7680
ODD_COLS = 6176
NEGB = -30000.0
TMAX = 8192


class Buf:
    __slots__ = ("name", "w", "r")

    def __init__(self, name=""):
        self.name = name
        self.w = None
        self.r = {}


class TK:
    NDMA = 20

    def __init__(self, nc):
        self.nc = nc
        self.engs = {"pe": nc.tensor, "act": nc.scalar, "dve": nc.vector, "pool": nc.gpsimd, "sp": nc.sync}
        self.sems = {}
        self.cnt = {}
        for e in self.engs:
            self.sems[e] = nc.alloc_semaphore("s_" + e)
            self.cnt[e] = 0
        self.waited = {e: {} for e in self.engs}
        self.dq = {}
        for q in ("sp", "pool"):
            lst = []
            for i in range(self.NDMA):
                k = "d_%s_%d" % (q, i)
                self.sems[k] = nc.alloc_semaphore(k)
                self.cnt[k] = 0
                lst.append(k)
            self.dq[q] = [lst, 0]
        self.bar = nc.alloc_semaphore("s_bar")
        self.barcnt = 0
        self.n_ins = 0

    def _wait(self, e, key, val):
        if key == "pe" and e == "pe":
            return
        if self.waited[e].get(key, 0) >= val:
            return
        self.engs[e].wait_ge(self.sems[key], val)
        self.waited[e][key] = val
        self.n_ins += 1

    def _deps(self, e, reads, writes):
        for b in reads:
            if b.w is not None:
                self._wait(e, b.w[0], b.w[1])
        for b in writes:
            if b.w is not None:
                self._wait(e, b.w[0], b.w[1])
            for k, v in b.r.items():
                self._wait(e, k, v)

    def op(self, e, fn, reads=(), writes=()):
        self._deps(e, reads, writes)
        ins = fn(self.engs[e])
        self.cnt[e] += 1
        ins.then_inc(self.sems[e], 1)
        self.n_ins += 1
        c = self.cnt[e]
        for b in reads:
            b.r[e] = c
        for b in writes:
            b.w = (e, c)
            b.r = {}
        return ins

    def dma(self, q, out, in_, reads=(), writes=()):
        lst, idx = self.dq[q]
        key = lst[idx % len(lst)]
        self.dq[q][1] = idx + 1
        if self.cnt[key] > 0:
            self._wait(q, key, self.cnt[key])
        self._deps(q, reads, writes)
        ins = self.engs[q].dma_start(out=out, in_=in_)
        self.cnt[key] += 16
        ins.then_inc(self.sems[key], 16)
        self.n_ins += 1
        c = self.cnt[key]
        for b in reads:
            b.r[key] = c
        for b in writes:
            b.w = (key, c)
            b.r = {}
        return ins

    def barrier(self):
        for q in ("sp", "pool"):
            for key in self.dq[q][0]:
                if self.cnt[key] > 0:
                    self._wait(q, key, self.cnt[key])
        self.barcnt += 1
        n = len(self.engs)
        for e, eng in self.engs.items():
            if self.cnt[e] > 0:
                eng.wait_ge(self.sems[e], self.cnt[e]).then_inc(self.bar, 1)
            else:
                eng.wait_ge(self.bar, 0).then_inc(self.bar, 1)
        for e, eng in self.engs.items():
            eng.wait_ge(self.bar, n * self.barcnt)
            for k in self.sems:
                self.waited[e][k] = self.cnt[k]
        self.n_ins += 2 * n


class Ring:
    def __init__(self, tiles):
        self.tiles = tiles
        self.bufs = [Buf() for _ in tiles]
        self.i = 0

    def next(self):
        k = self.i % len(self.tiles)
        self.i += 1
        return self.tiles[k], self.bufs[k]


class Prog:
    def __init__(self, seq_lens, debug=False):
        self.seq_lens = seq_lens
        self.debug = debug
        self.nc = bass.Bass("TRN2", target_bir_lowering=False)
        self.tk = TK(self.nc)
        self.uid = 0

    def name(self, p):
        self.uid += 1
        return "%s_%d" % (p, self.uid)

    def sb(self, es, shape, dt, name="sb"):
        return es.enter_context(self.nc.sbuf_tensor(self.name(name), list(shape), dt))

    def ps(self, es, shape, dt, name="ps"):
        return es.enter_context(self.nc.psum_tensor(self.name(name), list(shape), dt))

    def sb_ring(self, es, n, shape, dt, name="r"):
        return Ring([self.sb(es, shape, dt, name) for _ in range(n)])

    def ps_ring(self, es, n, shape, dt, name="p"):
        return Ring([self.ps(es, shape, dt, name) for _ in range(n)])

    def dram(self, name, shape, dt, kind=None):
        if kind is None:
            return self.nc.dram_tensor(name, list(shape), dt).ap()
        return self.nc.dram_tensor(name, list(shape), dt, kind=kind).ap()

    def load_const(self, es, dram_ap, shape, dt, name):
        t = self.sb(es, shape, dt, name)
        b = Buf(name)
        self.tk.dma("sp", t[:], dram_ap, writes=[b])
        return t, b

    def prep_weights(self, w_srcs, wb_dsts, g_cols):
        tk = self.tk
        with contextlib.ExitStack() as es:
            CW = 2048
            ld = self.sb_ring(es, 3, [128, CW], F32, "wld")
            st = self.sb_ring(es, 3, [128, CW], BF16, "wst")
            flip = 0
            for src, dst, gc in zip(w_srcs, wb_dsts, g_cols):
                R, C = src.shape
                for k in range(R // 128):
                    for c0 in range(0, C, CW):
                        cw = min(CW, C - c0)
                        lt, lb = ld.next()
                        tk.dma("sp", lt[:, :cw], src[k * 128:(k + 1) * 128, c0:c0 + cw], writes=[lb])
                        stt, sbf = st.next()
                        if gc is not None:
                            gap, gb = gc(k)
                            tk.op("dve", lambda e: e.tensor_scalar(out=stt[:, :cw], in0=lt[:, :cw], scalar1=gap,
                                                                    scalar2=None, op0=ALU.mult),
                                  reads=[lb, gb], writes=[sbf])
                        else:
                            if flip % 2 == 0:
                                tk.op("act", lambda e: e.activation(out=stt[:, :cw], in_=lt[:, :cw], func=AF.Copy),
                                      reads=[lb], writes=[sbf])
                            else:
                                tk.op("dve", lambda e: e.tensor_copy(out=stt[:, :cw], in_=lt[:, :cw]),
                                      reads=[lb], writes=[sbf])
                            flip += 1
                        tk.dma("pool", dst[k * 128:(k + 1) * 128, c0:c0 + cw], stt[:, :cw], reads=[sbf])
        tk.barrier()

    def phase_proj(self, T, x_src, wb, fm_jobs, tm_jobs):
        tk = self.tk
        nsg = T // 2048
        with contextlib.ExitStack() as es:
            xall = self.sb(es, [128, 16, 1024], F32, "xall")
            xb = [Buf() for _ in range(16)]
            hT = self.sb(es, [128, 8, 2048], BF16, "hT")
            hTb = [Buf() for _ in range(16)]
            hbr = self.sb_ring(es, 2, [128, 1024], BF16, "hb")
            junk = self.sb(es, [128, 1024], BF16, "junk")
            junkb = Buf()
            ssq = self.sb(es, [128, 16], F32, "ssq")
            ssqb = [Buf() for _ in range(16)]
            rstd = self.sb(es, [128, 16], F32, "rstd")
            rstdb = Buf()
            wfm = self.sb_ring(es, 3, [128, 8, 128], BF16, "wfm")
            wtm = self.sb_ring(es, 2, [128, 8, 512], BF16, "wtm")
            stb = self.sb_ring(es, 4, [128, 512], BF16, "stb")
            stf = self.sb_ring(es, 3, [128, 512], F32, "stf")
            pst = self.ps_ring(es, 2, [128, 1024], BF16, "pst")
            pso = self.ps_ring(es, 5, [128, 512], F32, "pso")
            ident, identb = self.ident, self.identb
            ev = 0
            for g in range(nsg):
                t0 = g * 2048
                for t in range(16):
                    tk.dma("sp", xall[:, t, :], x_src[t0 + t * 128:t0 + (t + 1) * 128, :], writes=[xb[t]])
                for t in range(16):
                    tk.op("act", lambda e: e.activation(out=junk[:], in_=xall[:, t, :], func=AF.Square,
                                                        accum_out=ssq[:, t:t + 1]),
                          reads=[xb[t]], writes=[junkb, ssqb[t]])
                tk.op("dve", lambda e: e.tensor_scalar(out=rstd[:], in0=ssq[:], scalar1=1.0 / D, scalar2=EPS,
                                                       op0=ALU.mult, op1=ALU.add), reads=ssqb, writes=[rstdb])
                tk.op("act", lambda e: e.activation(out=rstd[:], in_=rstd[:], func=AF.Sqrt), reads=[rstdb],
                      writes=[rstdb])
                tk.op("dve", lambda e: e.reciprocal(out=rstd[:], in_=rstd[:]), reads=[rstdb], writes=[rstdb])
                for t in range(16):
                    hb, hbb = hbr.next()
                    tk.op("act", lambda e: e.activation(out=hb[:], in_=xall[:, t, :], func=AF.Copy,
                                                        scale=rstd[:, t:t + 1]),
                          reads=[xb[t], rstdb], writes=[hbb])
                    pt, ptb = pst.next()
                    for k in range(8):
                        tk.op("pe", lambda e: e.transpose(out=pt[:, k * 128:(k + 1) * 128],
                                                          in_=hb[:, k * 128:(k + 1) * 128], identity=ident[:]),
                              reads=[hbb, identb], writes=[ptb])
                    tk.op("dve", lambda e: e.tensor_copy(out=hT[:, :, t * 128:(t + 1) * 128],
                                                         in_=pt[:].rearrange("p (k c) -> p k c", k=8)),
                          reads=[ptb], writes=[hTb[t]])

                def evac(po, pob, mc, cw, func, dt, shape3=None):
                    nonlocal ev
                    ring = stb if dt == BF16 else stf
                    s_t, s_b = ring.next()
                    if func == "silu":
                        tk.op("act", lambda e: e.activation(out=s_t[:mc, :cw], in_=po[:mc, :cw], func=AF.Silu),
                              reads=[pob], writes=[s_b])
                    else:
                        if True:
                            tk.op("act", lambda e: e.activation(out=s_t[:mc, :cw], in_=po[:mc, :cw], func=AF.Copy),
                                  reads=[pob], writes=[s_b])
                        else:
                            tk.op("dve", lambda e: e.tensor_copy(out=s_t[:mc, :cw], in_=po[:mc, :cw]),
                                  reads=[pob], writes=[s_b])
                        ev += 1
                    return s_t, s_b

                for (col0, ncols, dil, dst, row0, func, dt) in fm_jobs:
                    L = T // dil
                    for cc in range((ncols + 127) // 128):
                        mc = min(128, ncols - cc * 128)
                        c = col0 + cc * 128
                        w, wbuf = wfm.next()
                        tk.dma("sp", w[:, :, :mc], wb[:, c:c + mc].rearrange("(k p) c -> p k c", p=128),
                               writes=[wbuf])
                        for s in range(4):
                            po, pob = pso.next()
                            if dil == 1:
                                rd = hTb[4 * s:4 * s + 4]
                            else:
                                rd = hTb
                            for k in range(8):
                                if dil == 1:
                                    rhs = hT[:, k, s * 512:(s + 1) * 512]
                                    out = po[:mc, :]
                                elif dil == 4:
                                    rhs = hT[:, k, :].rearrange("p (m r) -> p r m", r=4)[:, s, :]
                                    out = po[:mc, :]
                                else:
                                    rhs = hT[:, k, :].rearrange("p (m r) -> p r m", r=16)[:, 4 * s:4 * s + 4, :]
                                    out = po[:mc, :].rearrange("p (r m) -> p r m", r=4)
                                tk.op("pe", lambda e: e.matmul(out, lhsT=w[:, k, :mc], rhs=rhs, start=(k == 0),
                                                               stop=(k == 7)),
                                      reads=[wbuf] + rd, writes=[pob])
                            s_t, s_b = evac(po, pob, mc, 512, func, dt)
                            r0 = row0 + cc * 128
                            if dil == 1:
                                tk.dma("pool", dst[r0:r0 + mc, t0 + s * 512:t0 + (s + 1) * 512], s_t[:mc, :],
                                       reads=[s_b])
                            elif dil == 4:
                                tk.dma("pool", dst[r0:r0 + mc, s * L + g * 512:s * L + (g + 1) * 512], s_t[:mc, :],
                                       reads=[s_b])
                            else:
                                d3 = dst.rearrange("c (r l) -> c r l", r=16)
                                tk.dma("pool", d3[r0:r0 + mc, 4 * s:4 * s + 4, g * 128:(g + 1) * 128],
                                       s_t[:mc, :].rearrange("p (r m) -> p r m", r=4), reads=[s_b])
                for (col0, ncols, dil, dst, cd0, func, dt) in tm_jobs:
                    L = T // dil
                    for cb in range((ncols + 511) // 512):
                        cw = min(512, ncols - cb * 512)
                        c = col0 + cb * 512
                        w, wbuf = wtm.next()
                        tk.dma("sp", w[:, :, :cw], wb[:, c:c + cw].rearrange("(k p) c -> p k c", p=128),
                               writes=[wbuf])
                        for u in range(16):
                            po, pob = pso.next()
                            if dil == 1:
                                rd = [hTb[u]]
                                rows = t0 + u * 128
                            elif dil == 4:
                                rd = hTb
                                rows = (u // 4) * L + g * 512 + (u % 4) * 128
                            else:
                                rd = hTb
                                rows = u * L + g * 128
                            for k in range(8):
                                if dil == 1:
                                    lhsT = hT[:, k, u * 128:(u + 1) * 128]
                                elif dil == 4:
                                    j = u % 4
                                    lhsT = hT[:, k, :].rearrange("p (m r) -> p r m", r=4)[:, u // 4,
                                                                                         j * 128:(j + 1) * 128]
                                else:
                                    lhsT = hT[:, k, :].rearrange("p (m r) -> p r m", r=16)[:, u, :]
                                tk.op("pe", lambda e: e.matmul(po[:, :cw], lhsT=lhsT, rhs=w[:, k, :cw],
                                                               start=(k == 0), stop=(k == 7)),
                                      reads=[wbuf] + rd, writes=[pob])
                            s_t, s_b = evac(po, pob, 128, cw, func, dt)
                            tk.dma("pool", dst[rows:rows + 128, cd0 + cb * 512:cd0 + cb * 512 + cw], s_t[:, :cw],
                                   reads=[s_b])
        tk.barrier()

    def phase_window(self, L, ncls, Hq, G, qT, kT, v, btab, btabb, mode, out, gate=None, esink=None,
                     esinkb=None, dil=1):
        tk = self.tk
        Hkv = Hq // G
        NTc = L // 128
        ngrp = Hq // 4
        with contextlib.ExitStack() as es:
            qwr = self.sb_ring(es, 2, [128, Hq, 128], BF16, "qw")
            kwr = self.sb_ring(es, 2, [128, Hkv, 384], BF16, "kw")
            for ring in (qwr, kwr):
                for t_, b_ in zip(ring.tiles, ring.bufs):
                    tk.op("pool", lambda e: e.memset(t_[:], 0.0), writes=[b_])
            vwr = self.sb_ring(es, 2, [128, 3, Hkv, 65], BF16, "vw")
            for vt, vb_ in zip(vwr.tiles, vwr.bufs):
                tk.op("pool", lambda e: e.memset(vt[:], 1.0), writes=[vb_])
            ptr = self.sb_ring(es, 3, [128, 512], BF16, "pT")
            scr = self.ps_ring(es, 2, [128, 512], F32, "sc")
            accr = self.ps_ring(es, 4, [128, 512], F32, "acc")
            small = self.sb_ring(es, 8, [128, 2], F32, "sm")
            if mode == "A":
                gr = self.sb_ring(es, 2, [128, Hq * 64], BF16, "gate")
                yr = self.sb_ring(es, 2, [128, Hq * 64], F32, "yf")
                ybr = self.sb_ring(es, 2, [128, Hq * 64], BF16, "yb")
            else:
                yr = self.sb_ring(es, 2, [128, Hq, 65], F32, "nb")
                out3 = out.rearrange("(m r) c -> r m c", r=dil)
            for r in range(ncls):
                c0 = r * L
                for t in range(NTc):
                    k0 = max(t - 1, 0)
                    k1 = min(t + 1, NTc - 1)
                    nk = k1 - k0 + 1
                    qw, qwb = qwr.next()
                    tk.dma("sp", qw[0:64], qT[:, c0 + t * 128:c0 + (t + 1) * 128].rearrange("(h d) t -> d h t", d=64),
                           writes=[qwb])
                    kw, kwb = kwr.next()
                    tk.dma("sp", kw[0:64, :, :nk * 128],
                           kT[:, c0 + k0 * 128:c0 + (k1 + 1) * 128].rearrange("(h d) t -> d h t", d=64),
                           writes=[kwb])
                    vw, vwb = vwr.next()
                    for j in range(nk):
                        tk.dma("sp", vw[:, j, :, 0:64],
                               v[c0 + (k0 + j) * 128:c0 + (k0 + j + 1) * 128, :].rearrange("p (h d) -> p h d", d=64),
                               writes=[vwb])
                    if mode == "A":
                        gt, gtb = gr.next()
                        tk.dma("sp", gt[:], gate[t * 128:(t + 1) * 128, :], writes=[gtb])
                    yt, ytb = yr.next()
                    for g in range(ngrp):
                        accs = [accr.next() for _ in range(4)]
                        for j in range(nk):
                            kind = (k0 + j) - t + 1
                            sc, scb = scr.next()
                            tk.op("pe", lambda e: e.matmul(sc[:], lhsT=self.ident[:],
                                                           rhs=btab[:, kind, 4 * g:4 * g + 4, :], start=True,
                                                           stop=False), reads=[self.identb, btabb], writes=[scb])
                            if G == 4:
                                tk.op("pe", lambda e: e.matmul(sc[:], lhsT=kw[:, g, j * 128:(j + 1) * 128],
                                                               rhs=qw[:, 4 * g:4 * g + 4, :], start=False, stop=True),
                                      reads=[kwb, qwb], writes=[scb])
                            else:
                                for hh in range(4):
                                    h = 4 * g + hh
                                    tk.op("pe", lambda e: e.matmul(sc[:, hh * 128:(hh + 1) * 128],
                                                                   lhsT=kw[:, h, j * 128:(j + 1) * 128],
                                                                   rhs=qw[:, h, :], start=False, stop=(hh == 3)),
                                          reads=[kwb, qwb], writes=[scb])
                            pT, pTb = ptr.next()
                            tk.op("act", lambda e: e.activation(out=pT[:], in_=sc[:], func=AF.Exp, scale=0.125),
                                  reads=[scb], writes=[pTb])
                            for hh in range(4):
                                h = 4 * g + hh
                                hv = g if G == 4 else h
                                ac, acb = accs[hh]
                                tk.op("pe", lambda e: e.matmul(ac[:, 0:65], lhsT=pT[:, hh * 128:(hh + 1) * 128],
                                                               rhs=vw[:, j, hv, :], start=(j == 0),
                                                               stop=(j == nk - 1)),
                                      reads=[pTb, vwb], writes=[acb])
                        for hh in range(4):
                            h = 4 * g + hh
                            ac, acb = accs[hh]
                            if mode == "A":
                                sm, smb = small.next()
                                tk.op("dve", lambda e: e.tensor_tensor(out=sm[:, 0:1], in0=ac[:, 64:65],
                                                                       in1=esink[:, h:h + 1], op=ALU.add),
                                      reads=[acb, esinkb], writes=[smb])
                                tk.op("dve", lambda e: e.reciprocal(out=sm[:, 1:2], in_=sm[:, 0:1]), reads=[smb],
                                      writes=[smb])
                                tk.op("dve", lambda e: e.tensor_scalar(out=yt[:, h * 64:(h + 1) * 64], in0=ac[:, 0:64],
                                                                       scalar1=sm[:, 1:2], scalar2=None,
                                                                       op0=ALU.mult),
                                      reads=[acb, smb], writes=[ytb])
                            else:
                                if hh % 2 == 0:
                                    tk.op("dve", lambda e: e.tensor_copy(out=yt[:, h, :], in_=ac[:, 0:65]),
                                          reads=[acb], writes=[ytb])
                                else:
                                    tk.op("act", lambda e: e.activation(out=yt[:, h, :], in_=ac[:, 0:65],
                                                                        func=AF.Copy),
                                          reads=[acb], writes=[ytb])
                    if mode == "A":
                        yb_, ybb = ybr.next()
                        tk.op("pool", lambda e: e.tensor_tensor(out=yb_[:], in0=yt[:], in1=gt[:], op=ALU.mult),
                              reads=[ytb, gtb], writes=[ybb])
                        tk.dma("pool", out[t * 128:(t + 1) * 128, :], yb_[:], reads=[ybb])
                    else:
                        tk.dma("pool", out3[r, t * 128:(t + 1) * 128, :], yt[:].rearrange("p h c -> p (h c)"),
                               reads=[ytb])
        tk.barrier()

    def phase_out(self, T, x_src, x_dst, wob, ytm_src, yT_srcs, nbs=None, gb=None):
        tk = self.tk
        NT = T // 128
        with contextlib.ExitStack() as es:
            wo = self.sb(es, [128, 12, 1024], BF16, "wo")
            wob_ = Buf()
            tk.dma("sp", wo[:], wob.rearrange("(k p) c -> p k c", p=128), writes=[wob_])
            ytr = self.sb_ring(es, 2, [128, 1536], BF16, "ytm")
            yTr = self.sb_ring(es, 2, [128, 12, 128], BF16, "yT")
            xr = self.sb_ring(es, 3, [128, 1024], F32, "xo")
            if nbs is not None:
                nbr = [self.sb_ring(es, 2, [128, 8, 65], F32, "nbl") for _ in range(3)]
                gbr = self.sb_ring(es, 2, [128, 512], BF16, "gbl")
                rr = self.sb_ring(es, 2, [128, 8], F32, "rr")
                ybf = self.sb_ring(es, 2, [128, 512], F32, "ybf")
            pst = self.ps_ring(es, 2, [128, 1024], BF16, "ptr")
            pso = self.ps_ring(es, 4, [128, 512], F32, "pso")
            for t in range(NT):
                rows = slice(t * 128, (t + 1) * 128)
                xt, xtb = xr.next()
                tk.dma("sp", xt[:], x_src[rows, :], writes=[xtb])
                yT, yTb = yTr.next()
                ntm = 0
                if ytm_src or nbs is not None:
                    yt, ytb = ytr.next()
                    for (src, c0, w) in ytm_src:
                        tk.dma("sp", yt[:, c0:c0 + w], src[rows, :], writes=[ytb])
                        ntm = max(ntm, c0 + w)
                    if nbs is not None:
                        nl = []
                        for p in range(3):
                            nt_, ntb_ = nbr[p].next()
                            tk.dma("sp", nt_[:].rearrange("p h c -> p (h c)"), nbs[p][rows, :], writes=[ntb_])
                            nl.append((nt_, ntb_))
                        gbt, gbtb = gbr.next()
                        tk.dma("sp", gbt[:], gb[rows, :], writes=[gbtb])
                        n0, n0b = nl[0]
                        tk.op("pool", lambda e: e.tensor_tensor(out=n0[:], in0=n0[:], in1=nl[1][0][:], op=ALU.add),
                              reads=[n0b, nl[1][1]], writes=[n0b])
                        tk.op("pool", lambda e: e.tensor_tensor(out=n0[:], in0=n0[:], in1=nl[2][0][:], op=ALU.add),
                              reads=[n0b, nl[2][1]], writes=[n0b])
                        r_, rb_ = rr.next()
                        tk.op("dve", lambda e: e.reciprocal(out=r_[:], in_=n0[:, :, 64]), reads=[n0b], writes=[rb_])
                        yf, yfb = ybf.next()
                        for h in range(8):
                            tk.op("dve", lambda e: e.tensor_scalar(out=yf[:, h * 64:(h + 1) * 64], in0=n0[:, h, 0:64],
                                                                   scalar1=r_[:, h:h + 1], scalar2=None,
                                                                   op0=ALU.mult),
                                  reads=[n0b, rb_], writes=[yfb])
                        tk.op("pool", lambda e: e.tensor_tensor(out=yt[:, 1024:1536], in0=yf[:], in1=gbt[:],
                                                                op=ALU.mult),
                              reads=[yfb, gbtb], writes=[ytb])
                        ntm = 1536
                    nch = ntm // 128
                    for c in range(nch):
                        if c % 8 == 0:
                            pt, ptb = pst.next()
                        tk.op("pe", lambda e: e.transpose(out=pt[:, (c % 8) * 128:(c % 8 + 1) * 128],
                                                          in_=yt[:, c * 128:(c + 1) * 128], identity=self.ident[:]),
                              reads=[ytb, self.identb], writes=[ptb])
                        if c % 8 == 7 or c == nch - 1:
                            cs = (c // 8) * 8
                            n = c - cs + 1
                            tk.op("dve", lambda e: e.tensor_copy(
                                out=yT[:, cs:cs + n, :],
                                in_=pt[:, 0:n * 128].rearrange("p (k c) -> p k c", k=n)),
                                reads=[ptb], writes=[yTb])
                for (srcT, ch0, nchk) in yT_srcs:
                    tk.dma("sp", yT[:, ch0:ch0 + nchk, :],
                           srcT[:, t * 128:(t + 1) * 128].rearrange("(k p) t -> p k t", p=128), writes=[yTb])
                for nh in range(2):
                    po, pob = pso.next()
                    for c in range(12):
                        tk.op("pe", lambda e: e.matmul(po[:], lhsT=yT[:, c, :], rhs=wo[:, c, nh * 512:(nh + 1) * 512],
                                                       start=(c == 0), stop=(c == 11)),
                              reads=[yTb, wob_], writes=[pob])
                    tk.op("dve", lambda e: e.tensor_tensor(out=xt[:, nh * 512:(nh + 1) * 512], in0=po[:],
                                                           in1=xt[:, nh * 512:(nh + 1) * 512], op=ALU.add),
                          reads=[pob, xtb], writes=[xtb])
                tk.dma("pool", x_dst[rows, :], xt[:], reads=[xtb])
        tk.barrier()

    def phase_diff(self, T, qcT, kcT, vc, gcT, ycT, cst, neglam, neglamb, sg, sgb):
        tk = self.tk
        NT = T // 128
        NG = T // 512
        SKIP = 150.0
        LA = 3
        sl = alibi(8)
        with contextlib.ExitStack() as es:
            kr = self.sb_ring(es, 2, [128, T], BF16, "kTh")
            vr = self.sb_ring(es, 2, [128, NT, 128], BF16, "vh")
            ULr = self.sb_ring(es, 2, [128, 512], F32, "UL")
            URr = self.sb_ring(es, 2, [128, 512], F32, "UR")
            T1r = self.sb_ring(es, 2, [128, 7, 128], BF16, "T1")
            crr = self.sb_ring(es, 2, [128, 7, 128], BF16, "cr")
            bLr = self.sb_ring(es, 2, [128, 64], F32, "bL")
            bRr = self.sb_ring(es, 2, [128, 64], F32, "bR")
            onesf = self.sb(es, [128, 128], F32, "onesf")
            onesb = self.sb(es, [128, 128], BF16, "onesb")
            cb = Buf()
            tk.op("pool", lambda e: e.memset(onesf[:], 1.0), writes=[cb])
            tk.op("pool", lambda e: e.memset(onesb[:], 1.0), writes=[cb])
            qr = self.sb_ring(es, 2, [128, 2, 512], BF16, "qTh")
            for t_, b_ in zip(qr.tiles, qr.bufs):
                tk.op("pool", lambda e: e.memset(t_[:], 0.0), writes=[b_])
            ones1 = self.sb(es, [128, 128], BF16, "ones1p")
            tk.op("pool", lambda e: e.memset(ones1[:], 0.0), writes=[cb])
            tk.op("pool", lambda e: e.memset(ones1[0:1, :], 1.0), writes=[cb])
            for t_, b_ in zip(crr.tiles, crr.bufs):
                tk.op("pool", lambda e: e.memset(t_[:], 0.0), writes=[b_])
            gr = self.sb_ring(es, 2, [128, 512], BF16, "gch")
            ptr = self.sb_ring(es, 6, [128, 512], BF16, "pT")
            osr = self.sb_ring(es, 2, [128, 4, 512], F32, "osum")
            tmpr = self.sb_ring(es, 3, [128, 512], F32, "tmp")
            yr = self.sb_ring(es, 2, [128, 512], BF16, "yc")
            scr = self.ps_ring(es, 4, [128, 512], F32, "sc")
            accn = [(self.ps(es, [128, 512], F32, "accn"), Buf()) for _ in range(2)]
            accd = [(self.ps(es, [128, 512], F32, "accd"), Buf()) for _ in range(2)]
            items = []
            for h in range(8):
                hctx = {"h": h}
                for G in range(NG):
                    gctx = {"G": G, "hctx": hctx}
                    zl = []
                    for zn, kts in (("L", range(0, 4 * G)), ("D", range(4 * G, 4 * G + 4)),
                                    ("R", range(4 * G + 4, NT))):
                        keep = []
                        for kt in kts:
                            if zn == "L":
                                md = 128 * (4 * G - kt) - 127
                            elif zn == "R":
                                md = 128 * kt - (512 * G + 511)
                            else:
                                md = 0
                            if sl[h] * md >= SKIP:
                                continue
                            keep.append(kt)
                        if keep:
                            zl.append((zn, keep))
                    for zi, (zn, keep) in enumerate(zl):
                        for idx, kt in enumerate(keep):
                            for p in range(2):
                                items.append({"g": gctx, "zn": zn, "kt": kt, "p": p, "first": idx == 0,
                                              "last": idx == len(keep) - 1, "zfirst": zi == 0,
                                              "zlast": zi == len(zl) - 1,
                                              "gstart": zi == 0 and idx == 0 and p == 0})

            def stage_a(it):
                g = it["g"]
                hc = g["hctx"]
                h = hc["h"]
                G = g["G"]
                hr = slice(h * 128, (h + 1) * 128)
                if "k" not in hc:
                    hc["k"] = kr.next()
                    hc["v"] = vr.next()
                    tk.dma("sp", hc["k"][0][:], kcT[hr, :], writes=[hc["k"][1]])
                    tk.dma("sp", hc["v"][0][:], vc[:, hr].rearrange("(n p) d -> p n d", p=128), writes=[hc["v"][1]])
                    for nm, ring, src in (("UL", ULr, cst["UL"][:, h, :]), ("UR", URr, cst["UR"][:, h, :]),
                                          ("T1", T1r, cst["T1"][:, h, :, :]), ("CR", crr, cst["CR"][:, h, :, :]),
                                          ("BL", bLr, cst["BL"][:, h, :]), ("BR", bRr, cst["BR"][:, h, :])):
                        hc[nm] = ring.next()
                        if nm == "CR":
                            tk.dma("sp", hc[nm][0][0:1, :, :], src, writes=[hc[nm][1]])
                        else:
                            tk.dma("sp", hc[nm][0][:], src, writes=[hc[nm][1]])
                if "q" not in g:
                    g["q"] = qr.next()
                    tk.dma("sp", g["q"][0][0:64, 0, :], qcT[h * 128:h * 128 + 64, G * 512:(G + 1) * 512],
                           writes=[g["q"][1]])
                    tk.dma("sp", g["q"][0][64:128, 1, :], qcT[h * 128 + 64:h * 128 + 128, G * 512:(G + 1) * 512],
                           writes=[g["q"][1]])
                    g["gt"] = gr.next()
                    tk.dma("sp", g["gt"][0][:], gcT[hr, G * 512:(G + 1) * 512], writes=[g["gt"][1]])
                    g["os"] = None
                kT, kTb = hc["k"]
                qt, qtb = g["q"]
                p = it["p"]
                kt = it["kt"]
                pr = slice(64 * p, 64 * p + 64)
                sc, scb = scr.next()
                it["sc"] = (sc, scb)
                if it["zn"] == "D":
                    ktl = kt - 4 * G
                    T1h, T1b = hc["T1"]
                    crh, crb = hc["CR"]
                    tk.op("pe", lambda e: e.matmul(sc[:], lhsT=self.ident[:], rhs=T1h[:, 3 - ktl:7 - ktl, :],
                                                   start=True, stop=False),
                          reads=[self.identb, T1b], writes=[scb])
                    tk.op("pe", lambda e: e.matmul(sc[:], lhsT=ones1[:, :], rhs=crh[:, 3 - ktl:7 - ktl, :],
                                                   start=False, stop=False),
                          reads=[cb, crb], writes=[scb])
                    tk.op("pe", lambda e: e.matmul(sc[:], lhsT=kT[:, kt * 128:(kt + 1) * 128], rhs=qt[:, p, :],
                                                   start=False, stop=True),
                          reads=[kTb, qtb], writes=[scb])
                else:
                    tk.op("pe", lambda e: e.matmul(sc[:], lhsT=kT[:, kt * 128:(kt + 1) * 128], rhs=qt[:, p, :],
                                                   start=True, stop=True),
                          reads=[kTb, qtb], writes=[scb])

            cnt = [0]

            def stage_b(it):
                g = it["g"]
                hc = g["hctx"]
                G = g["G"]
                p = it["p"]
                kt = it["kt"]
                sc, scb = it["sc"]
                zn = it["zn"]
                rd = [scb]
                if zn == "D":
                    bias = 0.0
                elif zn == "L":
                    dl = 4 * G - kt
                    bias = hc["BL"][0][:, dl:dl + 1]
                    rd.append(hc["BL"][1])
                else:
                    dr = kt - 4 * G - 4
                    bias = hc["BR"][0][:, dr:dr + 1]
                    rd.append(hc["BR"][1])
                pT, pTb = ptr.next()
                tk.op("act", lambda e: e.activation(out=pT[:], in_=sc[:], func=AF.Exp, scale=0.125, bias=bias),
                      reads=rd, writes=[pTb])
                vh, vhb = hc["v"]
                an, anb = accn[p]
                tk.op("pe", lambda e: e.matmul(an[:], lhsT=vh[:, kt, :], rhs=pT[:], start=it["first"],
                                               stop=it["last"]),
                      reads=[vhb, pTb], writes=[anb])
                ad_, adb = accd[p]
                tk.op("pe", lambda e: e.matmul(ad_[:], lhsT=onesb[:], rhs=pT[:], start=it["first"], stop=it["last"]),
                      reads=[cb, pTb], writes=[adb])
                if not it["last"]:
                    return
                if g["os"] is None:
                    g["os"] = osr.next()
                    g["osb"] = [Buf() for _ in range(4)]
                osum = g["os"][0]
                osb = g["osb"]
                for (ac, acb, i) in ((an, anb, 2 * p), (ad_, adb, 2 * p + 1)):
                    if zn == "D":
                        if it["zfirst"]:
                            tk.op("dve", lambda e: e.tensor_copy(out=osum[:, i, :], in_=ac[:]), reads=[acb],
                                  writes=[osb[i]])
                        else:
                            tk.op("dve", lambda e: e.tensor_tensor(out=osum[:, i, :], in0=ac[:], in1=osum[:, i, :],
                                                                   op=ALU.add),
                                  reads=[acb, osb[i]], writes=[osb[i]])
                    else:
                        U, Ub = hc["UL"] if zn == "L" else hc["UR"]
                        if it["zfirst"]:
                            tk.op("dve", lambda e: e.tensor_tensor(out=osum[:, i, :], in0=ac[:], in1=U[:], op=ALU.mult),
                                  reads=[acb, Ub], writes=[osb[i]])
                        else:
                            tm_, tmb = tmpr.next()
                            tk.op("dve", lambda e: e.tensor_tensor(out=tm_[:], in0=ac[:], in1=U[:], op=ALU.mult),
                                  reads=[acb, Ub], writes=[tmb])
                            tk.op("pool", lambda e: e.tensor_tensor(out=osum[:, i, :], in0=tm_[:], in1=osum[:, i, :],
                                                                    op=ALU.add),
                                  reads=[tmb, osb[i]], writes=[osb[i]])
                if not (it["zlast"] and p == 1):
                    return
                h = hc["h"]
                hr = slice(h * 128, (h + 1) * 128)
                for pp in range(2):
                    tk.op("dve", lambda e: e.reciprocal(out=osum[:, 2 * pp + 1, :], in_=osum[:, 2 * pp + 1, :]),
                          reads=[osb[2 * pp + 1]], writes=[osb[2 * pp + 1]])
                    tk.op("dve", lambda e: e.tensor_tensor(out=osum[:, 2 * pp, :], in0=osum[:, 2 * pp, :],
                                                           in1=osum[:, 2 * pp + 1, :], op=ALU.mult),
                          reads=[osb[2 * pp], osb[2 * pp + 1]], writes=[osb[2 * pp]])
                tk.op("dve", lambda e: e.scalar_tensor_tensor(out=osum[:, 0, :], in0=osum[:, 2, :], scalar=neglam,
                                                              in1=osum[:, 0, :], op0=ALU.mult, op1=ALU.add),
                      reads=[osb[0], osb[2], neglamb], writes=[osb[0]])
                tk.op("pool", lambda e: e.tensor_tensor(out=osum[:, 1, :], in0=osum[:, 0, :], in1=osum[:, 0, :],
                                                        op=ALU.mult),
                      reads=[osb[0]], writes=[osb[1]])
                sc2, sc2b = accd[1]
                tk.op("pe", lambda e: e.matmul(sc2[:], lhsT=onesf[:], rhs=osum[:, 1, :], start=True, stop=True),
                      reads=[cb, osb[1]], writes=[sc2b])
                tk.op("act", lambda e: e.activation(out=osum[:, 3, :], in_=sc2[:], func=AF.Ln, scale=1.0 / 128,
                                                    bias=self.epsc[:, 0:1]),
                      reads=[sc2b, self.epscb], writes=[osb[3]])
                tk.op("act", lambda e: e.activation(out=osum[:, 3, :], in_=osum[:, 3, :], func=AF.Exp, scale=-0.5),
                      reads=[osb[3]], writes=[osb[3]])
                tk.op("dve", lambda e: e.scalar_tensor_tensor(out=osum[:, 0, :], in0=osum[:, 0, :], scalar=sg,
                                                              in1=osum[:, 3, :], op0=ALU.mult, op1=ALU.mult),
                      reads=[osb[0], osb[3], sgb], writes=[osb[0]])
                yt, ytb = yr.next()
                gt, gtb = g["gt"]
                tk.op("pool", lambda e: e.tensor_tensor(out=yt[:], in0=osum[:, 0, :], in1=gt[:], op=ALU.mult),
                      reads=[osb[0], gtb], writes=[ytb])
                tk.dma("pool", ycT[hr, G * 512:(G + 1) * 512], yt[:], reads=[ytb])

            n = len(items)
            for i in range(n + LA):
                if i < n:
                    stage_a(items[i])
                if i - LA >= 0:
                    stage_b(items[i - LA])
        tk.barrier()

    def phase_gla(self, T, qd, kd, vd, adT, gd, of, ydT, waug, gcst, glag, glagb):
        tk = self.tk
        NT = T // 128
        with contextlib.ExitStack() as es:
            S = self.sb(es, [128, 4, 128], F32, "S")
            Sb = self.sb(es, [128, 4, 128], BF16, "Sb")
            Sbuf = [Buf() for _ in range(4)]
            Sbb = [Buf() for _ in range(4)]
            cmat = self.sb(es, [128, 2, 128], F32, "cmat")
            maskr = self.sb(es, [128, 512], BF16, "mask")
            ind = self.sb(es, [128, 2], F32, "ind")
            wa = self.sb(es, [17, 512], F32, "waug")
            cb = Buf()
            tk.dma("sp", ind[:], gcst["IND"], writes=[cb])
            adr = self.sb_ring(es, 2, [17, 128], F32, "ad")
            for t_, b_ in zip(adr.tiles, adr.bufs):
                tk.op("pool", lambda e: e.memset(t_[0:1, :], 1.0), writes=[b_])
            qr = self.sb_ring(es, 2, [128, 512], F32, "q")
            kr = self.sb_ring(es, 2, [128, 512], F32, "k")
            vr = self.sb_ring(es, 2, [128, 512], BF16, "v")
            ofr = self.sb_ring(es, 2, [128, 512], F32, "of")
            gdr = self.sb_ring(es, 2, [128, 512], BF16, "gd")
            spr = self.sb_ring(es, 2, [128, 512], F32, "sp")
            er = self.sb_ring(es, 4, [128, 512], F32, "E")
            qer = self.sb_ring(es, 2, [128, 512], BF16, "qe")
            ker = self.sb_ring(es, 2, [128, 512], BF16, "ke")
            ksr = self.sb_ring(es, 2, [128, 512], BF16, "ks")
            qeTr = self.sb_ring(es, 2, [128, 4, 128], BF16, "qeT")
            keTr = self.sb_ring(es, 2, [128, 4, 128], BF16, "keT")
            amr = self.sb_ring(es, 2, [128, 512], BF16, "am")
            decr = self.sb_ring(es, 2, [128, 4, 2], F32, "dec")
            osr = self.sb_ring(es, 2, [128, 512], F32, "os")
            smr = self.sb_ring(es, 2, [128, 8], F32, "sm")
            ybr = self.sb_ring(es, 2, [128, 512], BF16, "yb")
            yTr = self.sb_ring(es, 2, [128, 4, 128], BF16, "yT")
            psr = self.ps_ring(es, 5, [128, 512], F32, "ps")
            ptr = self.ps_ring(es, 2, [128, 1024], BF16, "pt")
            kvr = self.ps_ring(es, 1, [128, 512], F32, "kv")
            assert True
            for dr in range(2):
                tk.dma("sp", cmat[:], gcst["CM"][dr], writes=[cb])
                tk.dma("sp", maskr[:], gcst["MASK"][dr], writes=[cb])
                tk.dma("sp", wa[:], waug[dr], writes=[cb])
                for h in range(4):
                    tk.op("pool", lambda e: e.memset(S[:, h, :], 0.0), writes=[Sbuf[h]])
                    tk.op("pool", lambda e: e.memset(Sb[:, h, :], 0.0), writes=[Sbb[h]])
                order = range(NT) if dr == 0 else range(NT - 1, -1, -1)
                for t in order:
                    rows = slice(t * 128, (t + 1) * 128)
                    ad_, adb = adr.next()
                    tk.dma("sp", ad_[1:17, :], adT[16 * dr:16 * dr + 16, rows], writes=[adb])
                    q_, qb_ = qr.next()
                    tk.dma("sp", q_[:], qd[rows, :], writes=[qb_])
                    k_, kb_ = kr.next()
                    tk.dma("sp", k_[:], kd[rows, :], writes=[kb_])
                    v_, vb_ = vr.next()
                    tk.dma("sp", v_[:], vd[rows, :], writes=[vb_])
                    if dr == 1:
                        of_, ofb = ofr.next()
                        tk.dma("sp", of_[:], of[rows, :], writes=[ofb])
                        gd_, gdb = gdr.next()
                        tk.dma("sp", gd_[:], gd[rows, :], writes=[gdb])
                    lvl = int(os.environ.get("GLALVL", "99"))
                    if lvl < 2:
                        continue
                    px, pxb = psr.next()
                    tk.op("pe", lambda e: e.matmul(px[:], lhsT=ad_[:, :], rhs=wa[:, :], start=True, stop=True),
                          reads=[adb, cb], writes=[pxb])
                    sp_, spb = spr.next()
                    tk.op("act", lambda e: e.activation(out=sp_[:], in_=px[:], func=AF.Exp, scale=-1.0),
                          reads=[pxb], writes=[spb])
                    tk.op("act", lambda e: e.activation(out=sp_[:], in_=sp_[:], func=AF.Ln, bias=self.onec[:, 0:1]),
                          reads=[spb, self.epscb], writes=[spb])
                    if lvl < 3:
                        continue
                    pcs, pcsb = psr.next()
                    tk.op("pe", lambda e: e.matmul(pcs[:], lhsT=cmat[:, 0, :], rhs=sp_[:], start=True, stop=True),
                          reads=[cb, spb], writes=[pcsb])
                    paf, pafb = psr.next()
                    tk.op("pe", lambda e: e.matmul(paf[:], lhsT=cmat[:, 1, :], rhs=sp_[:], start=True, stop=True),
                          reads=[cb, spb], writes=[pafb])
                    ptot, ptotb = psr.next()
                    for h in range(4):
                        tk.op("pe", lambda e: e.matmul(ptot[:, 2 * h:2 * h + 2], lhsT=sp_[:, h * 128:(h + 1) * 128],
                                                       rhs=ind[:, :], start=True, stop=True),
                              reads=[spb, cb], writes=[ptotb])
                    e1, e1b = er.next()
                    e2, e2b = er.next()
                    e3, e3b = er.next()
                    tk.op("act", lambda e: e.activation(out=e1[:], in_=pcs[:], func=AF.Exp, scale=-1.0 / 16),
                          reads=[pcsb], writes=[e1b])
                    tk.op("act", lambda e: e.activation(out=e2[:], in_=pcs[:], func=AF.Exp, scale=1.0 / 16),
                          reads=[pcsb], writes=[e2b])
                    tk.op("act", lambda e: e.activation(out=e3[:], in_=paf[:], func=AF.Exp, scale=-1.0 / 16),
                          reads=[pafb], writes=[e3b])
                    dec, decb = decr.next()
                    tk.op("act", lambda e: e.activation(out=dec[:].rearrange("p h c -> p (h c)"), in_=ptot[:, 0:8],
                                                        func=AF.Exp, scale=-1.0 / 16),
                          reads=[ptotb], writes=[decb])
                    if lvl < 4:
                        continue
                    qe, qeb = qer.next()
                    ke, keb = ker.next()
                    ks, ksb = ksr.next()
                    tk.op("dve", lambda e: e.scalar_tensor_tensor(out=qe[:], in0=q_[:], scalar=128 ** -0.5, in1=e1[:],
                                                                  op0=ALU.mult, op1=ALU.mult),
                          reads=[qb_, e1b], writes=[qeb])
                    tk.op("pool", lambda e: e.tensor_tensor(out=ke[:], in0=k_[:], in1=e2[:], op=ALU.mult),
                          reads=[kb_, e2b], writes=[keb])
                    tk.op("dve", lambda e: e.tensor_tensor(out=ks[:], in0=k_[:], in1=e3[:], op=ALU.mult),
                          reads=[kb_, e3b], writes=[ksb])
                    pt, ptb = ptr.next()
                    ptk, ptkb = ptr.next()
                    for h in range(4):
                        tk.op("pe", lambda e: e.transpose(out=pt[:, h * 128:(h + 1) * 128],
                                                          in_=qe[:, h * 128:(h + 1) * 128], identity=self.ident[:]),
                              reads=[qeb, self.identb], writes=[ptb])
                    for h in range(4):
                        tk.op("pe", lambda e: e.transpose(out=ptk[:, h * 128:(h + 1) * 128],
                                                          in_=ke[:, h * 128:(h + 1) * 128], identity=self.ident[:]),
                              reads=[keb, self.identb], writes=[ptkb])
                    qeT, qeTb = qeTr.next()
                    keT, keTb = keTr.next()
                    tk.op("dve", lambda e: e.tensor_copy(out=qeT[:].rearrange("p h c -> p (h c)"), in_=pt[:, 0:512]),
                          reads=[ptb], writes=[qeTb])
                    tk.op("act", lambda e: e.activation(out=keT[:].rearrange("p h c -> p (h c)"), in_=ptk[:, 0:512],
                                                        func=AF.Copy),
                          reads=[ptkb], writes=[keTb])
                    if lvl < 5:
                        continue
                    pa, pab = psr.next()
                    for h in range(4):
                        tk.op("pe", lambda e: e.matmul(pa[:, h * 128:(h + 1) * 128], lhsT=keT[:, h, :],
                                                       rhs=qeT[:, h, :], start=True, stop=True),
                              reads=[keTb, qeTb], writes=[pab])
                    am, amb = amr.next()
                    tk.op("dve", lambda e: e.tensor_tensor(out=am[:], in0=pa[:], in1=maskr[:], op=ALU.mult),
                          reads=[pab, cb], writes=[amb])
                    if lvl < 6:
                        continue
                    po, pob = psr.next()
                    for h in range(4):
                        tk.op("pe", lambda e: e.matmul(po[:, h * 128:(h + 1) * 128], lhsT=am[:, h * 128:(h + 1) * 128],
                                                       rhs=v_[:, h * 128:(h + 1) * 128], start=(h == 0), stop=False,
                                                       skip_group_check=True),
                              reads=[amb, vb_], writes=[pob])
                    corder = (0, 1) if dr == 0 else (1, 0)
                    for ci in corder:
                        cr_ = slice(64 * ci, 64 * ci + 64)
                        for h in range(4):
                            hs = slice(h * 128, (h + 1) * 128)
                            tk.op("pe", lambda e: e.matmul(po[cr_, hs], lhsT=qeT[:, h, cr_], rhs=Sb[:, h, :],
                                                           start=False, stop=True, skip_group_check=True),
                                  reads=[qeTb, Sbb[h]], writes=[pob])
                        if (dr == 0 and t == NT - 1 and ci == 1) or (dr == 1 and t == 0 and ci == 0):
                            continue
                        pkv, pkvb = kvr.next()
                        for h in range(4):
                            hs = slice(h * 128, (h + 1) * 128)
                            tk.op("pe", lambda e: e.matmul(pkv[:, hs], lhsT=ks[cr_, hs], rhs=v_[cr_, hs], start=True,
                                                           stop=True),
                                  reads=[ksb, vb_], writes=[pkvb])
                        for h in range(4):
                            hs = slice(h * 128, (h + 1) * 128)
                            tk.op("dve", lambda e: e.scalar_tensor_tensor(out=S[:, h, :], in0=S[:, h, :],
                                                                          scalar=dec[:, h, ci:ci + 1], in1=pkv[:, hs],
                                                                          op0=ALU.mult, op1=ALU.add),
                                  reads=[Sbuf[h], decb, pkvb], writes=[Sbuf[h]])
                            tk.op("act", lambda e: e.activation(out=Sb[:, h, :], in_=S[:, h, :], func=AF.Copy),
                                  reads=[Sbuf[h]], writes=[Sbb[h]])
                    if lvl < 7:
                        continue
                    os_, osb_ = osr.next()
                    if dr == 0:
                        tk.op("dve", lambda e: e.tensor_copy(out=os_[:], in_=po[:]), reads=[pob], writes=[osb_])
                        tk.dma("pool", of[rows, :], os_[:], reads=[osb_])
                        continue
                    tk.op("dve", lambda e: e.tensor_tensor(out=os_[:], in0=po[:], in1=of_[:], op=ALU.add),
                          reads=[pob, ofb], writes=[osb_])
                    e4, e4b = er.next()
                    tk.op("pool", lambda e: e.tensor_tensor(out=e4[:], in0=os_[:], in1=os_[:], op=ALU.mult),
                          reads=[osb_], writes=[e4b])
                    sm, smb = smr.next()
                    tk.op("dve", lambda e: e.tensor_reduce(out=sm[:, 0:4], in_=e4[:].rearrange("p (h d) -> p h d", h=4),
                                                           axis=AX.X, op=ALU.add),
                          reads=[e4b], writes=[smb])
                    tk.op("act", lambda e: e.activation(out=sm[:, 4:8], in_=sm[:, 0:4], func=AF.Ln, scale=1.0 / 128,
                                                        bias=self.epsc[:, 0:1]),
                          reads=[smb, self.epscb], writes=[smb])
                    tk.op("act", lambda e: e.activation(out=sm[:, 4:8], in_=sm[:, 4:8], func=AF.Exp, scale=-0.5),
                          reads=[smb], writes=[smb])
                    for h in range(4):
                        hs = slice(h * 128, (h + 1) * 128)
                        tk.op("dve", lambda e: e.scalar_tensor_tensor(out=os_[:, hs], in0=os_[:, hs],
                                                                      scalar=sm[:, 4 + h:5 + h], in1=glag[:, hs],
                                                                      op0=ALU.mult, op1=ALU.mult),
                              reads=[osb_, smb, glagb], writes=[osb_])
                    yb_, ybb = ybr.next()
                    tk.op("pool", lambda e: e.tensor_tensor(out=yb_[:], in0=os_[:], in1=gd_[:], op=ALU.mult),
                          reads=[osb_, gdb], writes=[ybb])
                    pt2, pt2b = ptr.next()
                    for h in range(4):
                        tk.op("pe", lambda e: e.transpose(out=pt2[:, h * 128:(h + 1) * 128],
                                                          in_=yb_[:, h * 128:(h + 1) * 128], identity=self.ident[:]),
                              reads=[ybb, self.identb], writes=[pt2b])
                    yT, yTb = yTr.next()
                    tk.op("act", lambda e: e.activation(out=yT[:].rearrange("p h c -> p (h c)"), in_=pt2[:, 0:512],
                                                        func=AF.Copy),
                          reads=[pt2b], writes=[yTb])
                    tk.dma("pool", ydT[:, rows].rearrange("(h p) t -> p h t", p=128), yT[:], reads=[yTb])
                tk.barrier()

    def phase_final(self, T, x_src, y_dst, fing, fingb):
        tk = self.tk
        with contextlib.ExitStack() as es:
            xr = self.sb_ring(es, 3, [128, 1024], F32, "xf")
            jr = self.sb_ring(es, 2, [128, 1024], BF16, "jf")
            sr = self.sb_ring(es, 3, [128, 2], F32, "sf")
            for t in range(T // 128):
                rows = slice(t * 128, (t + 1) * 128)
                xt, xtb = xr.next()
                tk.dma("sp", xt[:], x_src[rows, :], writes=[xtb])
                jt, jtb = jr.next()
                sm, smb = sr.next()
                tk.op("act", lambda e: e.activation(out=jt[:], in_=xt[:], func=AF.Square, accum_out=sm[:, 0:1]),
                      reads=[xtb], writes=[jtb, smb])
                tk.op("act", lambda e: e.activation(out=sm[:, 1:2], in_=sm[:, 0:1], func=AF.Ln, scale=1.0 / D,
                                                    bias=self.epsc[:, 0:1]),
                      reads=[smb, self.epscb], writes=[smb])
                tk.op("act", lambda e: e.activation(out=sm[:, 1:2], in_=sm[:, 1:2], func=AF.Exp, scale=-0.5),
                      reads=[smb], writes=[smb])
                tk.op("dve", lambda e: e.scalar_tensor_tensor(out=xt[:], in0=xt[:], scalar=sm[:, 1:2], in1=fing[:],
                                                              op0=ALU.mult, op1=ALU.mult),
                      reads=[xtb, smb, fingb], writes=[xtb])
                tk.dma("pool", y_dst[rows, :], xt[:], reads=[xtb])
        tk.barrier()

    def scratch(self, T):
        d = self.dram
        n = "_%d" % T
        S = {}
        S["x"] = d("x_scr" + n, [T, 1024], F32)
        S["qaT"] = d("qaT" + n, [1024, T], BF16)
        S["kaT"] = d("kaT" + n, [256, T], BF16)
        S["va"] = d("va" + n, [T, 256], BF16)
        S["ga"] = d("ga" + n, [T, 1024], BF16)
        S["qbT"] = [d("qbT%d" % p + n, [512, T], BF16) for p in range(3)]
        S["kbT"] = [d("kbT%d" % p + n, [512, T], BF16) for p in range(3)]
        S["vb"] = [d("vb%d" % p + n, [T, 512], BF16) for p in range(3)]
        S["gb"] = d("gb" + n, [T, 512], BF16)
        S["ya"] = d("ya" + n, [T, 1024], BF16)
        S["nb"] = [d("nb%d" % p + n, [T, 520], F32) for p in range(3)]
        S["qcT"] = d("qcT" + n, [1024, T], BF16)
        S["kcT"] = d("kcT" + n, [1024, T], BF16)
        S["vc"] = d("vc" + n, [T, 1024], BF16)
        S["gcT"] = d("gcT" + n, [1024, T], BF16)
        S["qd"] = d("qd" + n, [T, 512], F32)
        S["kd"] = d("kd" + n, [T, 512], F32)
        S["vd"] = d("vd" + n, [T, 512], BF16)
        S["gd"] = d("gd" + n, [T, 512], BF16)
        S["adT"] = d("adT" + n, [32, T], F32)
        S["ycT"] = d("ycT" + n, [1024, T], BF16)
        S["of"] = d("of" + n, [T, 512], F32)
        S["ydT"] = d("ydT" + n, [512, T], BF16)
        return S

    def build(self, layers=(0, 1, 2, 3), final=True):
        tk = self.tk
        I, O = "ExternalInput", "ExternalOutput"
        d = self.dram
        xs = [d("x%d" % i, [T, 1024], F32, I) for i, T in enumerate(self.seq_lens)]
        ys = [d("y%d" % i, [T, 1024], F32, O) for i, T in enumerate(self.seq_lens)]
        ewi = d("even_w_in", [2, 1024, EVEN_COLS], F32, I)
        ewo = d("even_w_out", [2, 1536, 1024], F32, I)
        owi = d("odd_w_in", [2, 1024, ODD_COLS], F32, I)
        owo = d("odd_w_out", [2, 1536, 1024], F32, I)
        g_t = d("g_t", [128, 32], F32, I)
        fin_g = d("fin_g", [128, 1024], F32, I)
        sink_d = d("sink", [128, 32], F32, I)
        dlam_d = d("dlam", [128, 512], F32, I)
        subg_d = d("subg", [128, 2], F32, I)
        waug_d = d("waug", [2, 2, 17, 512], F32, I)
        glag_d = d("glag", [128, 2, 512], F32, I)
        ident_d = d("ident", [128, 128], BF16, I)
        btA_d = d("btA", [128, 3, 16, 128], BF16, I)
        btB_d = [d("btB%d" % p, [128, 3, 8, 128], BF16, I) for p in range(3)]
        cst = {"UL": d("cUL", [128, 8, 512], F32, I), "UR": d("cUR", [128, 8, 512], F32, I),
               "BL": d("cBL", [128, 8, 64], F32, I), "BR": d("cBR", [128, 8, 64], F32, I),
               "T1": d("cT1", [128, 8, 7, 128], BF16, I), "CR": d("cCR", [1, 8, 7, 128], BF16, I)}
        gcst = {"CM": d("gCM", [2, 128, 2, 128], F32, I), "MASK": d("gMASK", [2, 128, 512], BF16, I),
                "IND": d("gIND", [128, 2], F32, I)}
        wbi = [d("wbi%d" % l, [1024, EVEN_COLS if l % 2 == 0 else ODD_COLS], BF16) for l in range(4)]
        wbo = [d("wbo%d" % l, [1536, 1024], BF16) for l in range(4)]
        scr = {T: self.scratch(T) for T in sorted(set(self.seq_lens))}
        dil = [1, 4, 16]
        with contextlib.ExitStack() as es:
            self.ident, self.identb = self.load_const(es, ident_d, [128, 128], BF16, "ident")
            gt, gtb = self.load_const(es, g_t, [128, 32], F32, "gt")
            fing, fingb = self.load_const(es, fin_g, [128, 1024], F32, "fing")
            btA, btAb = self.load_const(es, btA_d, [128, 3, 16, 128], BF16, "btA")
            btB = [self.load_const(es, btB_d[p], [128, 3, 8, 128], BF16, "btB") for p in range(3)]
            esink, esinkb = self.load_const(es, sink_d, [128, 32], F32, "esink")
            tk.op("act", lambda e: e.activation(out=esink[:], in_=esink[:], func=AF.Exp), reads=[esinkb],
                  writes=[esinkb])
            glag, glagb = self.load_const(es, glag_d, [128, 2, 512], F32, "glag")
            cst2 = self.sb(es, [128, 2], F32, "cst2")
            self.epscb = Buf()
            tk.op("pool", lambda e: e.memset(cst2[:, 0:1], EPS), writes=[self.epscb])
            tk.op("pool", lambda e: e.memset(cst2[:, 1:2], 1.0), writes=[self.epscb])
            self.epsc = cst2[:, 0:1]
            self.onec = cst2[:, 1:2]
            dl, dlb = self.load_const(es, dlam_d, [128, 512], F32, "dl")
            sg, sgb = self.load_const(es, subg_d, [128, 2], F32, "sg")
            lam = self.sb(es, [128, 8], F32, "lam")
            lamb = Buf()
            prod = self.sb(es, [128, 4, 64], F32, "prod")
            dl4 = dl[:].rearrange("p (j a d) -> p j a d", j=2, a=4)
            for j in range(2):
                for a in range(2):
                    tk.op("dve", lambda e: e.tensor_tensor(out=prod[:, 2 * j + a, :], in0=dl4[:, j, 2 * a, :],
                                                           in1=dl4[:, j, 2 * a + 1, :], op=ALU.mult),
                          reads=[dlb], writes=[lamb])
            tk.op("dve", lambda e: e.tensor_reduce(out=lam[:, 0:4], in_=prod[:], axis=AX.X, op=ALU.add),
                  reads=[lamb], writes=[lamb])
            tk.op("act", lambda e: e.activation(out=lam[:, 0:4], in_=lam[:, 0:4], func=AF.Exp), reads=[lamb],
                  writes=[lamb])
            for j in range(2):
                lam_init = 0.8 - 0.6 * math.exp(-0.3 * (2 * j + 1))
                tk.op("dve", lambda e: e.scalar_tensor_tensor(out=lam[:, 4 + j:5 + j], in0=lam[:, 2 * j + 1:2 * j + 2],
                                                              scalar=-lam_init, in1=lam[:, 2 * j:2 * j + 1],
                                                              op0=ALU.add, op1=ALU.subtract),
                      reads=[lamb], writes=[lamb])
                tk.op("dve", lambda e: e.tensor_scalar(out=sg[:, j:j + 1], in0=sg[:, j:j + 1], scalar1=1.0 - lam_init,
                                                       scalar2=None, op0=ALU.mult), reads=[sgb], writes=[sgb])
            srcs, dsts, gcs = [], [], []
            for l in layers:
                j = l // 2
                srcs += [ewi[j] if l % 2 == 0 else owi[j], ewo[j] if l % 2 == 0 else owo[j]]
                dsts += [wbi[l], wbo[l]]
                gcs += [(lambda k, l=l: (gt[:, 8 * l + k:8 * l + k + 1], gtb)), None]
            self.prep_weights(srcs, dsts, gcs)
            for si, T in enumerate(self.seq_lens):
                S = scr[T]
                first = True
                for l in layers:
                    j = l // 2
                    x_src = xs[si] if first else S["x"]
                    first = False
                    if l % 2 == 0:
                        fm = [(0, 1024, 1, S["qaT"], 0, "copy", BF16), (1024, 256, 1, S["kaT"], 0, "copy", BF16)]
                        tm = [(1280, 256, 1, S["va"], 0, "copy", BF16), (1536, 1024, 1, S["ga"], 0, "silu", BF16)]
                        for p in range(3):
                            fm.append((2560 + 512 * p, 512, dil[p], S["qbT"][p], 0, "copy", BF16))
                            fm.append((4096 + 512 * p, 512, dil[p], S["kbT"][p], 0, "copy", BF16))
                            tm.append((5632 + 512 * p, 512, dil[p], S["vb"][p], 0, "copy", BF16))
                        tm.append((7168, 512, 1, S["gb"], 0, "silu", BF16))
                        self.phase_proj(T, x_src, wbi[l], fm, tm)
                        self.phase_window(T, 1, 16, 4, S["qaT"], S["kaT"], S["va"], btA, btAb, "A", S["ya"],
                                          gate=S["ga"], esink=esink[:, 16 * j:16 * j + 16], esinkb=esinkb)
                        for p in range(3):
                            self.phase_window(T // dil[p], dil[p], 8, 1, S["qbT"][p], S["kbT"][p], S["vb"][p],
                                              btB[p][0], btB[p][1], "B", S["nb"][p], dil=dil[p])
                        self.phase_out(T, x_src, S["x"], wbo[l], [(S["ya"], 0, 1024)], [], nbs=S["nb"], gb=S["gb"])
                    else:
                        fm = [(0, 1024, 1, S["qcT"], 0, "copy", BF16), (1024, 1024, 1, S["kcT"], 0, "copy", BF16),
                              (3072, 1024, 1, S["gcT"], 0, "silu", BF16), (6144, 32, 1, S["adT"], 0, "copy", F32)]
                        tm = [(2048, 1024, 1, S["vc"], 0, "copy", BF16), (4096, 512, 1, S["qd"], 0, "copy", F32),
                              (4608, 512, 1, S["kd"], 0, "copy", F32), (5120, 512, 1, S["vd"], 0, "copy", BF16),
                              (5632, 512, 1, S["gd"], 0, "silu", BF16)]
                        self.phase_proj(T, x_src, wbi[l], fm, tm)
                        if "diff" not in getattr(self, "skip", ()):
                          self.phase_diff(T, S["qcT"], S["kcT"], S["vc"], S["gcT"], S["ycT"], cst,
                                        lam[:, 4 + j:5 + j], lamb, sg[:, j:j + 1], sgb)
                        if "gla" not in getattr(self, "skip", ()):
                          self.phase_gla(T, S["qd"], S["kd"], S["vd"], S["adT"], S["gd"], S["of"], S["ydT"],
                                       [waug_d[j, 0], waug_d[j, 1]], gcst, glag[:, j, :], glagb)
                        self.phase_out(T, x_src, S["x"], wbo[l], [], [(S["ycT"], 0, 8), (S["ydT"], 8, 4)])
                if final:
                    self.phase_final(T, S["x"], ys[si], fing, fingb)
                else:
                    self.phase_copy(T, S["x"], ys[si])
        return self.nc

    def phase_copy(self, T, src, dst):
        tk = self.tk
        with contextlib.ExitStack() as es:
            xr = self.sb_ring(es, 3, [128, 1024], F32, "xc")
            for t in range(T // 128):
                xt, xtb = xr.next()
                tk.dma("sp", xt[:], src[t * 128:(t + 1) * 128, :], writes=[xtb])
                tk.dma("pool", dst[t * 128:(t + 1) * 128, :], xt[:], reads=[xtb])
        tk.barrier()


def alibi(n):
    return np.array([2.0 ** (-8.0 * (i + 1) / n) for i in range(n)], dtype=np.float64)


def window_bias_table(nheads, radius, stride):
    b = np.arange(128)[:, None]
    a = np.arange(128)[None, :]
    sl = alibi(nheads)
    out = np.zeros((128, 3, nheads, 128), np.float32)
    for kind in range(3):
        dist = np.abs(a - (b + (kind - 1) * 128))
        for h in range(nheads):
            val = -(sl[h] * stride * 8.0) * dist
            out[:, kind, h, :] = np.where(dist <= radius, val, NEGB)
    return out.astype(ml_dtypes.bfloat16)


def diff_tables():
    sl = alibi(8)
    q = np.arange(512)
    b = np.arange(128)
    UL = np.zeros((128, 8, 512), np.float32)
    UR = np.zeros((128, 8, 512), np.float32)
    BL = np.zeros((128, 8, 64), np.float32)
    BR = np.zeros((128, 8, 64), np.float32)
    T1 = np.zeros((128, 8, 7, 128), np.float32)
    CR = np.zeros((1, 8, 7, 128), np.float32)
    a = np.arange(128)[None, :]
    bb = np.arange(128)[:, None]
    for h in range(8):
        UL[:, h, :] = np.exp(-sl[h] * q)[None, :]
        UR[:, h, :] = np.exp(-sl[h] * (511 - q))[None, :]
        for d in range(64):
            BL[:, h, d] = -sl[h] * (128.0 * d - b)
            BR[:, h, d] = -sl[h] * (128.0 * d + b + 1)
        c = sl[h] * 8.0
        for slot in range(7):
            dlt = slot - 3
            if dlt > 0:
                T1[:, h, slot, :] = -c * (a - bb)
            elif dlt < 0:
                T1[:, h, slot, :] = c * (a - bb)
            else:
                T1[:, h, slot, :] = -c * np.abs(a - bb)
            CR[0, h, slot, :] = -c * 128.0 * abs(dlt)
    return {"UL": UL, "UR": UR, "BL": BL, "BR": BR, "T1": T1.astype(ml_dtypes.bfloat16),
            "CR": CR.astype(ml_dtypes.bfloat16)}


def gla_tables():
    c = np.arange(128)
    same = (c[:, None] // 64) == (c[None, :] // 64)
    CM = np.zeros((2, 128, 2, 128), np.float32)
    MASK = np.zeros((2, 128, 4, 128), np.float32)
    CM[0, :, 0, :] = same & (c[:, None] <= c[None, :])
    CM[0, :, 1, :] = same & (c[:, None] > c[None, :])
    CM[1, :, 0, :] = same & (c[:, None] >= c[None, :])
    CM[1, :, 1, :] = same & (c[:, None] < c[None, :])
    for h in range(4):
        MASK[0, :, h, :] = same & (c[:, None] <= c[None, :])
        MASK[1, :, h, :] = same & (c[:, None] > c[None, :])
    IND = np.zeros((128, 2), np.float32)
    IND[:64, 0] = 1.0
    IND[64:, 1] = 1.0
    return {"CM": CM, "MASK": MASK.reshape(2, 128, 512).astype(ml_dtypes.bfloat16), "IND": IND}


def host_consts(inp):
    f = np.float32
    c = {}
    ng = np.asarray(inp["norm_g"], f)
    c["g_t"] = np.ascontiguousarray(ng.reshape(4, 8, 128).transpose(2, 0, 1).reshape(128, 32))
    c["fin_g"] = np.ascontiguousarray(np.broadcast_to(np.asarray(inp["final_norm_g"], f)[None, :], (128, 1024)))
    c["sink"] = np.ascontiguousarray(np.broadcast_to(np.asarray(inp["sink_logit"], f).reshape(1, 32), (128, 32)))
    c["dlam"] = np.ascontiguousarray(np.broadcast_to(np.asarray(inp["diff_lambda"], f).reshape(1, 512), (128, 512)))
    c["subg"] = np.ascontiguousarray(np.asarray(inp["diff_subln_g"], f).T)
    wg = np.asarray(inp["gla_w_gate2"], f)
    bg = np.asarray(inp["gla_b_gate"], f)
    c["waug"] = np.ascontiguousarray(np.concatenate([bg[:, :, None, :], wg], axis=2))
    gg = np.asarray(inp["gla_norm_g"], f)
    c["glag"] = np.ascontiguousarray(np.broadcast_to(np.tile(gg, (1, 4))[None, :, :], (128, 2, 512)))
    c["ident"] = np.eye(128).astype(ml_dtypes.bfloat16)
    c["btA"] = window_bias_table(16, 128, 1)
    for p, dl in enumerate([1, 4, 16]):
        c["btB%d" % p] = window_bias_table(8, 64, dl)
    for k, v in diff_tables().items():
        c["c" + k] = v
    for k, v in gla_tables().items():
        c["g" + k] = v
    for k in ("even_w_in", "even_w_out", "odd_w_in", "odd_w_out"):
        c[k] = np.ascontiguousarray(np.asarray(inp[k], f))
    return c


_PROG = None


def kernel(**inputs):
    global _PROG
    xp = np.asarray(inputs["x_prompt"], np.float32)
    xsm = np.asarray(inputs["x_sample"], np.float32)
    if _PROG is None:
        _PROG = Prog([2048, 2048, 2048, 2048, 8192]).build()
    nc = _PROG
    c = host_consts(inputs)
    in_maps = []
    for core in range(8):
        m = dict(c)
        for i in range(4):
            m["x%d" % i] = np.ascontiguousarray(xp[4 * core + i])
        m["x4"] = np.ascontiguousarray(xsm[core // 4])
        in_maps.append(m)
    res = run_bass_kernel_spmd(nc, in_maps, core_ids=list(range(8)))
    yp = np.stack([res.results[core]["y%d" % i] for core in range(8) for i in range(4)], axis=0)
    ysm = np.stack([res.results[0]["y4"], res.results[4]["y4"]], axis=0)
    return (np.asarray(yp, np.float32), np.asarray(ysm, np.float32))
```

```python
import contextlib
import math
import os
import numpy as np
import ml_dtypes
import concourse.bass as bass
import concourse.mybir as mybir
from concourse.bass_utils import run_bass_kernel_spmd

F32 = mybir.dt.float32
BF16 = mybir.dt.bfloat16
AF = mybir.ActivationFunctionType
ALU = mybir.AluOpType
AX = mybir.AxisListType

D = 1024
EPS = 1e-6
EVEN_COLS = 7680
ODD_COLS = 6176
NEGB = -30000.0
TMAX = 8192


class Buf:
    __slots__ = ("name", "w", "r")

    def __init__(self, name=""):
        self.name = name
        self.w = None
        self.r = {}


class TK:
    NDMA = 20

    def __init__(self, nc):
        self.nc = nc
        self.engs = {"pe": nc.tensor, "act": nc.scalar, "dve": nc.vector, "pool": nc.gpsimd, "sp": nc.sync}
        self.sems = {}
        self.cnt = {}
        for e in self.engs:
            self.sems[e] = nc.alloc_semaphore("s_" + e)
            self.cnt[e] = 0
        self.waited = {e: {} for e in self.engs}
        self.dq = {}
        for q in ("sp", "pool"):
            lst = []
            for i in range(self.NDMA):
                k = "d_%s_%d" % (q, i)
                self.sems[k] = nc.alloc_semaphore(k)
                self.cnt[k] = 0
                lst.append(k)
            self.dq[q] = [lst, 0]
        self.bar = nc.alloc_semaphore("s_bar")
        self.barcnt = 0
        self.n_ins = 0

    def _wait(self, e, key, val):
        if key == "pe" and e == "pe":
            return
        if self.waited[e].get(key, 0) >= val:
            return
        self.engs[e].wait_ge(self.sems[key], val)
        self.waited[e][key] = val
        self.n_ins += 1

    def _deps(self, e, reads, writes):
        for b in reads:
            if b.w is not None:
                self._wait(e, b.w[0], b.w[1])
        for b in writes:
            if b.w is not None:
                self._wait(e, b.w[0], b.w[1])
            for k, v in b.r.items():
                self._wait(e, k, v)

    def op(self, e, fn, reads=(), writes=()):
        self._deps(e, reads, writes)
        ins = fn(self.engs[e])
        self.cnt[e] += 1
        ins.then_inc(self.sems[e], 1)
        self.n_ins += 1
        c = self.cnt[e]
        for b in reads:
            b.r[e] = c
        for b in writes:
            b.w = (e, c)
            b.r = {}
        return ins

    def dma(self, q, out, in_, reads=(), writes=()):
        lst, idx = self.dq[q]
        key = lst[idx % len(lst)]
        self.dq[q][1] = idx + 1
        if self.cnt[key] > 0:
            self._wait(q, key, self.cnt[key])
        self._deps(q, reads, writes)
        ins = self.engs[q].dma_start(out=out, in_=in_)
        self.cnt[key] += 16
        ins.then_inc(self.sems[key], 16)
        self.n_ins += 1
        c = self.cnt[key]
        for b in reads:
            b.r[key] = c
        for b in writes:
            b.w = (key, c)
            b.r = {}
        return ins

    def barrier(self):
        for q in ("sp", "pool"):
            for key in self.dq[q][0]:
                if self.cnt[key] > 0:
                    self._wait(q, key, self.cnt[key])
        self.barcnt += 1
        n = len(self.engs)
        for e, eng in self.engs.items():
            if self.cnt[e] > 0:
                eng.wait_ge(self.sems[e], self.cnt[e]).then_inc(self.bar, 1)
            else:
                eng.wait_ge(self.bar, 0).then_inc(self.bar, 1)
        for e, eng in self.engs.items():
            eng.wait_ge(self.bar, n * self.barcnt)
            for k in self.sems:
                self.waited[e][k] = self.cnt[k]
        self.n_ins += 2 * n


class Ring:
    def __init__(self, tiles):
        self.tiles = tiles
        self.bufs = [Buf() for _ in tiles]
        self.i = 0

    def next(self):
        k = self.i % len(self.tiles)
        self.i += 1
        return self.tiles[k], self.bufs[k]


class Prog:
    def __init__(self, seq_lens, debug=False):
        self.seq_lens = seq_lens
        self.debug = debug
        self.nc = bass.Bass("TRN2", target_bir_lowering=False)
        self.tk = TK(self.nc)
        self.uid = 0

    def name(self, p):
        self.uid += 1
        return "%s_%d" % (p, self.uid)

    def sb(self, es, shape, dt, name="sb"):
        return es.enter_context(self.nc.sbuf_tensor(self.name(name), list(shape), dt))

    def ps(self, es, shape, dt, name="ps"):
        return es.enter_context(self.nc.psum_tensor(self.name(name), list(shape), dt))

    def sb_ring(self, es, n, shape, dt, name="r"):
        return Ring([self.sb(es, shape, dt, name) for _ in range(n)])

    def ps_ring(self, es, n, shape, dt, name="p"):
        return Ring([self.ps(es, shape, dt, name) for _ in range(n)])

    def dram(self, name, shape, dt, kind=None):
        if kind is None:
            return self.nc.dram_tensor(name, list(shape), dt).ap()
        return self.nc.dram_tensor(name, list(shape), dt, kind=kind).ap()

    def load_const(self, es, dram_ap, shape, dt, name):
        t = self.sb(es, shape, dt, name)
        b = Buf(name)
        self.tk.dma("sp", t[:], dram_ap, writes=[b])
        return t, b

    def prep_weights(self, w_srcs, wb_dsts, g_cols):
        tk = self.tk
        with contextlib.ExitStack() as es:
            CW = 2048
            ld = self.sb_ring(es, 3, [128, CW], F32, "wld")
            st = self.sb_ring(es, 3, [128, CW], BF16, "wst")
            flip = 0
            for src, dst, gc in zip(w_srcs, wb_dsts, g_cols):
                R, C = src.shape
                for k in range(R // 128):
                    for c0 in range(0, C, CW):
                        cw = min(CW, C - c0)
                        lt, lb = ld.next()
                        tk.dma("sp", lt[:, :cw], src[k * 128:(k + 1) * 128, c0:c0 + cw], writes=[lb])
                        stt, sbf = st.next()
                        if gc is not None:
                            gap, gb = gc(k)
                            tk.op("dve", lambda e: e.tensor_scalar(out=stt[:, :cw], in0=lt[:, :cw], scalar1=gap,
                                                                    scalar2=None, op0=ALU.mult),
                                  reads=[lb, gb], writes=[sbf])
                        else:
                            if flip % 2 == 0:
                                tk.op("act", lambda e: e.activation(out=stt[:, :cw], in_=lt[:, :cw], func=AF.Copy),
                                      reads=[lb], writes=[sbf])
                            else:
                                tk.op("dve", lambda e: e.tensor_copy(out=stt[:, :cw], in_=lt[:, :cw]),
                                      reads=[lb], writes=[sbf])
                            flip += 1
                        tk.dma("pool", dst[k * 128:(k + 1) * 128, c0:c0 + cw], stt[:, :cw], reads=[sbf])
        tk.barrier()

    def phase_proj(self, T, x_src, wb, fm_jobs, tm_jobs):
        tk = self.tk
        nsg = T // 2048
        with contextlib.ExitStack() as es:
            xall = self.sb(es, [128, 16, 1024], F32, "xall")
            xb = [Buf() for _ in range(16)]
            hT = self.sb(es, [128, 8, 2048], BF16, "hT")
            hTb = [Buf() for _ in range(16)]
            hbr = self.sb_ring(es, 2, [128, 1024], BF16, "hb")
            junk = self.sb(es, [128, 1024], BF16, "junk")
            junkb = Buf()
            ssq = self.sb(es, [128, 16], F32, "ssq")
            ssqb = [Buf() for _ in range(16)]
            rstd = self.sb(es, [128, 16], F32, "rstd")
            rstdb = Buf()
            wfm = self.sb_ring(es, 3, [128, 8, 128], BF16, "wfm")
            wtm = self.sb_ring(es, 2, [128, 8, 512], BF16, "wtm")
            stb = self.sb_ring(es, 4, [128, 512], BF16, "stb")
            stf = self.sb_ring(es, 3, [128, 512], F32, "stf")
            pst = self.ps_ring(es, 2, [128, 1024], BF16, "pst")
            pso = self.ps_ring(es, 5, [128, 512], F32, "pso")
            ident, identb = self.ident, self.identb
            ev = 0
            for g in range(nsg):
                t0 = g * 2048
                for t in range(16):
                    tk.dma("sp", xall[:, t, :], x_src[t0 + t * 128:t0 + (t + 1) * 128, :], writes=[xb[t]])
                for t in range(16):
                    tk.op("act", lambda e: e.activation(out=junk[:], in_=xall[:, t, :], func=AF.Square,
                                                        accum_out=ssq[:, t:t + 1]),
                          reads=[xb[t]], writes=[junkb, ssqb[t]])
                tk.op("dve", lambda e: e.tensor_scalar(out=rstd[:], in0=ssq[:], scalar1=1.0 / D, scalar2=EPS,
                                                       op0=ALU.mult, op1=ALU.add), reads=ssqb, writes=[rstdb])
                tk.op("act", lambda e: e.activation(out=rstd[:], in_=rstd[:], func=AF.Sqrt), reads=[rstdb],
                      writes=[rstdb])
                tk.op("dve", lambda e: e.reciprocal(out=rstd[:], in_=rstd[:]), reads=[rstdb], writes=[rstdb])
                for t in range(16):
                    hb, hbb = hbr.next()
                    tk.op("act", lambda e: e.activation(out=hb[:], in_=xall[:, t, :], func=AF.Copy,
                                                        scale=rstd[:, t:t + 1]),
                          reads=[xb[t], rstdb], writes=[hbb])
                    pt, ptb = pst.next()
                    for k in range(8):
                        tk.op("pe", lambda e: e.transpose(out=pt[:, k * 128:(k + 1) * 128],
                                                          in_=hb[:, k * 128:(k + 1) * 128], identity=ident[:]),
                              reads=[hbb, identb], writes=[ptb])
                    tk.op("dve", lambda e: e.tensor_copy(out=hT[:, :, t * 128:(t + 1) * 128],
                                                         in_=pt[:].rearrange("p (k c) -> p k c", k=8)),
                          reads=[ptb], writes=[hTb[t]])

                def evac(po, pob, mc, cw, func, dt, shape3=None):
                    nonlocal ev
                    ring = stb if dt == BF16 else stf
                    s_t, s_b = ring.next()
                    if func == "silu":
                        tk.op("act", lambda e: e.activation(out=s_t[:mc, :cw], in_=po[:mc, :cw], func=AF.Silu),
                              reads=[pob], writes=[s_b])
                    else:
                        if True:
                            tk.op("act", lambda e: e.activation(out=s_t[:mc, :cw], in_=po[:mc, :cw], func=AF.Copy),
                                  reads=[pob], writes=[s_b])
                        else:
                            tk.op("dve", lambda e: e.tensor_copy(out=s_t[:mc, :cw], in_=po[:mc, :cw]),
                                  reads=[pob], writes=[s_b])
                        ev += 1
                    return s_t, s_b

                for (col0, ncols, dil, dst, row0, func, dt) in fm_jobs:
                    L = T // dil
                    for cc in range((ncols + 127) // 128):
                        mc = min(128, ncols - cc * 128)
                        c = col0 + cc * 128
                        w, wbuf = wfm.next()
                        tk.dma("sp", w[:, :, :mc], wb[:, c:c + mc].rearrange("(k p) c -> p k c", p=128),
                               writes=[wbuf])
                        for s in range(4):
                            po, pob = pso.next()
                            if dil == 1:
                                rd = hTb[4 * s:4 * s + 4]
                            else:
                                rd = hTb
                            for k in range(8):
                                if dil == 1:
                                    rhs = hT[:, k, s * 512:(s + 1) * 512]
                                    out = po[:mc, :]
                                elif dil == 4:
                                    rhs = hT[:, k, :].rearrange("p (m r) -> p r m", r=4)[:, s, :]
                                    out = po[:mc, :]
                                else:
                                    rhs = hT[:, k, :].rearrange("p (m r) -> p r m", r=16)[:, 4 * s:4 * s + 4, :]
                                    out = po[:mc, :].rearrange("p (r m) -> p r m", r=4)
                                tk.op("pe", lambda e: e.matmul(out, lhsT=w[:, k, :mc], rhs=rhs, start=(k == 0),
                                                               stop=(k == 7)),
                                      reads=[wbuf] + rd, writes=[pob])
                            s_t, s_b = evac(po, pob, mc, 512, func, dt)
                            r0 = row0 + cc * 128
                            if dil == 1:
                                tk.dma("pool", dst[r0:r0 + mc, t0 + s * 512:t0 + (s + 1) * 512], s_t[:mc, :],
                                       reads=[s_b])
                            elif dil == 4:
                                tk.dma("pool", dst[r0:r0 + mc, s * L + g * 512:s * L + (g + 1) * 512], s_t[:mc, :],
                                       reads=[s_b])
                            else:
                                d3 = dst.rearrange("c (r l) -> c r l", r=16)
                                tk.dma("pool", d3[r0:r0 + mc, 4 * s:4 * s + 4, g * 128:(g + 1) * 128],
                                       s_t[:mc, :].rearrange("p (r m) -> p r m", r=4), reads=[s_b])
                for (col0, ncols, dil, dst, cd0, func, dt) in tm_jobs:
                    L = T // dil
                    for cb in range((ncols + 511) // 512):
                        cw = min(512, ncols - cb * 512)
                        c = col0 + cb * 512
                        w, wbuf = wtm.next()
                        tk.dma("sp", w[:, :, :cw], wb[:, c:c + cw].rearrange("(k p) c -> p k c", p=128),
                               writes=[wbuf])
                        for u in range(16):
                            po, pob = pso.next()
                            if dil == 1:
                                rd = [hTb[u]]
                                rows = t0 + u * 128
                            elif dil == 4:
                                rd = hTb
                                rows = (u // 4) * L + g * 512 + (u % 4) * 128
                            else:
                                rd = hTb
                                rows = u * L + g * 128
                            for k in range(8):
                                if dil == 1:
                                    lhsT = hT[:, k, u * 128:(u + 1) * 128]
                                elif dil == 4:
                                    j = u % 4
                                    lhsT = hT[:, k, :].rearrange("p (m r) -> p r m", r=4)[:, u // 4,
                                                                                         j * 128:(j + 1) * 128]
                                else:
                                    lhsT = hT[:, k, :].rearrange("p (m r) -> p r m", r=16)[:, u, :]
                                tk.op("pe", lambda e: e.matmul(po[:, :cw], lhsT=lhsT, rhs=w[:, k, :cw],
                                                               start=(k == 0), stop=(k == 7)),
                                      reads=[wbuf] + rd, writes=[pob])
                            s_t, s_b = evac(po, pob, 128, cw, func, dt)
                            tk.dma("pool", dst[rows:rows + 128, cd0 + cb * 512:cd0 + cb * 512 + cw], s_t[:, :cw],
                                   reads=[s_b])
        tk.barrier()

    def phase_window(self, L, ncls, Hq, G, qT, kT, v, btab, btabb, mode, out, gate=None, esink=None,
                     esinkb=None, dil=1):
        tk = self.tk
        Hkv = Hq // G
        NTc = L // 128
        ngrp = Hq // 4
        LA = 3
        with contextlib.ExitStack() as es:
            qwr = self.sb_ring(es, 6, [128, Hq, 128], BF16, "qw")
            kwr = self.sb_ring(es, 6, [128, Hkv, 384], BF16, "kw")
            for ring in (qwr, kwr):
                for t_, b_ in zip(ring.tiles, ring.bufs):
                    tk.op("pool", lambda e: e.memset(t_[:], 0.0), writes=[b_])
            vwr = self.sb_ring(es, 6, [128, 3, Hkv, 65], BF16, "vw")
            for vt, vb_ in zip(vwr.tiles, vwr.bufs):
                tk.op("pool", lambda e: e.memset(vt[:], 1.0), writes=[vb_])
            ptr = self.sb_ring(es, 6, [128, 512], BF16, "pT")
            scr = self.ps_ring(es, 4, [128, 512], F32, "sc")
            accr = self.ps_ring(es, 4, [128, 512], F32, "acc")
            small = self.sb_ring(es, 6, [128, 8], F32, "sm")
            if mode == "A":
                gr = self.sb_ring(es, 6, [128, Hq * 64], BF16, "gate")
                yr = self.sb_ring(es, 6, [128, Hq * 64], F32, "yf")
                ybr = self.sb_ring(es, 2, [128, Hq * 64], BF16, "yb")
            else:
                yr = self.sb_ring(es, 6, [128, Hq, 65], F32, "nb")
                out3 = out.rearrange("(m r) c -> r m c", r=dil)
            tiles = [(r, t) for r in range(ncls) for t in range(NTc)]

            def load_tile(n):
                r, t = tiles[n]
                c0 = r * L
                k0 = max(t - 1, 0)
                k1 = min(t + 1, NTc - 1)
                nk = k1 - k0 + 1
                ctx = {"r": r, "t": t, "k0": k0, "nk": nk}
                qw, qwb = qwr.next()
                tk.dma("sp", qw[0:64], qT[:, c0 + t * 128:c0 + (t + 1) * 128].rearrange("(h d) t -> d h t", d=64),
                       writes=[qwb])
                kw, kwb = kwr.next()
                tk.dma("sp", kw[0:64, :, :nk * 128],
                       kT[:, c0 + k0 * 128:c0 + (k1 + 1) * 128].rearrange("(h d) t -> d h t", d=64), writes=[kwb])
                vw, vwb = vwr.next()
                for j in range(nk):
                    tk.dma("sp", vw[:, j, :, 0:64],
                           v[c0 + (k0 + j) * 128:c0 + (k0 + j + 1) * 128, :].rearrange("p (h d) -> p h d", d=64),
                           writes=[vwb])
                ctx.update(q=(qw, qwb), k=(kw, kwb), v=(vw, vwb))
                if mode == "A":
                    gt, gtb = gr.next()
                    tk.dma("sp", gt[:], gate[t * 128:(t + 1) * 128, :], writes=[gtb])
                    ctx["gate"] = (gt, gtb)
                ctx["y"] = yr.next()
                return ctx

            ctxs = {0: load_tile(0)}
            items = []
            for n, (r, t) in enumerate(tiles):
                nk = min(t + 1, NTc - 1) - max(t - 1, 0) + 1
                for g in range(ngrp):
                    for j in range(nk):
                        items.append({"n": n, "g": g, "j": j, "nk": nk, "tstart": g == 0 and j == 0,
                                      "tlast": g == ngrp - 1 and j == nk - 1})

            def stage_a(it):
                n = it["n"]
                if it["tstart"] and n + 1 < len(tiles):
                    ctxs[n + 1] = load_tile(n + 1)
                c = ctxs[n]
                g, j = it["g"], it["j"]
                kind = (c["k0"] + j) - c["t"] + 1
                qw, qwb = c["q"]
                kw, kwb = c["k"]
                sc, scb = scr.next()
                it["sc"] = (sc, scb)
                tk.op("pe", lambda e: e.matmul(sc[:], lhsT=self.ident[:], rhs=btab[:, kind, 4 * g:4 * g + 4, :],
                                               start=True, stop=False), reads=[self.identb, btabb], writes=[scb])
                if G == 4:
                    tk.op("pe", lambda e: e.matmul(sc[:], lhsT=kw[:, g, j * 128:(j + 1) * 128],
                                                   rhs=qw[:, 4 * g:4 * g + 4, :], start=False, stop=True),
                          reads=[kwb, qwb], writes=[scb])
                else:
                    for hh in range(4):
                        h = 4 * g + hh
                        tk.op("pe", lambda e: e.matmul(sc[:, hh * 128:(hh + 1) * 128],
                                                       lhsT=kw[:, h, j * 128:(j + 1) * 128], rhs=qw[:, h, :],
                                                       start=False, stop=(hh == 3)),
                              reads=[kwb, qwb], writes=[scb])

            def stage_b(it):
                n = it["n"]
                c = ctxs[n]
                g, j, nk = it["g"], it["j"], it["nk"]
                sc, scb = it["sc"]
                vw, vwb = c["v"]
                yt, ytb = c["y"]
                pT, pTb = ptr.next()
                tk.op("act", lambda e: e.activation(out=pT[:], in_=sc[:], func=AF.Exp, scale=0.125),
                      reads=[scb], writes=[pTb])
                if j == 0:
                    c["acc"] = accr.next()
                ac, acb = c["acc"]
                for hh in range(4):
                    h = 4 * g + hh
                    hv = g if G == 4 else h
                    tk.op("pe", lambda e: e.matmul(ac[:, hh * 128:hh * 128 + 65], lhsT=pT[:, hh * 128:(hh + 1) * 128],
                                                   rhs=vw[:, j, hv, :], start=(j == 0 and hh == 0),
                                                   stop=(j == nk - 1), skip_group_check=True),
                          reads=[pTb, vwb], writes=[acb])
                if j < nk - 1:
                    return
                ac3 = ac[:].rearrange("p (h c) -> p h c", h=4)[:, :, 0:65]
                if mode == "A":
                    sm, smb = small.next()
                    tk.op("dve", lambda e: e.tensor_tensor(out=sm[:, 0:4], in0=ac3[:, :, 64],
                                                           in1=esink[:, 4 * g:4 * g + 4], op=ALU.add),
                          reads=[acb, esinkb], writes=[smb])
                    tk.op("dve", lambda e: e.reciprocal(out=sm[:, 4:8], in_=sm[:, 0:4]), reads=[smb], writes=[smb])
                    for hh in range(4):
                        h = 4 * g + hh
                        tk.op("dve", lambda e: e.tensor_scalar(out=yt[:, h * 64:(h + 1) * 64], in0=ac3[:, hh, 0:64],
                                                               scalar1=sm[:, 4 + hh:5 + hh], scalar2=None,
                                                               op0=ALU.mult),
                              reads=[acb, smb], writes=[ytb])
                else:
                    tk.op("dve", lambda e: e.tensor_copy(out=yt[:, 4 * g:4 * g + 4, :], in_=ac3),
                          reads=[acb], writes=[ytb])
                if not it["tlast"]:
                    return
                r, t = c["r"], c["t"]
                if mode == "A":
                    gt, gtb = c["gate"]
                    yb_, ybb = ybr.next()
                    tk.op("pool", lambda e: e.tensor_tensor(out=yb_[:], in0=yt[:], in1=gt[:], op=ALU.mult),
                          reads=[ytb, gtb], writes=[ybb])
                    tk.dma("pool", out[t * 128:(t + 1) * 128, :], yb_[:], reads=[ybb])
                else:
                    tk.dma("pool", out3[r, t * 128:(t + 1) * 128, :], yt[:].rearrange("p h c -> p (h c)"),
                           reads=[ytb])
                del ctxs[n]

            ni = len(items)
            for i in range(ni + LA):
                if i < ni:
                    stage_a(items[i])
                if i - LA >= 0:
                    stage_b(items[i - LA])
        tk.barrier()

    def phase_out(self, T, x_src, x_dst, wob, ytm_src, yT_srcs, nbs=None, gb=None):
        tk = self.tk
        NT = T // 128
        with contextlib.ExitStack() as es:
            wo = self.sb(es, [128, 12, 1024], BF16, "wo")
            wob_ = Buf()
            tk.dma("sp", wo[:], wob.rearrange("(k p) c -> p k c", p=128), writes=[wob_])
            ytr = self.sb_ring(es, 3, [128, 1536], BF16, "ytm")
            yTr = self.sb_ring(es, 3, [128, 12, 128], BF16, "yT")
            xr = self.sb_ring(es, 4, [128, 1024], F32, "xo")
            if nbs is not None:
                nbr = [self.sb_ring(es, 3, [128, 8, 65], F32, "nbl") for _ in range(3)]
                gbr = self.sb_ring(es, 3, [128, 512], BF16, "gbl")
                rr = self.sb_ring(es, 2, [128, 8], F32, "rr")
                ybf = self.sb_ring(es, 2, [128, 512], F32, "ybf")
            pst = self.ps_ring(es, 2, [128, 1024], BF16, "ptr")
            pso = self.ps_ring(es, 4, [128, 512], F32, "pso")
            def load(t):
                rows = slice(t * 128, (t + 1) * 128)
                c = {}
                c["x"] = xr.next()
                tk.dma("sp", c["x"][0][:], x_src[rows, :], writes=[c["x"][1]])
                c["yT"] = yTr.next()
                for (srcT, ch0, nchk) in yT_srcs:
                    tk.dma("sp", c["yT"][0][:, ch0:ch0 + nchk, :],
                           srcT[:, t * 128:(t + 1) * 128].rearrange("(k p) t -> p k t", p=128), writes=[c["yT"][1]])
                if ytm_src or nbs is not None:
                    c["yt"] = ytr.next()
                    for (src, c0, w) in ytm_src:
                        tk.dma("sp", c["yt"][0][:, c0:c0 + w], src[rows, :], writes=[c["yt"][1]])
                    if nbs is not None:
                        c["nl"] = []
                        for p in range(3):
                            nt_, ntb_ = nbr[p].next()
                            tk.dma("sp", nt_[:].rearrange("p h c -> p (h c)"), nbs[p][rows, :], writes=[ntb_])
                            c["nl"].append((nt_, ntb_))
                        c["gb"] = gbr.next()
                        tk.dma("sp", c["gb"][0][:], gb[rows, :], writes=[c["gb"][1]])
                return c

            nxt = load(0)
            for t in range(NT):
                rows = slice(t * 128, (t + 1) * 128)
                cur = nxt
                if t + 1 < NT:
                    nxt = load(t + 1)
                xt, xtb = cur["x"]
                yT, yTb = cur["yT"]
                ntm = 0
                if ytm_src or nbs is not None:
                    yt, ytb = cur["yt"]
                    for (src, c0, w) in ytm_src:
                        ntm = max(ntm, c0 + w)
                    if nbs is not None:
                        nl = cur["nl"]
                        gbt, gbtb = cur["gb"]
                        n0, n0b = nl[0]
                        tk.op("pool", lambda e: e.tensor_tensor(out=n0[:], in0=n0[:], in1=nl[1][0][:], op=ALU.add),
                              reads=[n0b, nl[1][1]], writes=[n0b])
                        tk.op("pool", lambda e: e.tensor_tensor(out=n0[:], in0=n0[:], in1=nl[2][0][:], op=ALU.add),
                              reads=[n0b, nl[2][1]], writes=[n0b])
                        r_, rb_ = rr.next()
                        tk.op("dve", lambda e: e.reciprocal(out=r_[:], in_=n0[:, :, 64]), reads=[n0b], writes=[rb_])
                        yf, yfb = ybf.next()
                        for h in range(8):
                            tk.op("dve", lambda e: e.tensor_scalar(out=yf[:, h * 64:(h + 1) * 64], in0=n0[:, h, 0:64],
                                                                   scalar1=r_[:, h:h + 1], scalar2=None,
                                                                   op0=ALU.mult),
                                  reads=[n0b, rb_], writes=[yfb])
                        tk.op("pool", lambda e: e.tensor_tensor(out=yt[:, 1024:1536], in0=yf[:], in1=gbt[:],
                                                                op=ALU.mult),
                              reads=[yfb, gbtb], writes=[ytb])
                        ntm = 1536
                    nch = ntm // 128
                    for c in range(nch):
                        if c % 8 == 0:
                            pt, ptb = pst.next()
                        tk.op("pe", lambda e: e.transpose(out=pt[:, (c % 8) * 128:(c % 8 + 1) * 128],
                                                          in_=yt[:, c * 128:(c + 1) * 128], identity=self.ident[:]),
                              reads=[ytb, self.identb], writes=[ptb])
                        if c % 8 == 7 or c == nch - 1:
                            cs = (c // 8) * 8
                            n = c - cs + 1
                            tk.op("dve", lambda e: e.tensor_copy(
                                out=yT[:, cs:cs + n, :],
                                in_=pt[:, 0:n * 128].rearrange("p (k c) -> p k c", k=n)),
                                reads=[ptb], writes=[yTb])
                for nh in range(2):
                    po, pob = pso.next()
                    for c in range(12):
                        tk.op("pe", lambda e: e.matmul(po[:], lhsT=yT[:, c, :], rhs=wo[:, c, nh * 512:(nh + 1) * 512],
                                                       start=(c == 0), stop=(c == 11)),
                              reads=[yTb, wob_], writes=[pob])
                    tk.op("dve", lambda e: e.tensor_tensor(out=xt[:, nh * 512:(nh + 1) * 512], in0=po[:],
                                                           in1=xt[:, nh * 512:(nh + 1) * 512], op=ALU.add),
                          reads=[pob, xtb], writes=[xtb])
                tk.dma("pool", x_dst[rows, :], xt[:], reads=[xtb])
        tk.barrier()

    def phase_diff(self, T, qcT, kcT, vc, gcT, ycT, cst, neglam, neglamb, sg, sgb):
        tk = self.tk
        NT = T // 128
        NG = T // 512
        SKIP = 150.0
        LA = 3
        sl = alibi(8)
        with contextlib.ExitStack() as es:
            kr = self.sb_ring(es, 2, [128, T], BF16, "kTh")
            vr = self.sb_ring(es, 2, [128, NT, 128], BF16, "vh")
            ULr = self.sb_ring(es, 2, [128, 512], F32, "UL")
            URr = self.sb_ring(es, 2, [128, 512], F32, "UR")
            T1r = self.sb_ring(es, 2, [128, 7, 128], BF16, "T1")
            crr = self.sb_ring(es, 2, [128, 7, 128], BF16, "cr")
            bLr = self.sb_ring(es, 2, [128, 64], F32, "bL")
            bRr = self.sb_ring(es, 2, [128, 64], F32, "bR")
            onesf = self.sb(es, [128, 128], F32, "onesf")
            onesb = self.sb(es, [128, 128], BF16, "onesb")
            cb = Buf()
            tk.op("pool", lambda e: e.memset(onesf[:], 1.0), writes=[cb])
            tk.op("pool", lambda e: e.memset(onesb[:], 1.0), writes=[cb])
            qr = self.sb_ring(es, 2, [128, 2, 512], BF16, "qTh")
            for t_, b_ in zip(qr.tiles, qr.bufs):
                tk.op("pool", lambda e: e.memset(t_[:], 0.0), writes=[b_])
            ones1 = self.sb(es, [128, 128], BF16, "ones1p")
            tk.op("pool", lambda e: e.memset(ones1[:], 0.0), writes=[cb])
            tk.op("pool", lambda e: e.memset(ones1[0:1, :], 1.0), writes=[cb])
            for t_, b_ in zip(crr.tiles, crr.bufs):
                tk.op("pool", lambda e: e.memset(t_[:], 0.0), writes=[b_])
            gr = self.sb_ring(es, 2, [128, 512], BF16, "gch")
            ptr = self.sb_ring(es, 6, [128, 512], BF16, "pT")
            osr = self.sb_ring(es, 2, [128, 4, 512], F32, "osum")
            tmpr = self.sb_ring(es, 3, [128, 512], F32, "tmp")
            yr = self.sb_ring(es, 2, [128, 512], BF16, "yc")
            scr = self.ps_ring(es, 4, [128, 512], F32, "sc")
            accn = [(self.ps(es, [128, 512], F32, "accn"), Buf()) for _ in range(2)]
            accd = [(self.ps(es, [128, 512], F32, "accd"), Buf()) for _ in range(2)]
            items = []
            for h in range(8):
                hctx = {"h": h}
                for G in range(NG):
                    gctx = {"G": G, "hctx": hctx}
                    zl = []
                    for zn, kts in (("L", range(0, 4 * G)), ("D", range(4 * G, 4 * G + 4)),
                                    ("R", range(4 * G + 4, NT))):
                        keep = []
                        for kt in kts:
                            if zn == "L":
                                md = 128 * (4 * G - kt) - 127
                            elif zn == "R":
                                md = 128 * kt - (512 * G + 511)
                            else:
                                md = 0
                            if sl[h] * md >= SKIP:
                                continue
                            keep.append(kt)
                        if keep:
                            zl.append((zn, keep))
                    for zi, (zn, keep) in enumerate(zl):
                        for idx, kt in enumerate(keep):
                            for p in range(2):
                                items.append({"g": gctx, "zn": zn, "kt": kt, "p": p, "first": idx == 0,
                                              "last": idx == len(keep) - 1, "zfirst": zi == 0,
                                              "zlast": zi == len(zl) - 1,
                                              "gstart": zi == 0 and idx == 0 and p == 0})

            def stage_a(it):
                g = it["g"]
                hc = g["hctx"]
                h = hc["h"]
                G = g["G"]
                hr = slice(h * 128, (h + 1) * 128)
                if "k" not in hc:
                    hc["k"] = kr.next()
                    hc["v"] = vr.next()
                    tk.dma("sp", hc["k"][0][:], kcT[hr, :], writes=[hc["k"][1]])
                    tk.dma("sp", hc["v"][0][:], vc[:, hr].rearrange("(n p) d -> p n d", p=128), writes=[hc["v"][1]])
                    for nm, ring, src in (("UL", ULr, cst["UL"][:, h, :]), ("UR", URr, cst["UR"][:, h, :]),
                                          ("T1", T1r, cst["T1"][:, h, :, :]), ("CR", crr, cst["CR"][:, h, :, :]),
                                          ("BL", bLr, cst["BL"][:, h, :]), ("BR", bRr, cst["BR"][:, h, :])):
                        hc[nm] = ring.next()
                        if nm == "CR":
                            tk.dma("sp", hc[nm][0][0:1, :, :], src, writes=[hc[nm][1]])
                        else:
                            tk.dma("sp", hc[nm][0][:], src, writes=[hc[nm][1]])
                if "q" not in g:
                    g["q"] = qr.next()
                    tk.dma("sp", g["q"][0][0:64, 0, :], qcT[h * 128:h * 128 + 64, G * 512:(G + 1) * 512],
                           writes=[g["q"][1]])
                    tk.dma("sp", g["q"][0][64:128, 1, :], qcT[h * 128 + 64:h * 128 + 128, G * 512:(G + 1) * 512],
                           writes=[g["q"][1]])
                    g["gt"] = gr.next()
                    tk.dma("sp", g["gt"][0][:], gcT[hr, G * 512:(G + 1) * 512], writes=[g["gt"][1]])
                    g["os"] = None
                kT, kTb = hc["k"]
                qt, qtb = g["q"]
                p = it["p"]
                kt = it["kt"]
                pr = slice(64 * p, 64 * p + 64)
                sc, scb = scr.next()
                it["sc"] = (sc, scb)
                if it["zn"] == "D":
                    ktl = kt - 4 * G
                    T1h, T1b = hc["T1"]
                    crh, crb = hc["CR"]
                    tk.op("pe", lambda e: e.matmul(sc[:], lhsT=self.ident[:], rhs=T1h[:, 3 - ktl:7 - ktl, :],
                                                   start=True, stop=False),
                          reads=[self.identb, T1b], writes=[scb])
                    tk.op("pe", lambda e: e.matmul(sc[:], lhsT=ones1[:, :], rhs=crh[:, 3 - ktl:7 - ktl, :],
                                                   start=False, stop=False),
                          reads=[cb, crb], writes=[scb])
                    tk.op("pe", lambda e: e.matmul(sc[:], lhsT=kT[:, kt * 128:(kt + 1) * 128], rhs=qt[:, p, :],
                                                   start=False, stop=True),
                          reads=[kTb, qtb], writes=[scb])
                else:
                    tk.op("pe", lambda e: e.matmul(sc[:], lhsT=kT[:, kt * 128:(kt + 1) * 128], rhs=qt[:, p, :],
                                                   start=True, stop=True),
                          reads=[kTb, qtb], writes=[scb])

            cnt = [0]

            def stage_b(it):
                g = it["g"]
                hc = g["hctx"]
                G = g["G"]
                p = it["p"]
                kt = it["kt"]
                sc, scb = it["sc"]
                zn = it["zn"]
                rd = [scb]
                if zn == "D":
                    bias = 0.0
                elif zn == "L":
                    dl = 4 * G - kt
                    bias = hc["BL"][0][:, dl:dl + 1]
                    rd.append(hc["BL"][1])
                else:
                    dr = kt - 4 * G - 4
                    bias = hc["BR"][0][:, dr:dr + 1]
                    rd.append(hc["BR"][1])
                pT, pTb = ptr.next()
                tk.op("act", lambda e: e.activation(out=pT[:], in_=sc[:], func=AF.Exp, scale=0.125, bias=bias),
                      reads=rd, writes=[pTb])
                vh, vhb = hc["v"]
                an, anb = accn[p]
                tk.op("pe", lambda e: e.matmul(an[:], lhsT=vh[:, kt, :], rhs=pT[:], start=it["first"],
                                               stop=it["last"]),
                      reads=[vhb, pTb], writes=[anb])
                ad_, adb = accd[p]
                tk.op("pe", lambda e: e.matmul(ad_[:], lhsT=onesb[:], rhs=pT[:], start=it["first"], stop=it["last"]),
                      reads=[cb, pTb], writes=[adb])
                if not it["last"]:
                    return
                if g["os"] is None:
                    g["os"] = osr.next()
                    g["osb"] = [Buf() for _ in range(4)]
                osum = g["os"][0]
                osb = g["osb"]
                for (ac, acb, i) in ((an, anb, 2 * p), (ad_, adb, 2 * p + 1)):
                    if zn == "D":
                        if it["zfirst"]:
                            tk.op("dve", lambda e: e.tensor_copy(out=osum[:, i, :], in_=ac[:]), reads=[acb],
                                  writes=[osb[i]])
                        else:
                            tk.op("dve", lambda e: e.tensor_tensor(out=osum[:, i, :], in0=ac[:], in1=osum[:, i, :],
                                                                   op=ALU.add),
                                  reads=[acb, osb[i]], writes=[osb[i]])
                    else:
                        U, Ub = hc["UL"] if zn == "L" else hc["UR"]
                        if it["zfirst"]:
                            tk.op("dve", lambda e: e.tensor_tensor(out=osum[:, i, :], in0=ac[:], in1=U[:], op=ALU.mult),
                                  reads=[acb, Ub], writes=[osb[i]])
                        else:
                            tm_, tmb = tmpr.next()
                            tk.op("dve", lambda e: e.tensor_tensor(out=tm_[:], in0=ac[:], in1=U[:], op=ALU.mult),
                                  reads=[acb, Ub], writes=[tmb])
                            tk.op("pool", lambda e: e.tensor_tensor(out=osum[:, i, :], in0=tm_[:], in1=osum[:, i, :],
                                                                    op=ALU.add),
                                  reads=[tmb, osb[i]], writes=[osb[i]])
                if not (it["zlast"] and p == 1):
                    return
                h = hc["h"]
                hr = slice(h * 128, (h + 1) * 128)
                for pp in range(2):
                    tk.op("dve", lambda e: e.reciprocal(out=osum[:, 2 * pp + 1, :], in_=osum[:, 2 * pp + 1, :]),
                          reads=[osb[2 * pp + 1]], writes=[osb[2 * pp + 1]])
                    tk.op("dve", lambda e: e.tensor_tensor(out=osum[:, 2 * pp, :], in0=osum[:, 2 * pp, :],
                                                           in1=osum[:, 2 * pp + 1, :], op=ALU.mult),
                          reads=[osb[2 * pp], osb[2 * pp + 1]], writes=[osb[2 * pp]])
                tk.op("dve", lambda e: e.scalar_tensor_tensor(out=osum[:, 0, :], in0=osum[:, 2, :], scalar=neglam,
                                                              in1=osum[:, 0, :], op0=ALU.mult, op1=ALU.add),
                      reads=[osb[0], osb[2], neglamb], writes=[osb[0]])
                tk.op("pool", lambda e: e.tensor_tensor(out=osum[:, 1, :], in0=osum[:, 0, :], in1=osum[:, 0, :],
                                                        op=ALU.mult),
                      reads=[osb[0]], writes=[osb[1]])
                sc2, sc2b = accd[1]
                tk.op("pe", lambda e: e.matmul(sc2[:], lhsT=onesf[:], rhs=osum[:, 1, :], start=True, stop=True),
                      reads=[cb, osb[1]], writes=[sc2b])
                tk.op("act", lambda e: e.activation(out=osum[:, 3, :], in_=sc2[:], func=AF.Ln, scale=1.0 / 128,
                                                    bias=self.epsc[:, 0:1]),
                      reads=[sc2b, self.epscb], writes=[osb[3]])
                tk.op("act", lambda e: e.activation(out=osum[:, 3, :], in_=osum[:, 3, :], func=AF.Exp, scale=-0.5),
                      reads=[osb[3]], writes=[osb[3]])
                tk.op("dve", lambda e: e.scalar_tensor_tensor(out=osum[:, 0, :], in0=osum[:, 0, :], scalar=sg,
                                                              in1=osum[:, 3, :], op0=ALU.mult, op1=ALU.mult),
                      reads=[osb[0], osb[3], sgb], writes=[osb[0]])
                yt, ytb = yr.next()
                gt, gtb = g["gt"]
                tk.op("pool", lambda e: e.tensor_tensor(out=yt[:], in0=osum[:, 0, :], in1=gt[:], op=ALU.mult),
                      reads=[osb[0], gtb], writes=[ytb])
                tk.dma("pool", ycT[hr, G * 512:(G + 1) * 512], yt[:], reads=[ytb])

            n = len(items)
            for i in range(n + LA):
                if i < n:
                    stage_a(items[i])
                if i - LA >= 0:
                    stage_b(items[i - LA])
        tk.barrier()

    def phase_gla(self, T, qd, kd, vd, adT, gd, of, ydT, waug, gcst, glag, glagb):
        tk = self.tk
        NT = T // 128
        with contextlib.ExitStack() as es:
            S = self.sb(es, [128, 4, 128], F32, "S")
            Sb = self.sb(es, [128, 4, 128], BF16, "Sb")
            Sbuf = [Buf() for _ in range(4)]
            Sbb = [Buf() for _ in range(4)]
            cmat = self.sb(es, [128, 2, 128], F32, "cmat")
            maskr = self.sb(es, [128, 512], BF16, "mask")
            ind = self.sb(es, [128, 2], F32, "ind")
            wa = self.sb(es, [17, 512], F32, "waug")
            cb = Buf()
            tk.dma("sp", ind[:], gcst["IND"], writes=[cb])
            adr = self.sb_ring(es, 3, [17, 128], F32, "ad")
            for t_, b_ in zip(adr.tiles, adr.bufs):
                tk.op("pool", lambda e: e.memset(t_[0:1, :], 1.0), writes=[b_])
            qr = self.sb_ring(es, 3, [128, 512], F32, "q")
            kr = self.sb_ring(es, 3, [128, 512], F32, "k")
            vr = self.sb_ring(es, 3, [128, 512], BF16, "v")
            ofr = self.sb_ring(es, 3, [128, 512], F32, "of")
            gdr = self.sb_ring(es, 3, [128, 512], BF16, "gd")
            spr = self.sb_ring(es, 2, [128, 512], F32, "sp")
            er = self.sb_ring(es, 4, [128, 512], F32, "E")
            qer = self.sb_ring(es, 2, [128, 512], BF16, "qe")
            ker = self.sb_ring(es, 2, [128, 512], BF16, "ke")
            ksr = self.sb_ring(es, 2, [128, 512], BF16, "ks")
            qeTr = self.sb_ring(es, 2, [128, 4, 128], BF16, "qeT")
            keTr = self.sb_ring(es, 2, [128, 4, 128], BF16, "keT")
            amr = self.sb_ring(es, 2, [128, 512], BF16, "am")
            decr = self.sb_ring(es, 2, [128, 4, 2], F32, "dec")
            osr = self.sb_ring(es, 2, [128, 512], F32, "os")
            smr = self.sb_ring(es, 2, [128, 8], F32, "sm")
            ybr = self.sb_ring(es, 2, [128, 512], BF16, "yb")
            yTr = self.sb_ring(es, 2, [128, 4, 128], BF16, "yT")
            psr = self.ps_ring(es, 5, [128, 512], F32, "ps")
            ptr = self.ps_ring(es, 2, [128, 1024], BF16, "pt")
            kvr = self.ps_ring(es, 1, [128, 512], F32, "kv")
            assert True
            for dr in range(2):
                tk.dma("sp", cmat[:], gcst["CM"][dr], writes=[cb])
                tk.dma("sp", maskr[:], gcst["MASK"][dr], writes=[cb])
                tk.dma("sp", wa[:], waug[dr], writes=[cb])
                for h in range(4):
                    tk.op("pool", lambda e: e.memset(S[:, h, :], 0.0), writes=[Sbuf[h]])
                    tk.op("pool", lambda e: e.memset(Sb[:, h, :], 0.0), writes=[Sbb[h]])
                order = range(NT) if dr == 0 else range(NT - 1, -1, -1)
                def gload(t):
                    rows = slice(t * 128, (t + 1) * 128)
                    c = {}
                    c["ad"] = adr.next()
                    tk.dma("sp", c["ad"][0][1:17, :], adT[16 * dr:16 * dr + 16, rows], writes=[c["ad"][1]])
                    for nm, ring, src in (("q", qr, qd), ("k", kr, kd), ("v", vr, vd)):
                        c[nm] = ring.next()
                        tk.dma("sp", c[nm][0][:], src[rows, :], writes=[c[nm][1]])
                    if dr == 1:
                        c["of"] = ofr.next()
                        tk.dma("sp", c["of"][0][:], of[rows, :], writes=[c["of"][1]])
                        c["gd"] = gdr.next()
                        tk.dma("sp", c["gd"][0][:], gd[rows, :], writes=[c["gd"][1]])
                    return c

                order = list(order)
                nxt = gload(order[0])
                for oi, t in enumerate(order):
                    rows = slice(t * 128, (t + 1) * 128)
                    cur = nxt
                    if oi + 1 < len(order):
                        nxt = gload(order[oi + 1])
                    ad_, adb = cur["ad"]
                    q_, qb_ = cur["q"]
                    k_, kb_ = cur["k"]
                    v_, vb_ = cur["v"]
                    if dr == 1:
                        of_, ofb = cur["of"]
                        gd_, gdb = cur["gd"]
                    lvl = int(os.environ.get("GLALVL", "99"))
                    if lvl < 2:
                        continue
                    px, pxb = psr.next()
                    tk.op("pe", lambda e: e.matmul(px[:], lhsT=ad_[:, :], rhs=wa[:, :], start=True, stop=True),
                          reads=[adb, cb], writes=[pxb])
                    sp_, spb = spr.next()
                    tk.op("act", lambda e: e.activation(out=sp_[:], in_=px[:], func=AF.Exp, scale=-1.0),
                          reads=[pxb], writes=[spb])
                    tk.op("act", lambda e: e.activation(out=sp_[:], in_=sp_[:], func=AF.Ln, bias=self.onec[:, 0:1]),
                          reads=[spb, self.epscb], writes=[spb])
                    if lvl < 3:
                        continue
                    pcs, pcsb = psr.next()
                    tk.op("pe", lambda e: e.matmul(pcs[:], lhsT=cmat[:, 0, :], rhs=sp_[:], start=True, stop=True),
                          reads=[cb, spb], writes=[pcsb])
                    paf, pafb = psr.next()
                    tk.op("pe", lambda e: e.matmul(paf[:], lhsT=cmat[:, 1, :], rhs=sp_[:], start=True, stop=True),
                          reads=[cb, spb], writes=[pafb])
                    ptot, ptotb = psr.next()
                    for h in range(4):
                        tk.op("pe", lambda e: e.matmul(ptot[:, 2 * h:2 * h + 2], lhsT=sp_[:, h * 128:(h + 1) * 128],
                                                       rhs=ind[:, :], start=True, stop=True),
                              reads=[spb, cb], writes=[ptotb])
                    e1, e1b = er.next()
                    e2, e2b = er.next()
                    e3, e3b = er.next()
                    tk.op("act", lambda e: e.activation(out=e1[:], in_=pcs[:], func=AF.Exp, scale=-1.0 / 16),
                          reads=[pcsb], writes=[e1b])
                    tk.op("act", lambda e: e.activation(out=e2[:], in_=pcs[:], func=AF.Exp, scale=1.0 / 16),
                          reads=[pcsb], writes=[e2b])
                    tk.op("act", lambda e: e.activation(out=e3[:], in_=paf[:], func=AF.Exp, scale=-1.0 / 16),
                          reads=[pafb], writes=[e3b])
                    dec, decb = decr.next()
                    tk.op("act", lambda e: e.activation(out=dec[:].rearrange("p h c -> p (h c)"), in_=ptot[:, 0:8],
                                                        func=AF.Exp, scale=-1.0 / 16),
                          reads=[ptotb], writes=[decb])
                    if lvl < 4:
                        continue
                    qe, qeb = qer.next()
                    ke, keb = ker.next()
                    ks, ksb = ksr.next()
                    tk.op("dve", lambda e: e.scalar_tensor_tensor(out=qe[:], in0=q_[:], scalar=128 ** -0.5, in1=e1[:],
                                                                  op0=ALU.mult, op1=ALU.mult),
                          reads=[qb_, e1b], writes=[qeb])
                    tk.op("pool", lambda e: e.tensor_tensor(out=ke[:], in0=k_[:], in1=e2[:], op=ALU.mult),
                          reads=[kb_, e2b], writes=[keb])
                    tk.op("dve", lambda e: e.tensor_tensor(out=ks[:], in0=k_[:], in1=e3[:], op=ALU.mult),
                          reads=[kb_, e3b], writes=[ksb])
                    pt, ptb = ptr.next()
                    ptk, ptkb = ptr.next()
                    for h in range(4):
                        tk.op("pe", lambda e: e.transpose(out=pt[:, h * 128:(h + 1) * 128],
                                                          in_=qe[:, h * 128:(h + 1) * 128], identity=self.ident[:]),
                              reads=[qeb, self.identb], writes=[ptb])
                    for h in range(4):
                        tk.op("pe", lambda e: e.transpose(out=ptk[:, h * 128:(h + 1) * 128],
                                                          in_=ke[:, h * 128:(h + 1) * 128], identity=self.ident[:]),
                              reads=[keb, self.identb], writes=[ptkb])
                    qeT, qeTb = qeTr.next()
                    keT, keTb = keTr.next()
                    tk.op("dve", lambda e: e.tensor_copy(out=qeT[:].rearrange("p h c -> p (h c)"), in_=pt[:, 0:512]),
                          reads=[ptb], writes=[qeTb])
                    tk.op("act", lambda e: e.activation(out=keT[:].rearrange("p h c -> p (h c)"), in_=ptk[:, 0:512],
                                                        func=AF.Copy),
                          reads=[ptkb], writes=[keTb])
                    if lvl < 5:
                        continue
                    pa, pab = psr.next()
                    for h in range(4):
                        tk.op("pe", lambda e: e.matmul(pa[:, h * 128:(h + 1) * 128], lhsT=keT[:, h, :],
                                                       rhs=qeT[:, h, :], start=True, stop=True),
                              reads=[keTb, qeTb], writes=[pab])
                    am, amb = amr.next()
                    tk.op("dve", lambda e: e.tensor_tensor(out=am[:], in0=pa[:], in1=maskr[:], op=ALU.mult),
                          reads=[pab, cb], writes=[amb])
                    if lvl < 6:
                        continue
                    po, pob = psr.next()
                    for h in range(4):
                        tk.op("pe", lambda e: e.matmul(po[:, h * 128:(h + 1) * 128], lhsT=am[:, h * 128:(h + 1) * 128],
                                                       rhs=v_[:, h * 128:(h + 1) * 128], start=(h == 0), stop=False,
                                                       skip_group_check=True),
                              reads=[amb, vb_], writes=[pob])
                    corder = (0, 1) if dr == 0 else (1, 0)
                    for ci in corder:
                        cr_ = slice(64 * ci, 64 * ci + 64)
                        for h in range(4):
                            hs = slice(h * 128, (h + 1) * 128)
                            tk.op("pe", lambda e: e.matmul(po[cr_, hs], lhsT=qeT[:, h, cr_], rhs=Sb[:, h, :],
                                                           start=False, stop=True, skip_group_check=True),
                                  reads=[qeTb, Sbb[h]], writes=[pob])
                        if (dr == 0 and t == NT - 1 and ci == 1) or (dr == 1 and t == 0 and ci == 0):
                            continue
                        pkv, pkvb = kvr.next()
                        for h in range(4):
                            hs = slice(h * 128, (h + 1) * 128)
                            tk.op("pe", lambda e: e.matmul(pkv[:, hs], lhsT=ks[cr_, hs], rhs=v_[cr_, hs], start=True,
                                                           stop=True),
                                  reads=[ksb, vb_], writes=[pkvb])
                        for h in range(4):
                            hs = slice(h * 128, (h + 1) * 128)
                            tk.op("dve", lambda e: e.scalar_tensor_tensor(out=S[:, h, :], in0=S[:, h, :],
                                                                          scalar=dec[:, h, ci:ci + 1], in1=pkv[:, hs],
                                                                          op0=ALU.mult, op1=ALU.add),
                                  reads=[Sbuf[h], decb, pkvb], writes=[Sbuf[h]])
                            tk.op("act", lambda e: e.activation(out=Sb[:, h, :], in_=S[:, h, :], func=AF.Copy),
                                  reads=[Sbuf[h]], writes=[Sbb[h]])
                    if lvl < 7:
                        continue
                    os_, osb_ = osr.next()
                    if dr == 0:
                        tk.op("dve", lambda e: e.tensor_copy(out=os_[:], in_=po[:]), reads=[pob], writes=[osb_])
                        tk.dma("pool", of[rows, :], os_[:], reads=[osb_])
                        continue
                    tk.op("dve", lambda e: e.tensor_tensor(out=os_[:], in0=po[:], in1=of_[:], op=ALU.add),
                          reads=[pob, ofb], writes=[osb_])
                    e4, e4b = er.next()
                    tk.op("pool", lambda e: e.tensor_tensor(out=e4[:], in0=os_[:], in1=os_[:], op=ALU.mult),
                          reads=[osb_], writes=[e4b])
                    sm, smb = smr.next()
                    tk.op("dve", lambda e: e.tensor_reduce(out=sm[:, 0:4], in_=e4[:].rearrange("p (h d) -> p h d", h=4),
                                                           axis=AX.X, op=ALU.add),
                          reads=[e4b], writes=[smb])
                    tk.op("act", lambda e: e.activation(out=sm[:, 4:8], in_=sm[:, 0:4], func=AF.Ln, scale=1.0 / 128,
                                                        bias=self.epsc[:, 0:1]),
                          reads=[smb, self.epscb], writes=[smb])
                    tk.op("act", lambda e: e.activation(out=sm[:, 4:8], in_=sm[:, 4:8], func=AF.Exp, scale=-0.5),
                          reads=[smb], writes=[smb])
                    for h in range(4):
                        hs = slice(h * 128, (h + 1) * 128)
                        tk.op("dve", lambda e: e.scalar_tensor_tensor(out=os_[:, hs], in0=os_[:, hs],
                                                                      scalar=sm[:, 4 + h:5 + h], in1=glag[:, hs],
                                                                      op0=ALU.mult, op1=ALU.mult),
                              reads=[osb_, smb, glagb], writes=[osb_])
                    yb_, ybb = ybr.next()
                    tk.op("pool", lambda e: e.tensor_tensor(out=yb_[:], in0=os_[:], in1=gd_[:], op=ALU.mult),
                          reads=[osb_, gdb], writes=[ybb])
                    pt2, pt2b = ptr.next()
                    for h in range(4):
                        tk.op("pe", lambda e: e.transpose(out=pt2[:, h * 128:(h + 1) * 128],
                                                          in_=yb_[:, h * 128:(h + 1) * 128], identity=self.ident[:]),
                              reads=[ybb, self.identb], writes=[pt2b])
                    yT, yTb = yTr.next()
                    tk.op("act", lambda e: e.activation(out=yT[:].rearrange("p h c -> p (h c)"), in_=pt2[:, 0:512],
                                                        func=AF.Copy),
                          reads=[pt2b], writes=[yTb])
                    tk.dma("pool", ydT[:, rows].rearrange("(h p) t -> p h t", p=128), yT[:], reads=[yTb])
                tk.barrier()

    def phase_final(self, T, x_src, y_dst, fing, fingb):
        tk = self.tk
        with contextlib.ExitStack() as es:
            xr = self.sb_ring(es, 3, [128, 1024], F32, "xf")
            jr = self.sb_ring(es, 2, [128, 1024], BF16, "jf")
            sr = self.sb_ring(es, 3, [128, 2], F32, "sf")
            for t in range(T // 128):
                rows = slice(t * 128, (t + 1) * 128)
                xt, xtb = xr.next()
                tk.dma("sp", xt[:], x_src[rows, :], writes=[xtb])
                jt, jtb = jr.next()
                sm, smb = sr.next()
                tk.op("act", lambda e: e.activation(out=jt[:], in_=xt[:], func=AF.Square, accum_out=sm[:, 0:1]),
                      reads=[xtb], writes=[jtb, smb])
                tk.op("act", lambda e: e.activation(out=sm[:, 1:2], in_=sm[:, 0:1], func=AF.Ln, scale=1.0 / D,
                                                    bias=self.epsc[:, 0:1]),
                      reads=[smb, self.epscb], writes=[smb])
                tk.op("act", lambda e: e.activation(out=sm[:, 1:2], in_=sm[:, 1:2], func=AF.Exp, scale=-0.5),
                      reads=[smb], writes=[smb])
                tk.op("dve", lambda e: e.scalar_tensor_tensor(out=xt[:], in0=xt[:], scalar=sm[:, 1:2], in1=fing[:],
                                                              op0=ALU.mult, op1=ALU.mult),
                      reads=[xtb, smb, fingb], writes=[xtb])
                tk.dma("pool", y_dst[rows, :], xt[:], reads=[xtb])
        tk.barrier()

    def scratch(self, T):
        d = self.dram
        n = "_%d" % T
        S = {}
        S["x"] = d("x_scr" + n, [T, 1024], F32)
        S["qaT"] = d("qaT" + n, [1024, T], BF16)
        S["kaT"] = d("kaT" + n, [256, T], BF16)
        S["va"] = d("va" + n, [T, 256], BF16)
        S["ga"] = d("ga" + n, [T, 1024], BF16)
        S["qbT"] = [d("qbT%d" % p + n, [512, T], BF16) for p in range(3)]
        S["kbT"] = [d("kbT%d" % p + n, [512, T], BF16) for p in range(3)]
        S["vb"] = [d("vb%d" % p + n, [T, 512], BF16) for p in range(3)]
        S["gb"] = d("gb" + n, [T, 512], BF16)
        S["ya"] = d("ya" + n, [T, 1024], BF16)
        S["nb"] = [d("nb%d" % p + n, [T, 520], F32) for p in range(3)]
        S["qcT"] = d("qcT" + n, [1024, T], BF16)
        S["kcT"] = d("kcT" + n, [1024, T], BF16)
        S["vc"] = d("vc" + n, [T, 1024], BF16)
        S["gcT"] = d("gcT" + n, [1024, T], BF16)
        S["qd"] = d("qd" + n, [T, 512], F32)
        S["kd"] = d("kd" + n, [T, 512], F32)
        S["vd"] = d("vd" + n, [T, 512], BF16)
        S["gd"] = d("gd" + n, [T, 512], BF16)
        S["adT"] = d("adT" + n, [32, T], F32)
        S["ycT"] = d("ycT" + n, [1024, T], BF16)
        S["of"] = d("of" + n, [T, 512], F32)
        S["ydT"] = d("ydT" + n, [512, T], BF16)
        return S

    def build(self, layers=(0, 1, 2, 3), final=True):
        tk = self.tk
        I, O = "ExternalInput", "ExternalOutput"
        d = self.dram
        xs = [d("x%d" % i, [T, 1024], F32, I) for i, T in enumerate(self.seq_lens)]
        ys = [d("y%d" % i, [T, 1024], F32, O) for i, T in enumerate(self.seq_lens)]
        ewi = d("even_w_in", [2, 1024, EVEN_COLS], F32, I)
        ewo = d("even_w_out", [2, 1536, 1024], F32, I)
        owi = d("odd_w_in", [2, 1024, ODD_COLS], F32, I)
        owo = d("odd_w_out", [2, 1536, 1024], F32, I)
        g_t = d("g_t", [128, 32], F32, I)
        fin_g = d("fin_g", [128, 1024], F32, I)
        sink_d = d("sink", [128, 32], F32, I)
        dlam_d = d("dlam", [128, 512], F32, I)
        subg_d = d("subg", [128, 2], F32, I)
        waug_d = d("waug", [2, 2, 17, 512], F32, I)
        glag_d = d("glag", [128, 2, 512], F32, I)
        ident_d = d("ident", [128, 128], BF16, I)
        btA_d = d("btA", [128, 3, 16, 128], BF16, I)
        btB_d = [d("btB%d" % p, [128, 3, 8, 128], BF16, I) for p in range(3)]
        cst = {"UL": d("cUL", [128, 8, 512], F32, I), "UR": d("cUR", [128, 8, 512], F32, I),
               "BL": d("cBL", [128, 8, 64], F32, I), "BR": d("cBR", [128, 8, 64], F32, I),
               "T1": d("cT1", [128, 8, 7, 128], BF16, I), "CR": d("cCR", [1, 8, 7, 128], BF16, I)}
        gcst = {"CM": d("gCM", [2, 128, 2, 128], F32, I), "MASK": d("gMASK", [2, 128, 512], BF16, I),
                "IND": d("gIND", [128, 2], F32, I)}
        wbi = [d("wbi%d" % l, [1024, EVEN_COLS if l % 2 == 0 else ODD_COLS], BF16) for l in range(4)]
        wbo = [d("wbo%d" % l, [1536, 1024], BF16) for l in range(4)]
        scr = {T: self.scratch(T) for T in sorted(set(self.seq_lens))}
        dil = [1, 4, 16]
        with contextlib.ExitStack() as es:
            self.ident, self.identb = self.load_const(es, ident_d, [128, 128], BF16, "ident")
            gt, gtb = self.load_const(es, g_t, [128, 32], F32, "gt")
            fing, fingb = self.load_const(es, fin_g, [128, 1024], F32, "fing")
            btA, btAb = self.load_const(es, btA_d, [128, 3, 16, 128], BF16, "btA")
            btB = [self.load_const(es, btB_d[p], [128, 3, 8, 128], BF16, "btB") for p in range(3)]
            esink, esinkb = self.load_const(es, sink_d, [128, 32], F32, "esink")
            tk.op("act", lambda e: e.activation(out=esink[:], in_=esink[:], func=AF.Exp), reads=[esinkb],
                  writes=[esinkb])
            glag, glagb = self.load_const(es, glag_d, [128, 2, 512], F32, "glag")
            cst2 = self.sb(es, [128, 2], F32, "cst2")
            self.epscb = Buf()
            tk.op("pool", lambda e: e.memset(cst2[:, 0:1], EPS), writes=[self.epscb])
            tk.op("pool", lambda e: e.memset(cst2[:, 1:2], 1.0), writes=[self.epscb])
            self.epsc = cst2[:, 0:1]
            self.onec = cst2[:, 1:2]
            dl, dlb = self.load_const(es, dlam_d, [128, 512], F32, "dl")
            sg, sgb = self.load_const(es, subg_d, [128, 2], F32, "sg")
            lam = self.sb(es, [128, 8], F32, "lam")
            lamb = Buf()
            prod = self.sb(es, [128, 4, 64], F32, "prod")
            dl4 = dl[:].rearrange("p (j a d) -> p j a d", j=2, a=4)
            for j in range(2):
                for a in range(2):
                    tk.op("dve", lambda e: e.tensor_tensor(out=prod[:, 2 * j + a, :], in0=dl4[:, j, 2 * a, :],
                                                           in1=dl4[:, j, 2 * a + 1, :], op=ALU.mult),
                          reads=[dlb], writes=[lamb])
            tk.op("dve", lambda e: e.tensor_reduce(out=lam[:, 0:4], in_=prod[:], axis=AX.X, op=ALU.add),
                  reads=[lamb], writes=[lamb])
            tk.op("act", lambda e: e.activation(out=lam[:, 0:4], in_=lam[:, 0:4], func=AF.Exp), reads=[lamb],
                  writes=[lamb])
            for j in range(2):
                lam_init = 0.8 - 0.6 * math.exp(-0.3 * (2 * j + 1))
                tk.op("dve", lambda e: e.scalar_tensor_tensor(out=lam[:, 4 + j:5 + j], in0=lam[:, 2 * j + 1:2 * j + 2],
                                                              scalar=-lam_init, in1=lam[:, 2 * j:2 * j + 1],
                                                              op0=ALU.add, op1=ALU.subtract),
                      reads=[lamb], writes=[lamb])
                tk.op("dve", lambda e: e.tensor_scalar(out=sg[:, j:j + 1], in0=sg[:, j:j + 1], scalar1=1.0 - lam_init,
                                                       scalar2=None, op0=ALU.mult), reads=[sgb], writes=[sgb])
            srcs, dsts, gcs = [], [], []
            for l in layers:
                j = l // 2
                srcs += [ewi[j] if l % 2 == 0 else owi[j], ewo[j] if l % 2 == 0 else owo[j]]
                dsts += [wbi[l], wbo[l]]
                gcs += [(lambda k, l=l: (gt[:, 8 * l + k:8 * l + k + 1], gtb)), None]
            self.prep_weights(srcs, dsts, gcs)
            for si, T in enumerate(self.seq_lens):
                S = scr[T]
                first = True
                for l in layers:
                    j = l // 2
                    x_src = xs[si] if first else S["x"]
                    first = False
                    if l % 2 == 0:
                        fm = [(0, 1024, 1, S["qaT"], 0, "copy", BF16), (1024, 256, 1, S["kaT"], 0, "copy", BF16)]
                        tm = [(1280, 256, 1, S["va"], 0, "copy", BF16), (1536, 1024, 1, S["ga"], 0, "silu", BF16)]
                        for p in range(3):
                            fm.append((2560 + 512 * p, 512, dil[p], S["qbT"][p], 0, "copy", BF16))
                            fm.append((4096 + 512 * p, 512, dil[p], S["kbT"][p], 0, "copy", BF16))
                            tm.append((5632 + 512 * p, 512, dil[p], S["vb"][p], 0, "copy", BF16))
                        tm.append((7168, 512, 1, S["gb"], 0, "silu", BF16))
                        self.phase_proj(T, x_src, wbi[l], fm, tm)
                        self.phase_window(T, 1, 16, 4, S["qaT"], S["kaT"], S["va"], btA, btAb, "A", S["ya"],
                                          gate=S["ga"], esink=esink[:, 16 * j:16 * j + 16], esinkb=esinkb)
                        for p in range(3):
                            self.phase_window(T // dil[p], dil[p], 8, 1, S["qbT"][p], S["kbT"][p], S["vb"][p],
                                              btB[p][0], btB[p][1], "B", S["nb"][p], dil=dil[p])
                        self.phase_out(T, x_src, S["x"], wbo[l], [(S["ya"], 0, 1024)], [], nbs=S["nb"], gb=S["gb"])
                    else:
                        fm = [(0, 1024, 1, S["qcT"], 0, "copy", BF16), (1024, 1024, 1, S["kcT"], 0, "copy", BF16),
                              (3072, 1024, 1, S["gcT"], 0, "silu", BF16), (6144, 32, 1, S["adT"], 0, "copy", F32)]
                        tm = [(2048, 1024, 1, S["vc"], 0, "copy", BF16), (4096, 512, 1, S["qd"], 0, "copy", F32),
                              (4608, 512, 1, S["kd"], 0, "copy", F32), (5120, 512, 1, S["vd"], 0, "copy", BF16),
                              (5632, 512, 1, S["gd"], 0, "silu", BF16)]
                        self.phase_proj(T, x_src, wbi[l], fm, tm)
                        if "diff" not in getattr(self, "skip", ()):
                          self.phase_diff(T, S["qcT"], S["kcT"], S["vc"], S["gcT"], S["ycT"], cst,
                                        lam[:, 4 + j:5 + j], lamb, sg[:, j:j + 1], sgb)
                        if "gla" not in getattr(self, "skip", ()):
                          self.phase_gla(T, S["qd"], S["kd"], S["vd"], S["adT"], S["gd"], S["of"], S["ydT"],
                                       [waug_d[j, 0], waug_d[j, 1]], gcst, glag[:, j, :], glagb)
                        self.phase_out(T, x_src, S["x"], wbo[l], [], [(S["ycT"], 0, 8), (S["ydT"], 8, 4)])
                if final:
                    self.phase_final(T, S["x"], ys[si], fing, fingb)
                else:
                    self.phase_copy(T, S["x"], ys[si])
        return self.nc

    def phase_copy(self, T, src, dst):
        tk = self.tk
        with contextlib.ExitStack() as es:
            xr = self.sb_ring(es, 3, [128, 1024], F32, "xc")
            for t in range(T // 128):
                xt, xtb = xr.next()
                tk.dma("sp", xt[:], src[t * 128:(t + 1) * 128, :], writes=[xtb])
                tk.dma("pool", dst[t * 128:(t + 1) * 128, :], xt[:], reads=[xtb])
        tk.barrier()


def alibi(n):
    return np.array([2.0 ** (-8.0 * (i + 1) / n) for i in range(n)], dtype=np.float64)


def window_bias_table(nheads, radius, stride):
    b = np.arange(128)[:, None]
    a = np.arange(128)[None, :]
    sl = alibi(nheads)
    out = np.zeros((128, 3, nheads, 128), np.float32)
    for kind in range(3):
        dist = np.abs(a - (b + (kind - 1) * 128))
        for h in range(nheads):
            val = -(sl[h] * stride * 8.0) * dist
            out[:, kind, h, :] = np.where(dist <= radius, val, NEGB)
    return out.astype(ml_dtypes.bfloat16)


def diff_tables():
    sl = alibi(8)
    q = np.arange(512)
    b = np.arange(128)
    UL = np.zeros((128, 8, 512), np.float32)
    UR = np.zeros((128, 8, 512), np.float32)
    BL = np.zeros((128, 8, 64), np.float32)
    BR = np.zeros((128, 8, 64), np.float32)
    T1 = np.zeros((128, 8, 7, 128), np.float32)
    CR = np.zeros((1, 8, 7, 128), np.float32)
    a = np.arange(128)[None, :]
    bb = np.arange(128)[:, None]
    for h in range(8):
        UL[:, h, :] = np.exp(-sl[h] * q)[None, :]
        UR[:, h, :] = np.exp(-sl[h] * (511 - q))[None, :]
        for d in range(64):
            BL[:, h, d] = -sl[h] * (128.0 * d - b)
            BR[:, h, d] = -sl[h] * (128.0 * d + b + 1)
        c = sl[h] * 8.0
        for slot in range(7):
            dlt = slot - 3
            if dlt > 0:
                T1[:, h, slot, :] = -c * (a - bb)
            elif dlt < 0:
                T1[:, h, slot, :] = c * (a - bb)
            else:
                T1[:, h, slot, :] = -c * np.abs(a - bb)
            CR[0, h, slot, :] = -c * 128.0 * abs(dlt)
    return {"UL": UL, "UR": UR, "BL": BL, "BR": BR, "T1": T1.astype(ml_dtypes.bfloat16),
            "CR": CR.astype(ml_dtypes.bfloat16)}


def gla_tables():
    c = np.arange(128)
    same = (c[:, None] // 64) == (c[None, :] // 64)
    CM = np.zeros((2, 128, 2, 128), np.float32)
    MASK = np.zeros((2, 128, 4, 128), np.float32)
    CM[0, :, 0, :] = same & (c[:, None] <= c[None, :])
    CM[0, :, 1, :] = same & (c[:, None] > c[None, :])
    CM[1, :, 0, :] = same & (c[:, None] >= c[None, :])
    CM[1, :, 1, :] = same & (c[:, None] < c[None, :])
    for h in range(4):
        MASK[0, :, h, :] = same & (c[:, None] <= c[None, :])
        MASK[1, :, h, :] = same & (c[:, None] > c[None, :])
    IND = np.zeros((128, 2), np.float32)
    IND[:64, 0] = 1.0
    IND[64:, 1] = 1.0
    return {"CM": CM, "MASK": MASK.reshape(2, 128, 512).astype(ml_dtypes.bfloat16), "IND": IND}


def host_consts(inp):
    f = np.float32
    c = {}
    ng = np.asarray(inp["norm_g"], f)
    c["g_t"] = np.ascontiguousarray(ng.reshape(4, 8, 128).transpose(2, 0, 1).reshape(128, 32))
    c["fin_g"] = np.ascontiguousarray(np.broadcast_to(np.asarray(inp["final_norm_g"], f)[None, :], (128, 1024)))
    c["sink"] = np.ascontiguousarray(np.broadcast_to(np.asarray(inp["sink_logit"], f).reshape(1, 32), (128, 32)))
    c["dlam"] = np.ascontiguousarray(np.broadcast_to(np.asarray(inp["diff_lambda"], f).reshape(1, 512), (128, 512)))
    c["subg"] = np.ascontiguousarray(np.asarray(inp["diff_subln_g"], f).T)
    wg = np.asarray(inp["gla_w_gate2"], f)
    bg = np.asarray(inp["gla_b_gate"], f)
    c["waug"] = np.ascontiguousarray(np.concatenate([bg[:, :, None, :], wg], axis=2))
    gg = np.asarray(inp["gla_norm_g"], f)
    c["glag"] = np.ascontiguousarray(np.broadcast_to(np.tile(gg, (1, 4))[None, :, :], (128, 2, 512)))
    c["ident"] = np.eye(128).astype(ml_dtypes.bfloat16)
    c["btA"] = window_bias_table(16, 128, 1)
    for p, dl in enumerate([1, 4, 16]):
        c["btB%d" % p] = window_bias_table(8, 64, dl)
    for k, v in diff_tables().items():
        c["c" + k] = v
    for k, v in gla_tables().items():
        c["g" + k] = v
    for k in ("even_w_in", "even_w_out", "odd_w_in", "odd_w_out"):
        c[k] = np.ascontiguousarray(np.asarray(inp[k], f))
    return c


_PROG = None


def kernel(**inputs):
    global _PROG
    xp = np.asarray(inputs["x_prompt"], np.float32)
    xsm = np.asarray(inputs["x_sample"], np.float32)
    if _PROG is None:
        _PROG = Prog([2048, 2048, 2048, 2048, 8192]).build()
    nc = _PROG
    c = host_consts(inputs)
    in_maps = []
    for core in range(8):
        m = dict(c)
        for i in range(4):
            m["x%d" % i] = np.ascontiguousarray(xp[4 * core + i])
        m["x4"] = np.ascontiguousarray(xsm[core // 4])
        in_maps.append(m)
    res = run_bass_kernel_spmd(nc, in_maps, core_ids=list(range(8)))
    yp = np.stack([res.results[core]["y%d" % i] for core in range(8) for i in range(4)], axis=0)
    ysm = np.stack([res.results[0]["y4"], res.results[4]["y4"]], axis=0)
    return (np.asarray(yp, np.float32), np.asarray(ysm, np.float32))
```

```python
import contextlib
import math
import os
import numpy as np
import ml_dtypes
import concourse.bass as bass
import concourse.mybir as mybir
from concourse.bass_utils import run_bass_kernel_spmd

F32 = mybir.dt.float32
BF16 = mybir.dt.bfloat16
AF = mybir.ActivationFunctionType
ALU = mybir.AluOpType
AX = mybir.AxisListType

D = 1024
EPS = 1e-6
EVEN_COLS = 7680
ODD_COLS = 6176
NEGB = -30000.0
TMAX = 8192


class Buf:
    __slots__ = ("name", "w", "r")

    def __init__(self, name=""):
        self.name = name
        self.w = None
        self.r = {}


class TK:
    NDMA = 20

    def __init__(self, nc):
        self.nc = nc
        self.engs = {"pe": nc.tensor, "act": nc.scalar, "dve": nc.vector, "pool": nc.gpsimd, "sp": nc.sync}
        self.sems = {}
        self.cnt = {}
        for e in self.engs:
            self.sems[e] = nc.alloc_semaphore("s_" + e)
            self.cnt[e] = 0
        self.waited = {e: {} for e in self.engs}
        self.dq = {}
        for q in ("sp", "pool"):
            lst = []
            for i in range(self.NDMA):
                k = "d_%s_%d" % (q, i)
                self.sems[k] = nc.alloc_semaphore(k)
                self.cnt[k] = 0
                lst.append(k)
            self.dq[q] = [lst, 0]
        self.bar = nc.alloc_semaphore("s_bar")
        self.barcnt = 0
        self.n_ins = 0

    def _wait(self, e, key, val):
        if key == "pe" and e == "pe":
            return
        if self.waited[e].get(key, 0) >= val:
            return
        self.engs[e].wait_ge(self.sems[key], val)
        self.waited[e][key] = val
        self.n_ins += 1

    def _deps(self, e, reads, writes):
        for b in reads:
            if b.w is not None:
                self._wait(e, b.w[0], b.w[1])
        for b in writes:
            if b.w is not None:
                self._wait(e, b.w[0], b.w[1])
            for k, v in b.r.items():
                self._wait(e, k, v)

    def op(self, e, fn, reads=(), writes=()):
        self._deps(e, reads, writes)
        ins = fn(self.engs[e])
        self.cnt[e] += 1
        ins.then_inc(self.sems[e], 1)
        self.n_ins += 1
        c = self.cnt[e]
        for b in reads:
            b.r[e] = c
        for b in writes:
            b.w = (e, c)
            b.r = {}
        return ins

    def dma(self, q, out, in_, reads=(), writes=()):
        lst, idx = self.dq[q]
        key = lst[idx % len(lst)]
        self.dq[q][1] = idx + 1
        if self.cnt[key] > 0:
            self._wait(q, key, self.cnt[key])
        self._deps(q, reads, writes)
        ins = self.engs[q].dma_start(out=out, in_=in_)
        self.cnt[key] += 16
        ins.then_inc(self.sems[key], 16)
        self.n_ins += 1
        c = self.cnt[key]
        for b in reads:
            b.r[key] = c
        for b in writes:
            b.w = (key, c)
            b.r = {}
        return ins

    def barrier(self):
        for q in ("sp", "pool"):
            for key in self.dq[q][0]:
                if self.cnt[key] > 0:
                    self._wait(q, key, self.cnt[key])
        self.barcnt += 1
        n = len(self.engs)
        for e, eng in self.engs.items():
            if self.cnt[e] > 0:
                eng.wait_ge(self.sems[e], self.cnt[e]).then_inc(self.bar, 1)
            else:
                eng.wait_ge(self.bar, 0).then_inc(self.bar, 1)
        for e, eng in self.engs.items():
            eng.wait_ge(self.bar, n * self.barcnt)
            for k in self.sems:
                self.waited[e][k] = self.cnt[k]
        self.n_ins += 2 * n


class Ring:
    def __init__(self, tiles):
        self.tiles = tiles
        self.bufs = [Buf() for _ in tiles]
        self.i = 0

    def next(self):
        k = self.i % len(self.tiles)
        self.i += 1
        return self.tiles[k], self.bufs[k]


class Prog:
    def __init__(self, seq_lens, debug=False):
        self.seq_lens = seq_lens
        self.debug = debug
        self.nc = bass.Bass("TRN2", target_bir_lowering=False)
        self.tk = TK(self.nc)
        self.uid = 0

    def name(self, p):
        self.uid += 1
        return "%s_%d" % (p, self.uid)

    def sb(self, es, shape, dt, name="sb"):
        return es.enter_context(self.nc.sbuf_tensor(self.name(name), list(shape), dt))

    def ps(self, es, shape, dt, name="ps"):
        return es.enter_context(self.nc.psum_tensor(self.name(name), list(shape), dt))

    def sb_ring(self, es, n, shape, dt, name="r"):
        return Ring([self.sb(es, shape, dt, name) for _ in range(n)])

    def ps_ring(self, es, n, shape, dt, name="p"):
        return Ring([self.ps(es, shape, dt, name) for _ in range(n)])

    def dram(self, name, shape, dt, kind=None):
        if kind is None:
            return self.nc.dram_tensor(name, list(shape), dt).ap()
        return self.nc.dram_tensor(name, list(shape), dt, kind=kind).ap()

    def load_const(self, es, dram_ap, shape, dt, name):
        t = self.sb(es, shape, dt, name)
        b = Buf(name)
        self.tk.dma("sp", t[:], dram_ap, writes=[b])
        return t, b

    def prep_weights(self, w_srcs, wb_dsts, g_cols):
        tk = self.tk
        with contextlib.ExitStack() as es:
            CW = 2048
            ld = self.sb_ring(es, 3, [128, CW], F32, "wld")
            st = self.sb_ring(es, 3, [128, CW], BF16, "wst")
            flip = 0
            for src, dst, gc in zip(w_srcs, wb_dsts, g_cols):
                R, C = src.shape
                for k in range(R // 128):
                    for c0 in range(0, C, CW):
                        cw = min(CW, C - c0)
                        lt, lb = ld.next()
                        tk.dma("sp", lt[:, :cw], src[k * 128:(k + 1) * 128, c0:c0 + cw], writes=[lb])
                        stt, sbf = st.next()
                        if gc is not None:
                            gap, gb = gc(k)
                            tk.op("dve", lambda e: e.tensor_scalar(out=stt[:, :cw], in0=lt[:, :cw], scalar1=gap,
                                                                    scalar2=None, op0=ALU.mult),
                                  reads=[lb, gb], writes=[sbf])
                        else:
                            if flip % 2 == 0:
                                tk.op("act", lambda e: e.activation(out=stt[:, :cw], in_=lt[:, :cw], func=AF.Copy),
                                      reads=[lb], writes=[sbf])
                            else:
                                tk.op("dve", lambda e: e.tensor_copy(out=stt[:, :cw], in_=lt[:, :cw]),
                                      reads=[lb], writes=[sbf])
                            flip += 1
                        tk.dma("pool", dst[k * 128:(k + 1) * 128, c0:c0 + cw], stt[:, :cw], reads=[sbf])
        tk.barrier()

    def phase_proj(self, T, x_src, wb, fm_jobs, tm_jobs):
        tk = self.tk
        nsg = T // 2048
        with contextlib.ExitStack() as es:
            xall = self.sb(es, [128, 16, 1024], F32, "xall")
            xb = [Buf() for _ in range(16)]
            hT = self.sb(es, [128, 8, 2048], BF16, "hT")
            hTb = [Buf() for _ in range(16)]
            hbr = self.sb_ring(es, 2, [128, 1024], BF16, "hb")
            junk = self.sb(es, [128, 1024], BF16, "junk")
            junkb = Buf()
            ssq = self.sb(es, [128, 16], F32, "ssq")
            ssqb = [Buf() for _ in range(16)]
            rstd = self.sb(es, [128, 16], F32, "rstd")
            rstdb = Buf()
            wfm = self.sb_ring(es, 3, [128, 8, 128], BF16, "wfm")
            wtm = self.sb_ring(es, 2, [128, 8, 512], BF16, "wtm")
            stb = self.sb_ring(es, 4, [128, 512], BF16, "stb")
            stf = self.sb_ring(es, 3, [128, 512], F32, "stf")
            pst = self.ps_ring(es, 2, [128, 1024], BF16, "pst")
            pso = self.ps_ring(es, 5, [128, 512], F32, "pso")
            ident, identb = self.ident, self.identb
            ev = 0
            for g in range(nsg):
                t0 = g * 2048
                for t in range(16):
                    tk.dma("sp", xall[:, t, :], x_src[t0 + t * 128:t0 + (t + 1) * 128, :], writes=[xb[t]])
                for t in range(16):
                    tk.op("act", lambda e: e.activation(out=junk[:], in_=xall[:, t, :], func=AF.Square,
                                                        accum_out=ssq[:, t:t + 1]),
                          reads=[xb[t]], writes=[junkb, ssqb[t]])
                tk.op("dve", lambda e: e.tensor_scalar(out=rstd[:], in0=ssq[:], scalar1=1.0 / D, scalar2=EPS,
                                                       op0=ALU.mult, op1=ALU.add), reads=ssqb, writes=[rstdb])
                tk.op("act", lambda e: e.activation(out=rstd[:], in_=rstd[:], func=AF.Sqrt), reads=[rstdb],
                      writes=[rstdb])
                tk.op("dve", lambda e: e.reciprocal(out=rstd[:], in_=rstd[:]), reads=[rstdb], writes=[rstdb])
                for t in range(16):
                    hb, hbb = hbr.next()
                    tk.op("act", lambda e: e.activation(out=hb[:], in_=xall[:, t, :], func=AF.Copy,
                                                        scale=rstd[:, t:t + 1]),
                          reads=[xb[t], rstdb], writes=[hbb])
                    pt, ptb = pst.next()
                    for k in range(8):
                        tk.op("pe", lambda e: e.transpose(out=pt[:, k * 128:(k + 1) * 128],
                                                          in_=hb[:, k * 128:(k + 1) * 128], identity=ident[:]),
                              reads=[hbb, identb], writes=[ptb])
                    tk.op("dve", lambda e: e.tensor_copy(out=hT[:, :, t * 128:(t + 1) * 128],
                                                         in_=pt[:].rearrange("p (k c) -> p k c", k=8)),
                          reads=[ptb], writes=[hTb[t]])

                def evac(po, pob, mc, cw, func, dt, shape3=None):
                    nonlocal ev
                    ring = stb if dt == BF16 else stf
                    s_t, s_b = ring.next()
                    if func == "silu":
                        tk.op("act", lambda e: e.activation(out=s_t[:mc, :cw], in_=po[:mc, :cw], func=AF.Silu),
                              reads=[pob], writes=[s_b])
                    else:
                        if True:
                            tk.op("act", lambda e: e.activation(out=s_t[:mc, :cw], in_=po[:mc, :cw], func=AF.Copy),
                                  reads=[pob], writes=[s_b])
                        else:
                            tk.op("dve", lambda e: e.tensor_copy(out=s_t[:mc, :cw], in_=po[:mc, :cw]),
                                  reads=[pob], writes=[s_b])
                        ev += 1
                    return s_t, s_b

                for (col0, ncols, dil, dst, row0, func, dt) in fm_jobs:
                    L = T // dil
                    for cc in range((ncols + 127) // 128):
                        mc = min(128, ncols - cc * 128)
                        c = col0 + cc * 128
                        w, wbuf = wfm.next()
                        tk.dma("sp", w[:, :, :mc], wb[:, c:c + mc].rearrange("(k p) c -> p k c", p=128),
                               writes=[wbuf])
                        for s in range(4):
                            po, pob = pso.next()
                            if dil == 1:
                                rd = hTb[4 * s:4 * s + 4]
                            else:
                                rd = hTb
                            for k in range(8):
                                if dil == 1:
                                    rhs = hT[:, k, s * 512:(s + 1) * 512]
                                    out = po[:mc, :]
                                elif dil == 4:
                                    rhs = hT[:, k, :].rearrange("p (m r) -> p r m", r=4)[:, s, :]
                                    out = po[:mc, :]
                                else:
                                    rhs = hT[:, k, :].rearrange("p (m r) -> p r m", r=16)[:, 4 * s:4 * s + 4, :]
                                    out = po[:mc, :].rearrange("p (r m) -> p r m", r=4)
                                tk.op("pe", lambda e: e.matmul(out, lhsT=w[:, k, :mc], rhs=rhs, start=(k == 0),
                                                               stop=(k == 7)),
                                      reads=[wbuf] + rd, writes=[pob])
                            s_t, s_b = evac(po, pob, mc, 512, func, dt)
                            r0 = row0 + cc * 128
                            if dil == 1:
                                tk.dma("pool", dst[r0:r0 + mc, t0 + s * 512:t0 + (s + 1) * 512], s_t[:mc, :],
                                       reads=[s_b])
                            elif dil == 4:
                                tk.dma("pool", dst[r0:r0 + mc, s * L + g * 512:s * L + (g + 1) * 512], s_t[:mc, :],
                                       reads=[s_b])
                            else:
                                d3 = dst.rearrange("c (r l) -> c r l", r=16)
                                tk.dma("pool", d3[r0:r0 + mc, 4 * s:4 * s + 4, g * 128:(g + 1) * 128],
                                       s_t[:mc, :].rearrange("p (r m) -> p r m", r=4), reads=[s_b])
                for (col0, ncols, dil, dst, cd0, func, dt) in tm_jobs:
                    L = T // dil
                    for cb in range((ncols + 511) // 512):
                        cw = min(512, ncols - cb * 512)
                        c = col0 + cb * 512
                        w, wbuf = wtm.next()
                        tk.dma("sp", w[:, :, :cw], wb[:, c:c + cw].rearrange("(k p) c -> p k c", p=128),
                               writes=[wbuf])
                        for u in range(16):
                            po, pob = pso.next()
                            if dil == 1:
                                rd = [hTb[u]]
                                rows = t0 + u * 128
                            elif dil == 4:
                                rd = hTb
                                rows = (u // 4) * L + g * 512 + (u % 4) * 128
                            else:
                                rd = hTb
                                rows = u * L + g * 128
                            for k in range(8):
                                if dil == 1:
                                    lhsT = hT[:, k, u * 128:(u + 1) * 128]
                                elif dil == 4:
                                    j = u % 4
                                    lhsT = hT[:, k, :].rearrange("p (m r) -> p r m", r=4)[:, u // 4,
                                                                                         j * 128:(j + 1) * 128]
                                else:
                                    lhsT = hT[:, k, :].rearrange("p (m r) -> p r m", r=16)[:, u, :]
                                tk.op("pe", lambda e: e.matmul(po[:, :cw], lhsT=lhsT, rhs=w[:, k, :cw],
                                                               start=(k == 0), stop=(k == 7)),
                                      reads=[wbuf] + rd, writes=[pob])
                            s_t, s_b = evac(po, pob, 128, cw, func, dt)
                            tk.dma("pool", dst[rows:rows + 128, cd0 + cb * 512:cd0 + cb * 512 + cw], s_t[:, :cw],
                                   reads=[s_b])
        tk.barrier()

    def phase_window(self, L, ncls, Hq, G, qT, kT, v, btab, btabb, mode, out, gate=None, esink=None,
                     esinkb=None, dil=1):
        tk = self.tk
        Hkv = Hq // G
        NTc = L // 128
        ngrp = Hq // 4
        LA = 3
        with contextlib.ExitStack() as es:
            qwr = self.sb_ring(es, 6, [128, Hq, 128], BF16, "qw")
            kwr = self.sb_ring(es, 6, [128, Hkv, 384], BF16, "kw")
            for ring in (qwr, kwr):
                for t_, b_ in zip(ring.tiles, ring.bufs):
                    tk.op("pool", lambda e: e.memset(t_[:], 0.0), writes=[b_])
            vwr = self.sb_ring(es, 6, [128, 3, Hkv, 65], BF16, "vw")
            for vt, vb_ in zip(vwr.tiles, vwr.bufs):
                tk.op("pool", lambda e: e.memset(vt[:], 1.0), writes=[vb_])
            ptr = self.sb_ring(es, 6, [128, 512], BF16, "pT")
            scr = self.ps_ring(es, 4, [128, 512], F32, "sc")
            accr = self.ps_ring(es, 4, [128, 512], F32, "acc")
            small = self.sb_ring(es, 6, [128, 8], F32, "sm")
            if mode == "A":
                gr = self.sb_ring(es, 6, [128, Hq * 64], BF16, "gate")
                yr = self.sb_ring(es, 6, [128, Hq * 64], F32, "yf")
                ybr = self.sb_ring(es, 2, [128, Hq * 64], BF16, "yb")
            else:
                yr = self.sb_ring(es, 6, [128, Hq, 65], F32, "nb")
                out3 = out.rearrange("(m r) c -> r m c", r=dil)
            tiles = [(r, t) for r in range(ncls) for t in range(NTc)]

            def load_tile(n):
                r, t = tiles[n]
                c0 = r * L
                k0 = max(t - 1, 0)
                k1 = min(t + 1, NTc - 1)
                nk = k1 - k0 + 1
                ctx = {"r": r, "t": t, "k0": k0, "nk": nk}
                qw, qwb = qwr.next()
                tk.dma("sp", qw[0:64], qT[:, c0 + t * 128:c0 + (t + 1) * 128].rearrange("(h d) t -> d h t", d=64),
                       writes=[qwb])
                kw, kwb = kwr.next()
                tk.dma("sp", kw[0:64, :, :nk * 128],
                       kT[:, c0 + k0 * 128:c0 + (k1 + 1) * 128].rearrange("(h d) t -> d h t", d=64), writes=[kwb])
                vw, vwb = vwr.next()
                for j in range(nk):
                    tk.dma("sp", vw[:, j, :, 0:64],
                           v[c0 + (k0 + j) * 128:c0 + (k0 + j + 1) * 128, :].rearrange("p (h d) -> p h d", d=64),
                           writes=[vwb])
                ctx.update(q=(qw, qwb), k=(kw, kwb), v=(vw, vwb))
                if mode == "A":
                    gt, gtb = gr.next()
                    tk.dma("sp", gt[:], gate[t * 128:(t + 1) * 128, :], writes=[gtb])
                    ctx["gate"] = (gt, gtb)
                ctx["y"] = yr.next()
                return ctx

            ctxs = {0: load_tile(0)}
            items = []
            for n, (r, t) in enumerate(tiles):
                nk = min(t + 1, NTc - 1) - max(t - 1, 0) + 1
                for g in range(ngrp):
                    for j in range(nk):
                        items.append({"n": n, "g": g, "j": j, "nk": nk, "tstart": g == 0 and j == 0,
                                      "tlast": g == ngrp - 1 and j == nk - 1})

            def stage_a(it):
                n = it["n"]
                if it["tstart"] and n + 1 < len(tiles):
                    ctxs[n + 1] = load_tile(n + 1)
                c = ctxs[n]
                g, j = it["g"], it["j"]
                kind = (c["k0"] + j) - c["t"] + 1
                qw, qwb = c["q"]
                kw, kwb = c["k"]
                sc, scb = scr.next()
                it["sc"] = (sc, scb)
                tk.op("pe", lambda e: e.matmul(sc[:], lhsT=self.ident[:], rhs=btab[:, kind, 4 * g:4 * g + 4, :],
                                               start=True, stop=False), reads=[self.identb, btabb], writes=[scb])
                if G == 4:
                    tk.op("pe", lambda e: e.matmul(sc[:], lhsT=kw[:, g, j * 128:(j + 1) * 128],
                                                   rhs=qw[:, 4 * g:4 * g + 4, :], start=False, stop=True),
                          reads=[kwb, qwb], writes=[scb])
                else:
                    for hh in range(4):
                        h = 4 * g + hh
                        tk.op("pe", lambda e: e.matmul(sc[:, hh * 128:(hh + 1) * 128],
                                                       lhsT=kw[:, h, j * 128:(j + 1) * 128], rhs=qw[:, h, :],
                                                       start=False, stop=(hh == 3)),
                              reads=[kwb, qwb], writes=[scb])

            def stage_b(it):
                n = it["n"]
                c = ctxs[n]
                g, j, nk = it["g"], it["j"], it["nk"]
                sc, scb = it["sc"]
                vw, vwb = c["v"]
                yt, ytb = c["y"]
                pT, pTb = ptr.next()
                tk.op("act", lambda e: e.activation(out=pT[:], in_=sc[:], func=AF.Exp, scale=0.125),
                      reads=[scb], writes=[pTb])
                if j == 0:
                    c["acc"] = accr.next()
                ac, acb = c["acc"]
                for hh in range(4):
                    h = 4 * g + hh
                    hv = g if G == 4 else h
                    tk.op("pe", lambda e: e.matmul(ac[:, hh * 128:hh * 128 + 65], lhsT=pT[:, hh * 128:(hh + 1) * 128],
                                                   rhs=vw[:, j, hv, :], start=(j == 0 and hh == 0),
                                                   stop=(j == nk - 1), skip_group_check=True),
                          reads=[pTb, vwb], writes=[acb])
                if j < nk - 1:
                    return
                ac3 = ac[:].rearrange("p (h c) -> p h c", h=4)[:, :, 0:65]
                if mode == "A":
                    sm, smb = small.next()
                    tk.op("dve", lambda e: e.tensor_tensor(out=sm[:, 0:4], in0=ac3[:, :, 64],
                                                           in1=esink[:, 4 * g:4 * g + 4], op=ALU.add),
                          reads=[acb, esinkb], writes=[smb])
                    tk.op("dve", lambda e: e.reciprocal(out=sm[:, 4:8], in_=sm[:, 0:4]), reads=[smb], writes=[smb])
                    for hh in range(4):
                        h = 4 * g + hh
                        tk.op("dve", lambda e: e.tensor_scalar(out=yt[:, h * 64:(h + 1) * 64], in0=ac3[:, hh, 0:64],
                                                               scalar1=sm[:, 4 + hh:5 + hh], scalar2=None,
                                                               op0=ALU.mult),
                              reads=[acb, smb], writes=[ytb])
                else:
                    tk.op("dve", lambda e: e.tensor_copy(out=yt[:, 4 * g:4 * g + 4, :], in_=ac3),
                          reads=[acb], writes=[ytb])
                if not it["tlast"]:
                    return
                r, t = c["r"], c["t"]
                if mode == "A":
                    gt, gtb = c["gate"]
                    yb_, ybb = ybr.next()
                    tk.op("pool", lambda e: e.tensor_tensor(out=yb_[:], in0=yt[:], in1=gt[:], op=ALU.mult),
                          reads=[ytb, gtb], writes=[ybb])
                    tk.dma("pool", out[t * 128:(t + 1) * 128, :], yb_[:], reads=[ybb])
                else:
                    tk.dma("pool", out3[r, t * 128:(t + 1) * 128, :], yt[:].rearrange("p h c -> p (h c)"),
                           reads=[ytb])
                del ctxs[n]

            ni = len(items)
            for i in range(ni + LA):
                if i < ni:
                    stage_a(items[i])
                if i - LA >= 0:
                    stage_b(items[i - LA])
        tk.barrier()

    def phase_out(self, T, x_src, x_dst, wob, ytm_src, yT_srcs, nbs=None, gb=None, wperm=None):
        tk = self.tk
        NT = T // 128
        with contextlib.ExitStack() as es:
            wo = self.sb(es, [128, 12, 1024], BF16, "wo")
            wob_ = Buf()
            tk.dma("sp", wo[:], wob.rearrange("(k p) c -> p k c", p=128), writes=[wob_])
            ytr = self.sb_ring(es, 3, [128, 1536], BF16, "ytm")
            yTr = self.sb_ring(es, 3, [128, 12, 128], BF16, "yT")
            xr = self.sb_ring(es, 4, [128, 1024], F32, "xo")
            if nbs is not None:
                nbr = [self.sb_ring(es, 3, [128, 8, 65], F32, "nbl") for _ in range(3)]
                gbr = self.sb_ring(es, 3, [128, 512], BF16, "gbl")
                rr = self.sb_ring(es, 2, [128, 8], F32, "rr")
                ybf = self.sb_ring(es, 2, [128, 512], F32, "ybf")
            pst = self.ps_ring(es, 2, [128, 1024], BF16, "ptr")
            pso = self.ps_ring(es, 4, [128, 512], F32, "pso")
            def load(t):
                rows = slice(t * 128, (t + 1) * 128)
                c = {}
                c["x"] = xr.next()
                tk.dma("sp", c["x"][0][:], x_src[rows, :], writes=[c["x"][1]])
                c["yT"] = yTr.next()
                for (srcT, ch0, nchk) in yT_srcs:
                    tk.dma("sp", c["yT"][0][:, ch0:ch0 + nchk, :],
                           srcT[:, t * 128:(t + 1) * 128].rearrange("(k p) t -> p k t", p=128), writes=[c["yT"][1]])
                if ytm_src or nbs is not None:
                    c["yt"] = ytr.next()
                    for (src, c0, w) in ytm_src:
                        tk.dma("sp", c["yt"][0][:, c0:c0 + w], src[rows, :], writes=[c["yt"][1]])
                    if nbs is not None:
                        c["nl"] = []
                        for p in range(3):
                            nt_, ntb_ = nbr[p].next()
                            tk.dma("sp", nt_[:].rearrange("p h c -> p (h c)"), nbs[p][rows, :], writes=[ntb_])
                            c["nl"].append((nt_, ntb_))
                        c["gb"] = gbr.next()
                        tk.dma("sp", c["gb"][0][:], gb[rows, :], writes=[c["gb"][1]])
                return c

            nxt = load(0)
            for t in range(NT):
                rows = slice(t * 128, (t + 1) * 128)
                cur = nxt
                if t + 1 < NT:
                    nxt = load(t + 1)
                xt, xtb = cur["x"]
                yT, yTb = cur["yT"]
                ntm = 0
                if ytm_src or nbs is not None:
                    yt, ytb = cur["yt"]
                    for (src, c0, w) in ytm_src:
                        ntm = max(ntm, c0 + w)
                    if nbs is not None:
                        nl = cur["nl"]
                        gbt, gbtb = cur["gb"]
                        n0, n0b = nl[0]
                        tk.op("pool", lambda e: e.tensor_tensor(out=n0[:], in0=n0[:], in1=nl[1][0][:], op=ALU.add),
                              reads=[n0b, nl[1][1]], writes=[n0b])
                        tk.op("pool", lambda e: e.tensor_tensor(out=n0[:], in0=n0[:], in1=nl[2][0][:], op=ALU.add),
                              reads=[n0b, nl[2][1]], writes=[n0b])
                        r_, rb_ = rr.next()
                        tk.op("dve", lambda e: e.reciprocal(out=r_[:], in_=n0[:, :, 64]), reads=[n0b], writes=[rb_])
                        yf, yfb = ybf.next()
                        for h in range(8):
                            tk.op("dve", lambda e: e.tensor_scalar(out=yf[:, h * 64:(h + 1) * 64], in0=n0[:, h, 0:64],
                                                                   scalar1=r_[:, h:h + 1], scalar2=None,
                                                                   op0=ALU.mult),
                                  reads=[n0b, rb_], writes=[yfb])
                        tk.op("pool", lambda e: e.tensor_tensor(out=yt[:, 1024:1536], in0=yf[:], in1=gbt[:],
                                                                op=ALU.mult),
                              reads=[yfb, gbtb], writes=[ytb])
                        ntm = 1536
                    nch = ntm // 128
                    for c in range(nch):
                        if c % 8 == 0:
                            pt, ptb = pst.next()
                        tk.op("pe", lambda e: e.transpose(out=pt[:, (c % 8) * 128:(c % 8 + 1) * 128],
                                                          in_=yt[:, c * 128:(c + 1) * 128], identity=self.ident[:]),
                              reads=[ytb, self.identb], writes=[ptb])
                        if c % 8 == 7 or c == nch - 1:
                            cs = (c // 8) * 8
                            n = c - cs + 1
                            tk.op("dve", lambda e: e.tensor_copy(
                                out=yT[:, cs:cs + n, :],
                                in_=pt[:, 0:n * 128].rearrange("p (k c) -> p k c", k=n)),
                                reads=[ptb], writes=[yTb])
                for nh in range(2):
                    po, pob = pso.next()
                    for c in range(12):
                        tk.op("pe", lambda e: e.matmul(po[:], lhsT=yT[:, c, :],
                                                       rhs=wo[:, (c if wperm is None else wperm[c]), nh * 512:(nh + 1) * 512],
                                                       start=(c == 0), stop=(c == 11)),
                              reads=[yTb, wob_], writes=[pob])
                    tk.op("dve", lambda e: e.tensor_tensor(out=xt[:, nh * 512:(nh + 1) * 512], in0=po[:],
                                                           in1=xt[:, nh * 512:(nh + 1) * 512], op=ALU.add),
                          reads=[pob, xtb], writes=[xtb])
                tk.dma("pool", x_dst[rows, :], xt[:], reads=[xtb])
        tk.barrier()

    def phase_diff(self, T, qcT, kcT, vc, gcT, ycT, cst, neglam, neglamb, sg, sgb, nheads=8, skip_slopes=None):
        tk = self.tk
        NT = T // 128
        NG = T // 512
        SKIP = 150.0
        LA = 3
        sl = alibi(8) if skip_slopes is None else skip_slopes
        with contextlib.ExitStack() as es:
            kr = self.sb_ring(es, 2, [128, T], BF16, "kTh")
            vr = self.sb_ring(es, 2, [128, NT, 128], BF16, "vh")
            ULr = self.sb_ring(es, 2, [128, 512], F32, "UL")
            URr = self.sb_ring(es, 2, [128, 512], F32, "UR")
            T1r = self.sb_ring(es, 2, [128, 7, 128], BF16, "T1")
            crr = self.sb_ring(es, 2, [128, 7, 128], BF16, "cr")
            bLr = self.sb_ring(es, 2, [128, 64], F32, "bL")
            bRr = self.sb_ring(es, 2, [128, 64], F32, "bR")
            onesf = self.sb(es, [128, 128], F32, "onesf")
            onesb = self.sb(es, [128, 128], BF16, "onesb")
            cb = Buf()
            tk.op("pool", lambda e: e.memset(onesf[:], 1.0), writes=[cb])
            tk.op("pool", lambda e: e.memset(onesb[:], 1.0), writes=[cb])
            qr = self.sb_ring(es, 2, [128, 2, 512], BF16, "qTh")
            for t_, b_ in zip(qr.tiles, qr.bufs):
                tk.op("pool", lambda e: e.memset(t_[:], 0.0), writes=[b_])
            ones1 = self.sb(es, [128, 128], BF16, "ones1p")
            tk.op("pool", lambda e: e.memset(ones1[:], 0.0), writes=[cb])
            tk.op("pool", lambda e: e.memset(ones1[0:1, :], 1.0), writes=[cb])
            for t_, b_ in zip(crr.tiles, crr.bufs):
                tk.op("pool", lambda e: e.memset(t_[:], 0.0), writes=[b_])
            gr = self.sb_ring(es, 2, [128, 512], BF16, "gch")
            ptr = self.sb_ring(es, 6, [128, 512], BF16, "pT")
            osr = self.sb_ring(es, 2, [128, 4, 512], F32, "osum")
            tmpr = self.sb_ring(es, 3, [128, 512], F32, "tmp")
            yr = self.sb_ring(es, 2, [128, 512], BF16, "yc")
            scr = self.ps_ring(es, 4, [128, 512], F32, "sc")
            accn = [(self.ps(es, [128, 512], F32, "accn"), Buf()) for _ in range(2)]
            accd = [(self.ps(es, [128, 512], F32, "accd"), Buf()) for _ in range(2)]
            items = []
            for h in range(nheads):
                hctx = {"h": h}
                for G in range(NG):
                    gctx = {"G": G, "hctx": hctx}
                    zl = []
                    for zn, kts in (("L", range(0, 4 * G)), ("D", range(4 * G, 4 * G + 4)),
                                    ("R", range(4 * G + 4, NT))):
                        keep = []
                        for kt in kts:
                            if zn == "L":
                                md = 128 * (4 * G - kt) - 127
                            elif zn == "R":
                                md = 128 * kt - (512 * G + 511)
                            else:
                                md = 0
                            if sl[h] * md >= SKIP:
                                continue
                            keep.append(kt)
                        if keep:
                            zl.append((zn, keep))
                    for zi, (zn, keep) in enumerate(zl):
                        for idx, kt in enumerate(keep):
                            for p in range(2):
                                items.append({"g": gctx, "zn": zn, "kt": kt, "p": p, "first": idx == 0,
                                              "last": idx == len(keep) - 1, "zfirst": zi == 0,
                                              "zlast": zi == len(zl) - 1,
                                              "gstart": zi == 0 and idx == 0 and p == 0})

            def stage_a(it):
                g = it["g"]
                hc = g["hctx"]
                h = hc["h"]
                G = g["G"]
                hr = slice(h * 128, (h + 1) * 128)
                if "k" not in hc:
                    hc["k"] = kr.next()
                    hc["v"] = vr.next()
                    tk.dma("sp", hc["k"][0][:], kcT[hr, :], writes=[hc["k"][1]])
                    tk.dma("sp", hc["v"][0][:], vc[:, hr].rearrange("(n p) d -> p n d", p=128), writes=[hc["v"][1]])
                    for nm, ring, src in (("UL", ULr, cst["UL"][:, h, :]), ("UR", URr, cst["UR"][:, h, :]),
                                          ("T1", T1r, cst["T1"][:, h, :, :]), ("CR", crr, cst["CR"][:, h, :, :]),
                                          ("BL", bLr, cst["BL"][:, h, :]), ("BR", bRr, cst["BR"][:, h, :])):
                        hc[nm] = ring.next()
                        if nm == "CR":
                            tk.dma("sp", hc[nm][0][0:1, :, :], src, writes=[hc[nm][1]])
                        else:
                            tk.dma("sp", hc[nm][0][:], src, writes=[hc[nm][1]])
                if "q" not in g:
                    g["q"] = qr.next()
                    tk.dma("sp", g["q"][0][0:64, 0, :], qcT[h * 128:h * 128 + 64, G * 512:(G + 1) * 512],
                           writes=[g["q"][1]])
                    tk.dma("sp", g["q"][0][64:128, 1, :], qcT[h * 128 + 64:h * 128 + 128, G * 512:(G + 1) * 512],
                           writes=[g["q"][1]])
                    g["gt"] = gr.next()
                    tk.dma("sp", g["gt"][0][:], gcT[hr, G * 512:(G + 1) * 512], writes=[g["gt"][1]])
                    g["os"] = None
                kT, kTb = hc["k"]
                qt, qtb = g["q"]
                p = it["p"]
                kt = it["kt"]
                pr = slice(64 * p, 64 * p + 64)
                sc, scb = scr.next()
                it["sc"] = (sc, scb)
                if it["zn"] == "D":
                    ktl = kt - 4 * G
                    T1h, T1b = hc["T1"]
                    crh, crb = hc["CR"]
                    tk.op("pe", lambda e: e.matmul(sc[:], lhsT=self.ident[:], rhs=T1h[:, 3 - ktl:7 - ktl, :],
                                                   start=True, stop=False),
                          reads=[self.identb, T1b], writes=[scb])
                    tk.op("pe", lambda e: e.matmul(sc[:], lhsT=ones1[:, :], rhs=crh[:, 3 - ktl:7 - ktl, :],
                                                   start=False, stop=False),
                          reads=[cb, crb], writes=[scb])
                    tk.op("pe", lambda e: e.matmul(sc[:], lhsT=kT[:, kt * 128:(kt + 1) * 128], rhs=qt[:, p, :],
                                                   start=False, stop=True),
                          reads=[kTb, qtb], writes=[scb])
                else:
                    tk.op("pe", lambda e: e.matmul(sc[:], lhsT=kT[:, kt * 128:(kt + 1) * 128], rhs=qt[:, p, :],
                                                   start=True, stop=True),
                          reads=[kTb, qtb], writes=[scb])

            cnt = [0]

            def stage_b(it):
                g = it["g"]
                hc = g["hctx"]
                G = g["G"]
                p = it["p"]
                kt = it["kt"]
                sc, scb = it["sc"]
                zn = it["zn"]
                rd = [scb]
                if zn == "D":
                    bias = 0.0
                elif zn == "L":
                    dl = 4 * G - kt
                    bias = hc["BL"][0][:, dl:dl + 1]
                    rd.append(hc["BL"][1])
                else:
                    dr = kt - 4 * G - 4
                    bias = hc["BR"][0][:, dr:dr + 1]
                    rd.append(hc["BR"][1])
                pT, pTb = ptr.next()
                tk.op("act", lambda e: e.activation(out=pT[:], in_=sc[:], func=AF.Exp, scale=0.125, bias=bias),
                      reads=rd, writes=[pTb])
                vh, vhb = hc["v"]
                an, anb = accn[p]
                tk.op("pe", lambda e: e.matmul(an[:], lhsT=vh[:, kt, :], rhs=pT[:], start=it["first"],
                                               stop=it["last"]),
                      reads=[vhb, pTb], writes=[anb])
                ad_, adb = accd[p]
                tk.op("pe", lambda e: e.matmul(ad_[:], lhsT=onesb[:], rhs=pT[:], start=it["first"], stop=it["last"]),
                      reads=[cb, pTb], writes=[adb])
                if not it["last"]:
                    return
                if g["os"] is None:
                    g["os"] = osr.next()
                    g["osb"] = [Buf() for _ in range(4)]
                osum = g["os"][0]
                osb = g["osb"]
                for (ac, acb, i) in ((an, anb, 2 * p), (ad_, adb, 2 * p + 1)):
                    if zn == "D":
                        if it["zfirst"]:
                            tk.op("dve", lambda e: e.tensor_copy(out=osum[:, i, :], in_=ac[:]), reads=[acb],
                                  writes=[osb[i]])
                        else:
                            tk.op("dve", lambda e: e.tensor_tensor(out=osum[:, i, :], in0=ac[:], in1=osum[:, i, :],
                                                                   op=ALU.add),
                                  reads=[acb, osb[i]], writes=[osb[i]])
                    else:
                        U, Ub = hc["UL"] if zn == "L" else hc["UR"]
                        if it["zfirst"]:
                            tk.op("dve", lambda e: e.tensor_tensor(out=osum[:, i, :], in0=ac[:], in1=U[:], op=ALU.mult),
                                  reads=[acb, Ub], writes=[osb[i]])
                        else:
                            tm_, tmb = tmpr.next()
                            tk.op("dve", lambda e: e.tensor_tensor(out=tm_[:], in0=ac[:], in1=U[:], op=ALU.mult),
                                  reads=[acb, Ub], writes=[tmb])
                            tk.op("pool", lambda e: e.tensor_tensor(out=osum[:, i, :], in0=tm_[:], in1=osum[:, i, :],
                                                                    op=ALU.add),
                                  reads=[tmb, osb[i]], writes=[osb[i]])
                if not (it["zlast"] and p == 1):
                    return
                h = hc["h"]
                hr = slice(h * 128, (h + 1) * 128)
                for pp in range(2):
                    tk.op("dve", lambda e: e.reciprocal(out=osum[:, 2 * pp + 1, :], in_=osum[:, 2 * pp + 1, :]),
                          reads=[osb[2 * pp + 1]], writes=[osb[2 * pp + 1]])
                    tk.op("dve", lambda e: e.tensor_tensor(out=osum[:, 2 * pp, :], in0=osum[:, 2 * pp, :],
                                                           in1=osum[:, 2 * pp + 1, :], op=ALU.mult),
                          reads=[osb[2 * pp], osb[2 * pp + 1]], writes=[osb[2 * pp]])
                tk.op("dve", lambda e: e.scalar_tensor_tensor(out=osum[:, 0, :], in0=osum[:, 2, :], scalar=neglam,
                                                              in1=osum[:, 0, :], op0=ALU.mult, op1=ALU.add),
                      reads=[osb[0], osb[2], neglamb], writes=[osb[0]])
                tk.op("pool", lambda e: e.tensor_tensor(out=osum[:, 1, :], in0=osum[:, 0, :], in1=osum[:, 0, :],
                                                        op=ALU.mult),
                      reads=[osb[0]], writes=[osb[1]])
                sc2, sc2b = accd[1]
                tk.op("pe", lambda e: e.matmul(sc2[:], lhsT=onesf[:], rhs=osum[:, 1, :], start=True, stop=True),
                      reads=[cb, osb[1]], writes=[sc2b])
                tk.op("act", lambda e: e.activation(out=osum[:, 3, :], in_=sc2[:], func=AF.Ln, scale=1.0 / 128,
                                                    bias=self.epsc[:, 0:1]),
                      reads=[sc2b, self.epscb], writes=[osb[3]])
                tk.op("act", lambda e: e.activation(out=osum[:, 3, :], in_=osum[:, 3, :], func=AF.Exp, scale=-0.5),
                      reads=[osb[3]], writes=[osb[3]])
                tk.op("dve", lambda e: e.scalar_tensor_tensor(out=osum[:, 0, :], in0=osum[:, 0, :], scalar=sg,
                                                              in1=osum[:, 3, :], op0=ALU.mult, op1=ALU.mult),
                      reads=[osb[0], osb[3], sgb], writes=[osb[0]])
                yt, ytb = yr.next()
                gt, gtb = g["gt"]
                tk.op("pool", lambda e: e.tensor_tensor(out=yt[:], in0=osum[:, 0, :], in1=gt[:], op=ALU.mult),
                      reads=[osb[0], gtb], writes=[ytb])
                tk.dma("pool", ycT[hr, G * 512:(G + 1) * 512], yt[:], reads=[ytb])

            n = len(items)
            for i in range(n + LA):
                if i < n:
                    stage_a(items[i])
                if i - LA >= 0:
                    stage_b(items[i - LA])
        tk.barrier()

    def phase_gla(self, T, qd, kd, vd, adT, gd, of, ydT, waug, gcst, glag, glagb):
        tk = self.tk
        NT = T // 128
        with contextlib.ExitStack() as es:
            S = self.sb(es, [128, 4, 128], F32, "S")
            Sb = self.sb(es, [128, 4, 128], BF16, "Sb")
            Sbuf = [Buf() for _ in range(4)]
            Sbb = [Buf() for _ in range(4)]
            cmat = self.sb(es, [128, 2, 128], F32, "cmat")
            maskr = self.sb(es, [128, 512], BF16, "mask")
            ind = self.sb(es, [128, 2], F32, "ind")
            wa = self.sb(es, [17, 512], F32, "waug")
            cb = Buf()
            tk.dma("sp", ind[:], gcst["IND"], writes=[cb])
            adr = self.sb_ring(es, 3, [17, 128], F32, "ad")
            for t_, b_ in zip(adr.tiles, adr.bufs):
                tk.op("pool", lambda e: e.memset(t_[0:1, :], 1.0), writes=[b_])
            qr = self.sb_ring(es, 3, [128, 512], F32, "q")
            kr = self.sb_ring(es, 3, [128, 512], F32, "k")
            vr = self.sb_ring(es, 3, [128, 512], BF16, "v")
            ofr = self.sb_ring(es, 3, [128, 512], F32, "of")
            gdr = self.sb_ring(es, 3, [128, 512], BF16, "gd")
            spr = self.sb_ring(es, 2, [128, 512], F32, "sp")
            er = self.sb_ring(es, 4, [128, 512], F32, "E")
            qer = self.sb_ring(es, 2, [128, 512], BF16, "qe")
            ker = self.sb_ring(es, 2, [128, 512], BF16, "ke")
            ksr = self.sb_ring(es, 2, [128, 512], BF16, "ks")
            qeTr = self.sb_ring(es, 2, [128, 4, 128], BF16, "qeT")
            keTr = self.sb_ring(es, 2, [128, 4, 128], BF16, "keT")
            amr = self.sb_ring(es, 2, [128, 512], BF16, "am")
            decr = self.sb_ring(es, 2, [128, 4, 2], F32, "dec")
            osr = self.sb_ring(es, 2, [128, 512], F32, "os")
            smr = self.sb_ring(es, 2, [128, 8], F32, "sm")
            ybr = self.sb_ring(es, 2, [128, 512], BF16, "yb")
            yTr = self.sb_ring(es, 2, [128, 4, 128], BF16, "yT")
            psr = self.ps_ring(es, 5, [128, 512], F32, "ps")
            ptr = self.ps_ring(es, 2, [128, 1024], BF16, "pt")
            kvr = self.ps_ring(es, 1, [128, 512], F32, "kv")
            assert True
            for dr in range(2):
                tk.dma("sp", cmat[:], gcst["CM"][dr], writes=[cb])
                tk.dma("sp", maskr[:], gcst["MASK"][dr], writes=[cb])
                tk.dma("sp", wa[:], waug[dr], writes=[cb])
                for h in range(4):
                    tk.op("pool", lambda e: e.memset(S[:, h, :], 0.0), writes=[Sbuf[h]])
                    tk.op("pool", lambda e: e.memset(Sb[:, h, :], 0.0), writes=[Sbb[h]])
                order = range(NT) if dr == 0 else range(NT - 1, -1, -1)
                def gload(t):
                    rows = slice(t * 128, (t + 1) * 128)
                    c = {}
                    c["ad"] = adr.next()
                    tk.dma("sp", c["ad"][0][1:17, :], adT[16 * dr:16 * dr + 16, rows], writes=[c["ad"][1]])
                    for nm, ring, src in (("q", qr, qd), ("k", kr, kd), ("v", vr, vd)):
                        c[nm] = ring.next()
                        tk.dma("sp", c[nm][0][:], src[rows, :], writes=[c[nm][1]])
                    if dr == 1:
                        c["of"] = ofr.next()
                        tk.dma("sp", c["of"][0][:], of[rows, :], writes=[c["of"][1]])
                        c["gd"] = gdr.next()
                        tk.dma("sp", c["gd"][0][:], gd[rows, :], writes=[c["gd"][1]])
                    return c

                order = list(order)
                nxt = gload(order[0])
                for oi, t in enumerate(order):
                    rows = slice(t * 128, (t + 1) * 128)
                    cur = nxt
                    if oi + 1 < len(order):
                        nxt = gload(order[oi + 1])
                    ad_, adb = cur["ad"]
                    q_, qb_ = cur["q"]
                    k_, kb_ = cur["k"]
                    v_, vb_ = cur["v"]
                    if dr == 1:
                        of_, ofb = cur["of"]
                        gd_, gdb = cur["gd"]
                    lvl = int(os.environ.get("GLALVL", "99"))
                    if lvl < 2:
                        continue
                    px, pxb = psr.next()
                    tk.op("pe", lambda e: e.matmul(px[:], lhsT=ad_[:, :], rhs=wa[:, :], start=True, stop=True),
                          reads=[adb, cb], writes=[pxb])
                    sp_, spb = spr.next()
                    tk.op("act", lambda e: e.activation(out=sp_[:], in_=px[:], func=AF.Exp, scale=-1.0),
                          reads=[pxb], writes=[spb])
                    tk.op("act", lambda e: e.activation(out=sp_[:], in_=sp_[:], func=AF.Ln, bias=self.onec[:, 0:1]),
                          reads=[spb, self.epscb], writes=[spb])
                    if lvl < 3:
                        continue
                    pcs, pcsb = psr.next()
                    tk.op("pe", lambda e: e.matmul(pcs[:], lhsT=cmat[:, 0, :], rhs=sp_[:], start=True, stop=True),
                          reads=[cb, spb], writes=[pcsb])
                    paf, pafb = psr.next()
                    tk.op("pe", lambda e: e.matmul(paf[:], lhsT=cmat[:, 1, :], rhs=sp_[:], start=True, stop=True),
                          reads=[cb, spb], writes=[pafb])
                    ptot, ptotb = psr.next()
                    for h in range(4):
                        tk.op("pe", lambda e: e.matmul(ptot[:, 2 * h:2 * h + 2], lhsT=sp_[:, h * 128:(h + 1) * 128],
                                                       rhs=ind[:, :], start=True, stop=True),
                              reads=[spb, cb], writes=[ptotb])
                    e1, e1b = er.next()
                    e2, e2b = er.next()
                    e3, e3b = er.next()
                    tk.op("act", lambda e: e.activation(out=e1[:], in_=pcs[:], func=AF.Exp, scale=-1.0 / 16),
                          reads=[pcsb], writes=[e1b])
                    tk.op("act", lambda e: e.activation(out=e2[:], in_=pcs[:], func=AF.Exp, scale=1.0 / 16),
                          reads=[pcsb], writes=[e2b])
                    tk.op("act", lambda e: e.activation(out=e3[:], in_=paf[:], func=AF.Exp, scale=-1.0 / 16),
                          reads=[pafb], writes=[e3b])
                    dec, decb = decr.next()
                    tk.op("act", lambda e: e.activation(out=dec[:].rearrange("p h c -> p (h c)"), in_=ptot[:, 0:8],
                                                        func=AF.Exp, scale=-1.0 / 16),
                          reads=[ptotb], writes=[decb])
                    if lvl < 4:
                        continue
                    qe, qeb = qer.next()
                    ke, keb = ker.next()
                    ks, ksb = ksr.next()
                    tk.op("dve", lambda e: e.scalar_tensor_tensor(out=qe[:], in0=q_[:], scalar=128 ** -0.5, in1=e1[:],
                                                                  op0=ALU.mult, op1=ALU.mult),
                          reads=[qb_, e1b], writes=[qeb])
                    tk.op("pool", lambda e: e.tensor_tensor(out=ke[:], in0=k_[:], in1=e2[:], op=ALU.mult),
                          reads=[kb_, e2b], writes=[keb])
                    tk.op("dve", lambda e: e.tensor_tensor(out=ks[:], in0=k_[:], in1=e3[:], op=ALU.mult),
                          reads=[kb_, e3b], writes=[ksb])
                    pt, ptb = ptr.next()
                    ptk, ptkb = ptr.next()
                    for h in range(4):
                        tk.op("pe", lambda e: e.transpose(out=pt[:, h * 128:(h + 1) * 128],
                                                          in_=qe[:, h * 128:(h + 1) * 128], identity=self.ident[:]),
                              reads=[qeb, self.identb], writes=[ptb])
                    for h in range(4):
                        tk.op("pe", lambda e: e.transpose(out=ptk[:, h * 128:(h + 1) * 128],
                                                          in_=ke[:, h * 128:(h + 1) * 128], identity=self.ident[:]),
                              reads=[keb, self.identb], writes=[ptkb])
                    qeT, qeTb = qeTr.next()
                    keT, keTb = keTr.next()
                    tk.op("dve", lambda e: e.tensor_copy(out=qeT[:].rearrange("p h c -> p (h c)"), in_=pt[:, 0:512]),
                          reads=[ptb], writes=[qeTb])
                    tk.op("act", lambda e: e.activation(out=keT[:].rearrange("p h c -> p (h c)"), in_=ptk[:, 0:512],
                                                        func=AF.Copy),
                          reads=[ptkb], writes=[keTb])
                    if lvl < 5:
                        continue
                    pa, pab = psr.next()
                    for h in range(4):
                        tk.op("pe", lambda e: e.matmul(pa[:, h * 128:(h + 1) * 128], lhsT=keT[:, h, :],
                                                       rhs=qeT[:, h, :], start=True, stop=True),
                              reads=[keTb, qeTb], writes=[pab])
                    am, amb = amr.next()
                    tk.op("dve", lambda e: e.tensor_tensor(out=am[:], in0=pa[:], in1=maskr[:], op=ALU.mult),
                          reads=[pab, cb], writes=[amb])
                    if lvl < 6:
                        continue
                    po, pob = psr.next()
                    for h in range(4):
                        tk.op("pe", lambda e: e.matmul(po[:, h * 128:(h + 1) * 128], lhsT=am[:, h * 128:(h + 1) * 128],
                                                       rhs=v_[:, h * 128:(h + 1) * 128], start=(h == 0), stop=False,
                                                       skip_group_check=True),
                              reads=[amb, vb_], writes=[pob])
                    corder = (0, 1) if dr == 0 else (1, 0)
                    for ci in corder:
                        cr_ = slice(64 * ci, 64 * ci + 64)
                        for h in range(4):
                            hs = slice(h * 128, (h + 1) * 128)
                            tk.op("pe", lambda e: e.matmul(po[cr_, hs], lhsT=qeT[:, h, cr_], rhs=Sb[:, h, :],
                                                           start=False, stop=True, skip_group_check=True),
                                  reads=[qeTb, Sbb[h]], writes=[pob])
                        if (dr == 0 and t == NT - 1 and ci == 1) or (dr == 1 and t == 0 and ci == 0):
                            continue
                        pkv, pkvb = kvr.next()
                        for h in range(4):
                            hs = slice(h * 128, (h + 1) * 128)
                            tk.op("pe", lambda e: e.matmul(pkv[:, hs], lhsT=ks[cr_, hs], rhs=v_[cr_, hs], start=True,
                                                           stop=True),
                                  reads=[ksb, vb_], writes=[pkvb])
                        for h in range(4):
                            hs = slice(h * 128, (h + 1) * 128)
                            tk.op("dve", lambda e: e.scalar_tensor_tensor(out=S[:, h, :], in0=S[:, h, :],
                                                                          scalar=dec[:, h, ci:ci + 1], in1=pkv[:, hs],
                                                                          op0=ALU.mult, op1=ALU.add),
                                  reads=[Sbuf[h], decb, pkvb], writes=[Sbuf[h]])
                            tk.op("act", lambda e: e.activation(out=Sb[:, h, :], in_=S[:, h, :], func=AF.Copy),
                                  reads=[Sbuf[h]], writes=[Sbb[h]])
                    if lvl < 7:
                        continue
                    os_, osb_ = osr.next()
                    if dr == 0:
                        tk.op("dve", lambda e: e.tensor_copy(out=os_[:], in_=po[:]), reads=[pob], writes=[osb_])
                        tk.dma("pool", of[rows, :], os_[:], reads=[osb_])
                        continue
                    tk.op("dve", lambda e: e.tensor_tensor(out=os_[:], in0=po[:], in1=of_[:], op=ALU.add),
                          reads=[pob, ofb], writes=[osb_])
                    e4, e4b = er.next()
                    tk.op("pool", lambda e: e.tensor_tensor(out=e4[:], in0=os_[:], in1=os_[:], op=ALU.mult),
                          reads=[osb_], writes=[e4b])
                    sm, smb = smr.next()
                    tk.op("dve", lambda e: e.tensor_reduce(out=sm[:, 0:4], in_=e4[:].rearrange("p (h d) -> p h d", h=4),
                                                           axis=AX.X, op=ALU.add),
                          reads=[e4b], writes=[smb])
                    tk.op("act", lambda e: e.activation(out=sm[:, 4:8], in_=sm[:, 0:4], func=AF.Ln, scale=1.0 / 128,
                                                        bias=self.epsc[:, 0:1]),
                          reads=[smb, self.epscb], writes=[smb])
                    tk.op("act", lambda e: e.activation(out=sm[:, 4:8], in_=sm[:, 4:8], func=AF.Exp, scale=-0.5),
                          reads=[smb], writes=[smb])
                    for h in range(4):
                        hs = slice(h * 128, (h + 1) * 128)
                        tk.op("dve", lambda e: e.scalar_tensor_tensor(out=os_[:, hs], in0=os_[:, hs],
                                                                      scalar=sm[:, 4 + h:5 + h], in1=glag[:, hs],
                                                                      op0=ALU.mult, op1=ALU.mult),
                              reads=[osb_, smb, glagb], writes=[osb_])
                    yb_, ybb = ybr.next()
                    tk.op("pool", lambda e: e.tensor_tensor(out=yb_[:], in0=os_[:], in1=gd_[:], op=ALU.mult),
                          reads=[osb_, gdb], writes=[ybb])
                    pt2, pt2b = ptr.next()
                    for h in range(4):
                        tk.op("pe", lambda e: e.transpose(out=pt2[:, h * 128:(h + 1) * 128],
                                                          in_=yb_[:, h * 128:(h + 1) * 128], identity=self.ident[:]),
                              reads=[ybb, self.identb], writes=[pt2b])
                    yT, yTb = yTr.next()
                    tk.op("act", lambda e: e.activation(out=yT[:].rearrange("p h c -> p (h c)"), in_=pt2[:, 0:512],
                                                        func=AF.Copy),
                          reads=[pt2b], writes=[yTb])
                    tk.dma("pool", ydT[:, rows].rearrange("(h p) t -> p h t", p=128), yT[:], reads=[yTb])
                tk.barrier()

    def phase_gather_select(self, T, loc, allg, sel, msk, mskb):
        tk = self.tk
        nc = self.nc
        if not hasattr(self, "ccs"):
            self.ccs = nc.alloc_semaphore("ccs")
            self.ccn = 0
        nc.gpsimd.collective_compute("AllGather", ALU.bypass, replica_groups=[list(range(8))],
                                     ins=[loc], outs=[allg]).then_inc(self.ccs)
        self.ccn += 1
        nc.gpsimd.wait_ge(self.ccs, self.ccn)
        tk.barrier()
        with contextlib.ExitStack() as es:
            ar = self.sb_ring(es, 3, [128, 2048], BF16, "ga")
            br = self.sb_ring(es, 3, [128, 2048], BF16, "gb")
            orr = self.sb_ring(es, 3, [128, 2048], BF16, "go")
            i = 0
            for k in range(8):
                for c0 in range(0, T, 2048):
                    a, ab = ar.next()
                    b, bb = br.next()
                    o, ob = orr.next()
                    tk.dma("sp", a[:], allg[k * 128:(k + 1) * 128, c0:c0 + 2048], writes=[ab])
                    tk.dma("sp", b[:], allg[1024 + k * 128:1024 + (k + 1) * 128, c0:c0 + 2048], writes=[bb])
                    tk.op("pool", lambda e: e.tensor_scalar(out=b[:], in0=b[:], scalar1=msk[:, 1:2], scalar2=None,
                                                            op0=ALU.mult), reads=[bb, mskb], writes=[bb])
                    tk.op("dve", lambda e: e.scalar_tensor_tensor(out=o[:], in0=a[:], scalar=msk[:, 0:1], in1=b[:],
                                                               op0=ALU.mult, op1=ALU.add),
                          reads=[ab, bb, mskb], writes=[ob])
                    tk.dma("pool", sel[k * 128:(k + 1) * 128, c0:c0 + 2048], o[:], reads=[ob])
        tk.barrier()

    def phase_final(self, T, x_src, y_dst, fing, fingb):
        tk = self.tk
        with contextlib.ExitStack() as es:
            xr = self.sb_ring(es, 3, [128, 1024], F32, "xf")
            jr = self.sb_ring(es, 2, [128, 1024], BF16, "jf")
            sr = self.sb_ring(es, 3, [128, 2], F32, "sf")
            for t in range(T // 128):
                rows = slice(t * 128, (t + 1) * 128)
                xt, xtb = xr.next()
                tk.dma("sp", xt[:], x_src[rows, :], writes=[xtb])
                jt, jtb = jr.next()
                sm, smb = sr.next()
                tk.op("act", lambda e: e.activation(out=jt[:], in_=xt[:], func=AF.Square, accum_out=sm[:, 0:1]),
                      reads=[xtb], writes=[jtb, smb])
                tk.op("act", lambda e: e.activation(out=sm[:, 1:2], in_=sm[:, 0:1], func=AF.Ln, scale=1.0 / D,
                                                    bias=self.epsc[:, 0:1]),
                      reads=[smb, self.epscb], writes=[smb])
                tk.op("act", lambda e: e.activation(out=sm[:, 1:2], in_=sm[:, 1:2], func=AF.Exp, scale=-0.5),
                      reads=[smb], writes=[smb])
                tk.op("dve", lambda e: e.scalar_tensor_tensor(out=xt[:], in0=xt[:], scalar=sm[:, 1:2], in1=fing[:],
                                                              op0=ALU.mult, op1=ALU.mult),
                      reads=[xtb, smb, fingb], writes=[xtb])
                tk.dma("pool", y_dst[rows, :], xt[:], reads=[xtb])
        tk.barrier()

    def scratch(self, T):
        d = self.dram
        n = "_%d" % T
        S = {}
        S["x"] = d("x_scr" + n, [T, 1024], F32)
        S["qaT"] = d("qaT" + n, [1024, T], BF16)
        S["kaT"] = d("kaT" + n, [256, T], BF16)
        S["va"] = d("va" + n, [T, 256], BF16)
        S["ga"] = d("ga" + n, [T, 1024], BF16)
        S["qbT"] = [d("qbT%d" % p + n, [512, T], BF16) for p in range(3)]
        S["kbT"] = [d("kbT%d" % p + n, [512, T], BF16) for p in range(3)]
        S["vb"] = [d("vb%d" % p + n, [T, 512], BF16) for p in range(3)]
        S["gb"] = d("gb" + n, [T, 512], BF16)
        S["ya"] = d("ya" + n, [T, 1024], BF16)
        S["nb"] = [d("nb%d" % p + n, [T, 520], F32) for p in range(3)]
        S["qcT"] = d("qcT" + n, [1024, T], BF16)
        S["kcT"] = d("kcT" + n, [1024, T], BF16)
        S["vc"] = d("vc" + n, [T, 1024], BF16)
        S["gcT"] = d("gcT" + n, [1024, T], BF16)
        S["qd"] = d("qd" + n, [T, 512], F32)
        S["kd"] = d("kd" + n, [T, 512], F32)
        S["vd"] = d("vd" + n, [T, 512], BF16)
        S["gd"] = d("gd" + n, [T, 512], BF16)
        S["adT"] = d("adT" + n, [32, T], F32)
        S["ycT"] = d("ycT" + n, [1024, T], BF16)
        S["of"] = d("of" + n, [T, 512], F32)
        S["ydT"] = d("ydT" + n, [512, T], BF16)
        return S

    def build(self, layers=(0, 1, 2, 3), final=True):
        tk = self.tk
        I, O = "ExternalInput", "ExternalOutput"
        d = self.dram
        xs = [d("x%d" % i, [T, 1024], F32, I) for i, T in enumerate(self.seq_lens)]
        ys = [d("y%d" % i, [T, 1024], F32, O) for i, T in enumerate(self.seq_lens)]
        ewi = d("even_w_in", [2, 1024, EVEN_COLS], F32, I)
        ewo = d("even_w_out", [2, 1536, 1024], F32, I)
        owi = d("odd_w_in", [2, 1024, ODD_COLS], F32, I)
        owo = d("odd_w_out", [2, 1536, 1024], F32, I)
        g_t = d("g_t", [128, 32], F32, I)
        fin_g = d("fin_g", [128, 1024], F32, I)
        sink_d = d("sink", [128, 32], F32, I)
        dlam_d = d("dlam", [128, 512], F32, I)
        subg_d = d("subg", [128, 2], F32, I)
        waug_d = d("waug", [2, 2, 17, 512], F32, I)
        glag_d = d("glag", [128, 2, 512], F32, I)
        ident_d = d("ident", [128, 128], BF16, I)
        btA_d = d("btA", [128, 3, 16, 128], BF16, I)
        btB_d = [d("btB%d" % p, [128, 3, 8, 128], BF16, I) for p in range(3)]
        cst = {"UL": d("cUL", [128, 8, 512], F32, I), "UR": d("cUR", [128, 8, 512], F32, I),
               "BL": d("cBL", [128, 8, 64], F32, I), "BR": d("cBR", [128, 8, 64], F32, I),
               "T1": d("cT1", [128, 8, 7, 128], BF16, I), "CR": d("cCR", [1, 8, 7, 128], BF16, I)}
        gcst = {"CM": d("gCM", [2, 128, 2, 128], F32, I), "MASK": d("gMASK", [2, 128, 512], BF16, I),
                "IND": d("gIND", [128, 2], F32, I)}
        shard = getattr(self, "shard", False)
        if shard:
            owis = d("owi_s", [2, 1024, 3104], F32, I)
            cst_s = {"UL": d("sUL", [128, 2, 512], F32, I), "UR": d("sUR", [128, 2, 512], F32, I),
                     "BL": d("sBL", [128, 2, 64], F32, I), "BR": d("sBR", [128, 2, 64], F32, I),
                     "T1": d("sT1", [128, 2, 7, 128], BF16, I), "CR": d("sCR", [1, 2, 7, 128], BF16, I)}
            gmask_d = d("gmask", [128, 2], F32, I)
            wbis = [d("wbis%d" % j, [1024, 3104], BF16) for j in range(2)]
            TS = 8192
            SS = {"qcT": d("qcT_s", [256, TS], BF16), "kcT": d("kcT_s", [256, TS], BF16),
                  "gcT": d("gcT_s", [256, TS], BF16), "vc": d("vc_s", [TS, 256], BF16),
                  "loc": d("ycT_loc", [256, TS], BF16), "all": d("ycT_all", [2048, TS], BF16),
                  "sel": d("ycT_sel", [1024, TS], BF16)}
        wbi = [d("wbi%d" % l, [1024, EVEN_COLS if l % 2 == 0 else ODD_COLS], BF16) for l in range(4)]
        wbo = [d("wbo%d" % l, [1536, 1024], BF16) for l in range(4)]
        scr = {T: self.scratch(T) for T in sorted(set(self.seq_lens))}
        dil = [1, 4, 16]
        with contextlib.ExitStack() as es:
            self.ident, self.identb = self.load_const(es, ident_d, [128, 128], BF16, "ident")
            gt, gtb = self.load_const(es, g_t, [128, 32], F32, "gt")
            fing, fingb = self.load_const(es, fin_g, [128, 1024], F32, "fing")
            btA, btAb = self.load_const(es, btA_d, [128, 3, 16, 128], BF16, "btA")
            btB = [self.load_const(es, btB_d[p], [128, 3, 8, 128], BF16, "btB") for p in range(3)]
            esink, esinkb = self.load_const(es, sink_d, [128, 32], F32, "esink")
            tk.op("act", lambda e: e.activation(out=esink[:], in_=esink[:], func=AF.Exp), reads=[esinkb],
                  writes=[esinkb])
            glag, glagb = self.load_const(es, glag_d, [128, 2, 512], F32, "glag")
            cst2 = self.sb(es, [128, 2], F32, "cst2")
            self.epscb = Buf()
            tk.op("pool", lambda e: e.memset(cst2[:, 0:1], EPS), writes=[self.epscb])
            tk.op("pool", lambda e: e.memset(cst2[:, 1:2], 1.0), writes=[self.epscb])
            self.epsc = cst2[:, 0:1]
            self.onec = cst2[:, 1:2]
            dl, dlb = self.load_const(es, dlam_d, [128, 512], F32, "dl")
            sg, sgb = self.load_const(es, subg_d, [128, 2], F32, "sg")
            lam = self.sb(es, [128, 8], F32, "lam")
            lamb = Buf()
            prod = self.sb(es, [128, 4, 64], F32, "prod")
            dl4 = dl[:].rearrange("p (j a d) -> p j a d", j=2, a=4)
            for j in range(2):
                for a in range(2):
                    tk.op("dve", lambda e: e.tensor_tensor(out=prod[:, 2 * j + a, :], in0=dl4[:, j, 2 * a, :],
                                                           in1=dl4[:, j, 2 * a + 1, :], op=ALU.mult),
                          reads=[dlb], writes=[lamb])
            tk.op("dve", lambda e: e.tensor_reduce(out=lam[:, 0:4], in_=prod[:], axis=AX.X, op=ALU.add),
                  reads=[lamb], writes=[lamb])
            tk.op("act", lambda e: e.activation(out=lam[:, 0:4], in_=lam[:, 0:4], func=AF.Exp), reads=[lamb],
                  writes=[lamb])
            for j in range(2):
                lam_init = 0.8 - 0.6 * math.exp(-0.3 * (2 * j + 1))
                tk.op("dve", lambda e: e.scalar_tensor_tensor(out=lam[:, 4 + j:5 + j], in0=lam[:, 2 * j + 1:2 * j + 2],
                                                              scalar=-lam_init, in1=lam[:, 2 * j:2 * j + 1],
                                                              op0=ALU.add, op1=ALU.subtract),
                      reads=[lamb], writes=[lamb])
                tk.op("dve", lambda e: e.tensor_scalar(out=sg[:, j:j + 1], in0=sg[:, j:j + 1], scalar1=1.0 - lam_init,
                                                       scalar2=None, op0=ALU.mult), reads=[sgb], writes=[sgb])
            srcs, dsts, gcs = [], [], []
            for l in layers:
                j = l // 2
                srcs += [ewi[j] if l % 2 == 0 else owi[j], ewo[j] if l % 2 == 0 else owo[j]]
                dsts += [wbi[l], wbo[l]]
                gcs += [(lambda k, l=l: (gt[:, 8 * l + k:8 * l + k + 1], gtb)), None]
            if shard:
                gmask, gmaskb = self.load_const(es, gmask_d, [128, 2], F32, "gmask")
                for l in layers:
                    if l % 2 == 1:
                        srcs.append(owis[l // 2])
                        dsts.append(wbis[l // 2])
                        gcs.append(lambda k, l=l: (gt[:, 8 * l + k:8 * l + k + 1], gtb))
            self.prep_weights(srcs, dsts, gcs)
            for si, T in enumerate(self.seq_lens):
                S = scr[T]
                first = True
                for l in layers:
                    j = l // 2
                    x_src = xs[si] if first else S["x"]
                    first = False
                    if l % 2 == 0:
                        fm = [(0, 1024, 1, S["qaT"], 0, "copy", BF16), (1024, 256, 1, S["kaT"], 0, "copy", BF16)]
                        tm = [(1280, 256, 1, S["va"], 0, "copy", BF16), (1536, 1024, 1, S["ga"], 0, "silu", BF16)]
                        for p in range(3):
                            fm.append((2560 + 512 * p, 512, dil[p], S["qbT"][p], 0, "copy", BF16))
                            fm.append((4096 + 512 * p, 512, dil[p], S["kbT"][p], 0, "copy", BF16))
                            tm.append((5632 + 512 * p, 512, dil[p], S["vb"][p], 0, "copy", BF16))
                        tm.append((7168, 512, 1, S["gb"], 0, "silu", BF16))
                        self.phase_proj(T, x_src, wbi[l], fm, tm)
                        self.phase_window(T, 1, 16, 4, S["qaT"], S["kaT"], S["va"], btA, btAb, "A", S["ya"],
                                          gate=S["ga"], esink=esink[:, 16 * j:16 * j + 16], esinkb=esinkb)
                        for p in range(3):
                            self.phase_window(T // dil[p], dil[p], 8, 1, S["qbT"][p], S["kbT"][p], S["vb"][p],
                                              btB[p][0], btB[p][1], "B", S["nb"][p], dil=dil[p])
                        self.phase_out(T, x_src, S["x"], wbo[l], [(S["ya"], 0, 1024)], [], nbs=S["nb"], gb=S["gb"])
                    elif shard and T == 8192:
                        fm = [(0, 256, 1, SS["qcT"], 0, "copy", BF16), (256, 256, 1, SS["kcT"], 0, "copy", BF16),
                              (768, 256, 1, SS["gcT"], 0, "silu", BF16), (1024 + 2048, 32, 1, S["adT"], 0, "copy", F32)]
                        tm = [(512, 256, 1, SS["vc"], 0, "copy", BF16), (1024, 512, 1, S["qd"], 0, "copy", F32),
                              (1536, 512, 1, S["kd"], 0, "copy", F32), (2048, 512, 1, S["vd"], 0, "copy", BF16),
                              (2560, 512, 1, S["gd"], 0, "silu", BF16)]
                        self.phase_proj(T, x_src, wbis[j], fm, tm)
                        a8 = alibi(8)
                        self.phase_diff(T, SS["qcT"], SS["kcT"], SS["vc"], SS["gcT"], SS["loc"], cst_s,
                                        lam[:, 4 + j:5 + j], lamb, sg[:, j:j + 1], sgb, nheads=2,
                                        skip_slopes=[a8[3], a8[7]])
                        self.phase_gather_select(T, SS["loc"], SS["all"], SS["sel"], gmask, gmaskb)
                        self.phase_gla(T, S["qd"], S["kd"], S["vd"], S["adT"], S["gd"], S["of"], S["ydT"],
                                       [waug_d[j, 0], waug_d[j, 1]], gcst, glag[:, j, :], glagb)
                        self.phase_out(T, x_src, S["x"], wbo[l], [], [(SS["sel"], 0, 8), (S["ydT"], 8, 4)],
                                       wperm=[0, 7, 1, 6, 2, 5, 3, 4, 8, 9, 10, 11])
                    else:
                        fm = [(0, 1024, 1, S["qcT"], 0, "copy", BF16), (1024, 1024, 1, S["kcT"], 0, "copy", BF16),
                              (3072, 1024, 1, S["gcT"], 0, "silu", BF16), (6144, 32, 1, S["adT"], 0, "copy", F32)]
                        tm = [(2048, 1024, 1, S["vc"], 0, "copy", BF16), (4096, 512, 1, S["qd"], 0, "copy", F32),
                              (4608, 512, 1, S["kd"], 0, "copy", F32), (5120, 512, 1, S["vd"], 0, "copy", BF16),
                              (5632, 512, 1, S["gd"], 0, "silu", BF16)]
                        self.phase_proj(T, x_src, wbi[l], fm, tm)
                        if "diff" not in getattr(self, "skip", ()):
                          self.phase_diff(T, S["qcT"], S["kcT"], S["vc"], S["gcT"], S["ycT"], cst,
                                        lam[:, 4 + j:5 + j], lamb, sg[:, j:j + 1], sgb)
                        if "gla" not in getattr(self, "skip", ()):
                          self.phase_gla(T, S["qd"], S["kd"], S["vd"], S["adT"], S["gd"], S["of"], S["ydT"],
                                       [waug_d[j, 0], waug_d[j, 1]], gcst, glag[:, j, :], glagb)
                        self.phase_out(T, x_src, S["x"], wbo[l], [], [(S["ycT"], 0, 8), (S["ydT"], 8, 4)])
                if final:
                    self.phase_final(T, S["x"], ys[si], fing, fingb)
                else:
                    self.phase_copy(T, S["x"], ys[si])
        return self.nc

    def phase_copy(self, T, src, dst):
        tk = self.tk
        with contextlib.ExitStack() as es:
            xr = self.sb_ring(es, 3, [128, 1024], F32, "xc")
            for t in range(T // 128):
                xt, xtb = xr.next()
                tk.dma("sp", xt[:], src[t * 128:(t + 1) * 128, :], writes=[xtb])
                tk.dma("pool", dst[t * 128:(t + 1) * 128, :], xt[:], reads=[xtb])
        tk.barrier()


def alibi(n):
    return np.array([2.0 ** (-8.0 * (i + 1) / n) for i in range(n)], dtype=np.float64)


def window_bias_table(nheads, radius, stride):
    b = np.arange(128)[:, None]
    a = np.arange(128)[None, :]
    sl = alibi(nheads)
    out = np.zeros((128, 3, nheads, 128), np.float32)
    for kind in range(3):
        dist = np.abs(a - (b + (kind - 1) * 128))
        for h in range(nheads):
            val = -(sl[h] * stride * 8.0) * dist
            out[:, kind, h, :] = np.where(dist <= radius, val, NEGB)
    return out.astype(ml_dtypes.bfloat16)


def diff_tables():
    sl = alibi(8)
    q = np.arange(512)
    b = np.arange(128)
    UL = np.zeros((128, 8, 512), np.float32)
    UR = np.zeros((128, 8, 512), np.float32)
    BL = np.zeros((128, 8, 64), np.float32)
    BR = np.zeros((128, 8, 64), np.float32)
    T1 = np.zeros((128, 8, 7, 128), np.float32)
    CR = np.zeros((1, 8, 7, 128), np.float32)
    a = np.arange(128)[None, :]
    bb = np.arange(128)[:, None]
    for h in range(8):
        UL[:, h, :] = np.exp(-sl[h] * q)[None, :]
        UR[:, h, :] = np.exp(-sl[h] * (511 - q))[None, :]
        for d in range(64):
            BL[:, h, d] = -sl[h] * (128.0 * d - b)
            BR[:, h, d] = -sl[h] * (128.0 * d + b + 1)
        c = sl[h] * 8.0
        for slot in range(7):
            dlt = slot - 3
            if dlt > 0:
                T1[:, h, slot, :] = -c * (a - bb)
            elif dlt < 0:
                T1[:, h, slot, :] = c * (a - bb)
            else:
                T1[:, h, slot, :] = -c * np.abs(a - bb)
            CR[0, h, slot, :] = -c * 128.0 * abs(dlt)
    return {"UL": UL, "UR": UR, "BL": BL, "BR": BR, "T1": T1.astype(ml_dtypes.bfloat16),
            "CR": CR.astype(ml_dtypes.bfloat16)}


def gla_tables():
    c = np.arange(128)
    same = (c[:, None] // 64) == (c[None, :] // 64)
    CM = np.zeros((2, 128, 2, 128), np.float32)
    MASK = np.zeros((2, 128, 4, 128), np.float32)
    CM[0, :, 0, :] = same & (c[:, None] <= c[None, :])
    CM[0, :, 1, :] = same & (c[:, None] > c[None, :])
    CM[1, :, 0, :] = same & (c[:, None] >= c[None, :])
    CM[1, :, 1, :] = same & (c[:, None] < c[None, :])
    for h in range(4):
        MASK[0, :, h, :] = same & (c[:, None] <= c[None, :])
        MASK[1, :, h, :] = same & (c[:, None] > c[None, :])
    IND = np.zeros((128, 2), np.float32)
    IND[:64, 0] = 1.0
    IND[64:, 1] = 1.0
    return {"CM": CM, "MASK": MASK.reshape(2, 128, 512).astype(ml_dtypes.bfloat16), "IND": IND}


def host_consts(inp):
    f = np.float32
    c = {}
    ng = np.asarray(inp["norm_g"], f)
    c["g_t"] = np.ascontiguousarray(ng.reshape(4, 8, 128).transpose(2, 0, 1).reshape(128, 32))
    c["fin_g"] = np.ascontiguousarray(np.broadcast_to(np.asarray(inp["final_norm_g"], f)[None, :], (128, 1024)))
    c["sink"] = np.ascontiguousarray(np.broadcast_to(np.asarray(inp["sink_logit"], f).reshape(1, 32), (128, 32)))
    c["dlam"] = np.ascontiguousarray(np.broadcast_to(np.asarray(inp["diff_lambda"], f).reshape(1, 512), (128, 512)))
    c["subg"] = np.ascontiguousarray(np.asarray(inp["diff_subln_g"], f).T)
    wg = np.asarray(inp["gla_w_gate2"], f)
    bg = np.asarray(inp["gla_b_gate"], f)
    c["waug"] = np.ascontiguousarray(np.concatenate([bg[:, :, None, :], wg], axis=2))
    gg = np.asarray(inp["gla_norm_g"], f)
    c["glag"] = np.ascontiguousarray(np.broadcast_to(np.tile(gg, (1, 4))[None, :, :], (128, 2, 512)))
    c["ident"] = np.eye(128).astype(ml_dtypes.bfloat16)
    c["btA"] = window_bias_table(16, 128, 1)
    for p, dl in enumerate([1, 4, 16]):
        c["btB%d" % p] = window_bias_table(8, 64, dl)
    for k, v in diff_tables().items():
        c["c" + k] = v
    for k, v in gla_tables().items():
        c["g" + k] = v
    for k in ("even_w_in", "even_w_out", "odd_w_in", "odd_w_out"):
        c[k] = np.ascontiguousarray(np.asarray(inp[k], f))
    return c


_PROG = None


def kernel(**inputs):
    global _PROG
    xp = np.asarray(inputs["x_prompt"], np.float32)
    xsm = np.asarray(inputs["x_sample"], np.float32)
    if _PROG is None:
        P = Prog([2048, 2048, 2048, 2048, 8192])
        P.shard = True
        _PROG = P.build()
    nc = _PROG
    c = host_consts(inputs)
    dt_all = diff_tables()
    owi = c["odd_w_in"]
    in_maps = []
    for core in range(8):
        m = dict(c)
        p = core % 4
        hs = [p, 7 - p]
        cols = []
        for base in (0, 1024, 2048, 3072):
            for h in hs:
                cols.append(np.arange(base + h * 128, base + (h + 1) * 128))
        cols.append(np.arange(4096, ODD_COLS))
        m["owi_s"] = np.ascontiguousarray(owi[:, :, np.concatenate(cols)])
        for k in ("UL", "UR", "BL", "BR", "T1", "CR"):
            m["s" + k] = np.ascontiguousarray(dt_all[k][:, hs])
        gm = np.zeros((128, 2), np.float32)
        gm[:, core // 4] = 1.0
        m["gmask"] = gm
        for i in range(4):
            m["x%d" % i] = np.ascontiguousarray(xp[4 * core + i])
        m["x4"] = np.ascontiguousarray(xsm[core // 4])
        in_maps.append(m)
    res = run_bass_kernel_spmd(nc, in_maps, core_ids=list(range(8)))
    yp = np.stack([res.results[core]["y%d" % i] for core in range(8) for i in range(4)], axis=0)
    ysm = np.stack([res.results[0]["y4"], res.results[4]["y4"]], axis=0)
    return (np.asarray(yp, np.float32), np.asarray(ysm, np.float32))
```
